# Optimizing a Trainium2 kernel written in Bass

```python
import math
import jax, jax.numpy as jnp
from jax import lax
import numpy as np

D_MODEL = 1024
BATCH = 16
SEQ = 2048
DEPTH = 4

N_MIXERS = 3
N_POOL_LAYERS = (DEPTH + 2) // 3
N_CONV_LAYERS = (DEPTH + 1) // 3
N_SSM_LAYERS = DEPTH // 3
POOL_WINDOWS = (2, 4, 8, 16)
POOL_GROUPS = len(POOL_WINDOWS)
POOL_GROUP_DIM = D_MODEL // POOL_GROUPS
CONV_WIDTH = 31
SSM_GROUP_DIM = 16
SSM_GROUPS = D_MODEL // SSM_GROUP_DIM
SSM_STATE = 64
D_FF = -(-8 * D_MODEL // (3 * 256)) * 256
RMS_EPS = 1e-6
LN_EPS = 1e-5
DT_MIN = 1e-3
DT_MAX = 1e-1
LAM_RE_MAX = -1e-4

kernel_name = "hybrid_pool_conv_s5_swiglu"


def rmsnorm(x, g):
    xf = x.astype(jnp.float32)
    y = xf * lax.rsqrt(jnp.mean(xf * xf, axis=-1, keepdims=True) + RMS_EPS) * g.astype(jnp.float32)
    return y.astype(x.dtype)


def pool_mixer(h, w, b, scale):
    bsz, L, _ = h.shape
    hf = h.astype(jnp.float32)
    cs = jnp.cumsum(hf, axis=1)
    t = jnp.arange(L)
    pooled = []
    for g, win in enumerate(POOL_WINDOWS):
        c = cs[..., g * POOL_GROUP_DIM:(g + 1) * POOL_GROUP_DIM]
        prev = jnp.pad(c, ((0, 0), (win, 0), (0, 0)))[:, :L]
        cnt = jnp.minimum(t + 1, win).astype(jnp.float32)[None, :, None]
        pooled.append((c - prev) / cnt)
    diff = (jnp.concatenate(pooled, axis=-1) - hf).astype(h.dtype)
    diff = diff.reshape(bsz, L, POOL_GROUPS, POOL_GROUP_DIM)
    y = jnp.einsum('blgc,gcd->blgd', diff, w).reshape(bsz, L, D_MODEL) + b
    return y * scale


def conv_mixer(h, w_in, b_in, dw, dw_b, ln_g, ln_b, w_out):
    z = h @ w_in + b_in
    a, gate = jnp.split(z, 2, axis=-1)
    u = a * jax.nn.sigmoid(gate)
    u = lax.conv_general_dilated(u, dw[:, None, :], window_strides=(1,), padding=[(CONV_WIDTH - 1, 0)],
                                 dimension_numbers=('NWC', 'WIO', 'NWC'),
                                 feature_group_count=D_MODEL) + dw_b
    uf = u.astype(jnp.float32)
    mu = jnp.mean(uf, axis=-1, keepdims=True)
    var = jnp.mean(jnp.square(uf - mu), axis=-1, keepdims=True)
    un = (uf - mu) * lax.rsqrt(var + LN_EPS) * ln_g.astype(jnp.float32) + ln_b.astype(jnp.float32)
    un = jax.nn.silu(un).astype(h.dtype)
    return un @ w_out


def _ssm_combine(left, right):
    a_i, b_i = left
    a_j, b_j = right
    return a_j * a_i, a_j * b_i + b_j


def ssm_mixer(h, lam_re, lam_im, log_dt, b_re, b_im, c_re, c_im, d, w_a, w_b):
    bsz, L, _ = h.shape
    u = h.astype(jnp.float32)
    ug = u.reshape(bsz, L, SSM_GROUPS, SSM_GROUP_DIM).astype(jnp.complex64)
    lam = lax.complex(jnp.minimum(lam_re.astype(jnp.float32), LAM_RE_MAX), lam_im.astype(jnp.float32))
    dt = jnp.exp(log_dt.astype(jnp.float32))[:, None]
    lam_bar = jnp.exp(lam * dt)
    b_c = lax.complex(b_re.astype(jnp.float32), b_im.astype(jnp.float32))
    b_bar = ((lam_bar - 1.0) / lam)[..., None] * b_c
    bu = jnp.einsum('blgh,gph->blgp', ug, b_bar)
    a = jnp.broadcast_to(lam_bar[None, None], (1, L, SSM_GROUPS, SSM_STATE))
    _, states = lax.associative_scan(_ssm_combine, (a, bu), axis=1)
    c_c = lax.complex(c_re.astype(jnp.float32), c_im.astype(jnp.float32))
    y = jnp.einsum('blgp,ghp->blgh', states, c_c).real.reshape(bsz, L, D_MODEL)
    y = y + d.astype(jnp.float32) * u
    g = jax.nn.gelu(y).astype(h.dtype)
    return (g @ w_a) * jax.nn.sigmoid(g @ w_b)


def swiglu(h, w_gate, w_up, w_down):
    return (jax.nn.silu(h @ w_gate) * (h @ w_up)) @ w_down


def setup_inputs(seed: int = 0) -> dict:
    key = jax.random.key(seed)
    ks = jax.random.split(key, 32)
    f32 = jnp.float32
    nrm = lambda k, shape, s: jax.random.normal(k, shape, f32) * s
    x = jax.random.normal(ks[0], (BATCH, SEQ, D_MODEL), f32)
    mix_norm = 1.0 + nrm(ks[1], (DEPTH, D_MODEL), 0.02)
    ffn_norm = 1.0 + nrm(ks[2], (DEPTH, D_MODEL), 0.02)
    w_gate = nrm(ks[3], (DEPTH, D_MODEL, D_FF), D_MODEL ** -0.5)
    w_up = nrm(ks[4], (DEPTH, D_MODEL, D_FF), D_MODEL ** -0.5)
    w_down = nrm(ks[5], (DEPTH, D_FF, D_MODEL), D_FF ** -0.5)
    pool_w = nrm(ks[6], (N_POOL_LAYERS, POOL_GROUPS, POOL_GROUP_DIM, POOL_GROUP_DIM), POOL_GROUP_DIM ** -0.5)
    pool_b = nrm(ks[7], (N_POOL_LAYERS, D_MODEL), 0.01)
    pool_scale = 1.0 + nrm(ks[8], (N_POOL_LAYERS, D_MODEL), 0.02)
    conv_w_in = nrm(ks[9], (N_CONV_LAYERS, D_MODEL, 2 * D_MODEL), D_MODEL ** -0.5)
    conv_b_in = nrm(ks[10], (N_CONV_LAYERS, 2 * D_MODEL), 0.01)
    conv_dw = nrm(ks[11], (N_CONV_LAYERS, CONV_WIDTH, D_MODEL), CONV_WIDTH ** -0.5)
    conv_dw_b = nrm(ks[12], (N_CONV_LAYERS, D_MODEL), 0.01)
    conv_ln_g = 1.0 + nrm(ks[13], (N_CONV_LAYERS, D_MODEL), 0.02)
    conv_ln_b = nrm(ks[14], (N_CONV_LAYERS, D_MODEL), 0.01)
    conv_w_out = nrm(ks[15], (N_CONV_LAYERS, D_MODEL, D_MODEL), D_MODEL ** -0.5)
    n_idx = jnp.arange(SSM_STATE, dtype=f32)
    ssm_lam_re = -0.5 + nrm(ks[16], (N_SSM_LAYERS, SSM_GROUPS, SSM_STATE), 0.01)
    ssm_lam_im = jnp.broadcast_to(math.pi * n_idx, (N_SSM_LAYERS, SSM_GROUPS, SSM_STATE)) + nrm(ks[17], (N_SSM_LAYERS, SSM_GROUPS, SSM_STATE), 0.01)
    ssm_log_dt = jax.random.uniform(ks[18], (N_SSM_LAYERS, SSM_GROUPS), f32, math.log(DT_MIN), math.log(DT_MAX))
    ssm_b_re = nrm(ks[19], (N_SSM_LAYERS, SSM_GROUPS, SSM_STATE, SSM_GROUP_DIM), (2 * SSM_GROUP_DIM) ** -0.5)
    ssm_b_im = nrm(ks[20], (N_SSM_LAYERS, SSM_GROUPS, SSM_STATE, SSM_GROUP_DIM), (2 * SSM_GROUP_DIM) ** -0.5)
    ssm_c_re = nrm(ks[21], (N_SSM_LAYERS, SSM_GROUPS, SSM_GROUP_DIM, SSM_STATE), SSM_STATE ** -0.5)
    ssm_c_im = nrm(ks[22], (N_SSM_LAYERS, SSM_GROUPS, SSM_GROUP_DIM, SSM_STATE), SSM_STATE ** -0.5)
    ssm_d = nrm(ks[23], (N_SSM_LAYERS, D_MODEL), 1.0)
    ssm_w_glu_a = nrm(ks[24], (N_SSM_LAYERS, D_MODEL, D_MODEL), D_MODEL ** -0.5)
    ssm_w_glu_b = nrm(ks[25], (N_SSM_LAYERS, D_MODEL, D_MODEL), D_MODEL ** -0.5)
    final_norm = 1.0 + nrm(ks[26], (D_MODEL,), 0.02)
    return {"x": x, "mix_norm": mix_norm, "ffn_norm": ffn_norm, "w_gate": w_gate, "w_up": w_up, "w_down": w_down,
            "pool_w": pool_w, "pool_b": pool_b, "pool_scale": pool_scale,
            "conv_w_in": conv_w_in, "conv_b_in": conv_b_in, "conv_dw": conv_dw, "conv_dw_b": conv_dw_b,
            "conv_ln_g": conv_ln_g, "conv_ln_b": conv_ln_b, "conv_w_out": conv_w_out,
            "ssm_lam_re": ssm_lam_re, "ssm_lam_im": ssm_lam_im, "ssm_log_dt": ssm_log_dt,
            "ssm_b_re": ssm_b_re, "ssm_b_im": ssm_b_im, "ssm_c_re": ssm_c_re, "ssm_c_im": ssm_c_im,
            "ssm_d": ssm_d, "ssm_w_glu_a": ssm_w_glu_a, "ssm_w_glu_b": ssm_w_glu_b, "final_norm": final_norm}


def reference(x, mix_norm, ffn_norm, w_gate, w_up, w_down,
              pool_w, pool_b, pool_scale,
              conv_w_in, conv_b_in, conv_dw, conv_dw_b, conv_ln_g, conv_ln_b, conv_w_out,
              ssm_lam_re, ssm_lam_im, ssm_log_dt, ssm_b_re, ssm_b_im, ssm_c_re, ssm_c_im,
              ssm_d, ssm_w_glu_a, ssm_w_glu_b, final_norm):
    h = x
    for i in range(DEPTH):
        kind, j = i % N_MIXERS, i // N_MIXERS
        hn = rmsnorm(h, mix_norm[i])
        if kind == 0:
            mix = pool_mixer(hn, pool_w[j], pool_b[j], pool_scale[j])
        elif kind == 1:
            mix = conv_mixer(hn, conv_w_in[j], conv_b_in[j], conv_dw[j], conv_dw_b[j],
                             conv_ln_g[j], conv_ln_b[j], conv_w_out[j])
        else:
            mix = ssm_mixer(hn, ssm_lam_re[j], ssm_lam_im[j], ssm_log_dt[j], ssm_b_re[j], ssm_b_im[j],
                            ssm_c_re[j], ssm_c_im[j], ssm_d[j], ssm_w_glu_a[j], ssm_w_glu_b[j])
        h = h + mix
        h = h + swiglu(rmsnorm(h, ffn_norm[i]), w_gate[i], w_up[i], w_down[i])
    return rmsnorm(h, final_norm)
```

```python
import math
import numpy as np
from contextlib import ExitStack
import concourse.bass as bass
import concourse.mybir as mybir
from concourse.bass_utils import run_bass_kernel_spmd

F32 = mybir.dt.float32
BF16 = mybir.dt.bfloat16
ALU = mybir.AluOpType
AF = mybir.ActivationFunctionType

D_MODEL = 1024
SEQ = 2048
DEPTH = 4
D_FF = 2816
NCH = 8
NFF = 22
POOL_WINDOWS = (2, 4, 8, 16)
CONV_W = 31
RMS_EPS = 1e-6
LN_EPS = 1e-5
N_CORES = 8
TOK = 4096
NT = 256
NTILES = SEQ // NT
TPS = SEQ // NT
HP = 32
SB = 128
NSLOT = 5
SLOT_ELEMS = 2816

PCOL = {}
_off = 0
for _name, _n in [("mix_norm", 32), ("ffn_norm", 32), ("final_norm", 8), ("pool_b", 16), ("pool_scale", 16),
                  ("conv_b_a", 8), ("conv_b_g", 8), ("conv_dw", 248), ("conv_dw_b", 8), ("conv_ln_g", 8),
                  ("conv_ln_b", 8), ("ssm_d", 8), ("lamre", 32), ("lamim", 32), ("logdt", 32)]:
    PCOL[_name] = _off
    _off += _n
NPAR = _off


class Sched:
    EPOCH = 4000

    def __init__(self, nc, es, n_dma_sems=6):
        self.nc = nc
        self.es = es
        self.h = {'pe': nc.tensor, 'dve': nc.vector, 'act': nc.scalar, 'pool': nc.gpsimd, 'sp': nc.sync}
        self.prog = {e: [] for e in self.h}
        self.cnt = {e: 0 for e in self.h}
        self.sems = {}
        self.waited = {}
        self.last_w = {}
        self.readers = {}
        self.n_dma_sems = n_dma_sems
        self.dma_rr = {e: 0 for e in self.h}
        self.dma_cnt = {}
        self.dry = False

    def rekey(self, prefix, newkeys):
        if self.dry:
            return
        toks = set()
        for k in [k for k in self.last_w if k.startswith(prefix)]:
            t = self.last_w.pop(k)
            if t is not None:
                toks.add(t)
        for k in [k for k in self.readers if k.startswith(prefix)]:
            toks.update(self.readers.pop(k))
        for k in newkeys:
            self.last_w[k] = None
            self.readers[k] = list(toks)

    def _sem(self, key):
        if key not in self.sems:
            self.sems[key] = self.es.enter_context(self.nc.semaphore("s_" + "_".join(str(k) for k in key)))
        return self.sems[key]

    def _deps(self, eng, reads, writes, extra=()):
        deps = set(extra)
        for k in reads:
            t = self.last_w.get(k)
            if t is not None:
                deps.add(t)
            if k.startswith('PSB') and eng != 'pe':
                t = self.last_w.get(k[:-1] + ('1' if k[-1] == '0' else '0'))
                if t is not None:
                    deps.add(t)
        for k in writes:
            t = self.last_w.get(k)
            if t is not None:
                deps.add(t)
            for t in self.readers.get(k, ()):
                deps.add(t)
            if k.startswith('PSB'):
                sib = k[:-1] + ('1' if k[-1] == '0' else '0')
                if eng == 'pe':
                    for t in self.readers.get(sib, ()):
                        deps.add(t)
                t = self.last_w.get(sib)
                if t is not None:
                    deps.add(t)
        need = {}
        for (s, v, e) in deps:
            if e == 'pe' and eng == 'pe':
                continue
            if need.get(s, 0) < v:
                need[s] = v
        waits = []
        for s, v in need.items():
            if self.waited.get((eng, s), 0) < v:
                self.waited[(eng, s)] = v
                waits.append((s, v))
        return waits

    def _commit(self, tok, reads, writes):
        for k in writes:
            self.last_w[k] = tok
            self.readers[k] = []
        for k in reads:
            if k not in writes:
                self.readers.setdefault(k, []).append(tok)

    def op(self, eng, fn, reads=(), writes=()):
        if self.dry:
            return None
        waits = self._deps(eng, reads, writes)
        self.cnt[eng] += 1
        c = self.cnt[eng]
        ep = (c - 1) // self.EPOCH
        skey = (eng, ep)
        self._sem(skey)
        tok = (skey, c - ep * self.EPOCH, eng)
        self.prog[eng].append((waits, fn, skey, 1))
        self._commit(tok, reads, writes)
        return tok

    def dma(self, q, fn, reads=(), writes=()):
        if self.dry:
            return None
        j = self.dma_rr[q]
        self.dma_rr[q] = (j + 1) % self.n_dma_sems
        skey = ('dma', q, j)
        self._sem(skey)
        n = self.dma_cnt.get(skey, 0)
        extra = [(skey, 16 * n, 'dma')] if n > 0 else []
        waits = self._deps(q, reads, writes, extra)
        self.dma_cnt[skey] = n + 1
        tok = (skey, 16 * (n + 1), 'dma')
        self.prog[q].append((waits, fn, skey, 16))
        self._commit(tok, reads, writes)
        return tok

    def emit(self, final_engine='sp'):
        finals = []
        for e in self.h:
            c = self.cnt[e]
            if c > 0:
                ep = (c - 1) // self.EPOCH
                finals.append(((e, ep), c - ep * self.EPOCH))
        for skey, n in self.dma_cnt.items():
            finals.append((skey, 16 * n))
        sems = self.sems
        prog = self.prog
        with self.nc.Block() as block:
            def mk(e):
                def body(engh):
                    for waits, fn, skey, inc in prog[e]:
                        for s, v in waits:
                            engh.wait_ge(sems[s], v)
                        fn(engh).then_inc(sems[skey], inc)
                    if e == final_engine:
                        for s, v in finals:
                            engh.wait_ge(sems[s], v)
                return body
            block.sync(mk('sp'))
            block.tensor(mk('pe'))
            block.vector(mk('dve'))
            block.scalar(mk('act'))
            block.gpsimd(mk('pool'))


def build_nc(depth_run=DEPTH, do_final=True):
    nc = bass.Bass("TRN2", target_bir_lowering=False)

    def D(name, shape, dt, kind="ExternalInput"):
        return nc.dram_tensor(name, list(shape), dt, kind=kind).ap()

    xT = D("xT", [D_MODEL, TOK], F32)
    outT = D("outT", [D_MODEL, TOK], F32, "ExternalOutput")
    params = D("params", [128, NPAR], F32)
    tau1 = D("tau1", [128, SB], F32)
    w_gate = D("w_gate", [DEPTH, D_MODEL, D_FF], F32)
    w_up = D("w_up", [DEPTH, D_MODEL, D_FF], F32)
    w_down = D("w_down", [DEPTH, D_FF, D_MODEL], F32)
    pool_w = D("pool_w", [2, 4, 256, 256], F32)
    conv_w_in = D("conv_w_in", [D_MODEL, 2 * D_MODEL], F32)
    conv_w_out = D("conv_w_out", [D_MODEL, D_MODEL], F32)
    ssm_wa = D("ssm_wa", [D_MODEL, D_MODEL], F32)
    ssm_wb = D("ssm_wb", [D_MODEL, D_MODEL], F32)
    bre_l = D("bre_l", [128, 8 * 4 * 128], F32)
    bim_l = D("bim_l", [128, 8 * 4 * 128], F32)
    cre_l = D("cre_l", [128, 32 * 64], F32)
    cim_l = D("cim_l", [128, 32 * 64], F32)
    ident = D("ident", [128, 128], F32)

    wg_s = D("wg_s", [DEPTH, 11, 128, 8, 256], BF16, "Internal")
    wu_s = D("wu_s", [DEPTH, 11, 128, 8, 256], BF16, "Internal")
    wd_s = D("wd_s", [DEPTH, 4, 2, 128, 11, 256], BF16, "Internal")
    cin_s = D("cin_s", [8, 128, 8, 256], BF16, "Internal")
    cout_s = D("cout_s", [4, 128, 8, 256], BF16, "Internal")
    swa_s = D("swa_s", [4, 128, 8, 256], BF16, "Internal")
    swb_s = D("swb_s", [4, 128, 8, 256], BF16, "Internal")
    poolw_s = D("poolw_s", [2, 128, 8, 256], BF16, "Internal")
    dg_s = D("dg_s", [8, 128, CONV_W, 128], BF16, "Internal")

    with ExitStack() as es:
        S = Sched(nc, es)

        def sb(name, shape, dt):
            return es.enter_context(nc.sbuf_tensor(name, list(shape), dt))

        def v3(t, n):
            return t[:].rearrange("p (c n) -> p c n", n=n)

        def keys(name, n):
            return ['%s%d' % (name, i) for i in range(n)]

        HW = HP + NT
        AW = 16 + NT
        HNF = sb("HNF", [128, 8 * HW], F32)
        ACC = sb("ACC", [128, 4608], F32)
        ACCC = sb("ACCC", [128, 8 * AW], F32)
        WS = [sb("WS%d" % i, [128, SLOT_ELEMS], BF16) for i in range(NSLOT)]
        CT = sb("CT", [128, 32 * SB], F32)
        ST = sb("ST", [128, 32 * SB], F32)
        RT = sb("RT", [128, 32 * SB], F32)
        PR = sb("PR", [128, NPAR], F32)
        ONESF = sb("ONESF", [128, NT], F32)
        ONESB = sb("ONESB", [128, 128], BF16)
        BRE = sb("BRE", [128, 8 * 4 * 128], BF16)
        BIM = sb("BIM", [128, 8 * 4 * 128], BF16)
        CRE = sb("CRE", [128, 32 * 64], BF16)
        NCIM = sb("NCIM", [128, 32 * 64], BF16)
        SM = sb("SM", [128, 512], F32)
        INVC = sb("INVC", [128, 4 * 16], F32)
        CONST = sb("CONST", [128, 32], F32)
        PSB = [es.enter_context(nc.psum_tensor("PSB%d" % i, [128, 512], F32)) for i in range(8)]
        HNF3 = v3(HNF, HW)
        BRE3 = BRE[:].rearrange("p (q jj s) -> p q jj s", jj=4, s=128)
        BIM3 = BIM[:].rearrange("p (q jj s) -> p q jj s", jj=4, s=128)
        CRE3 = v3(CRE, 64)
        NCIM3 = v3(NCIM, 64)
        CT3 = v3(CT, SB)
        ST3 = v3(ST, SB)
        RT3 = v3(RT, SB)

        class Q:
            pass

        streams = []
        for s in range(2):
            q = Q()
            q.s = s
            q.p = 'q%d_' % s
            q.X = sb("X%d" % s, [128, 8 * NT], F32)
            q.HNB = sb("HNB%d" % s, [128, 8 * NT], BF16)
            q.BF = sb("BF%d" % s, [128, 8 * NT], BF16)
            q.TMP = sb("TMP%d" % s, [128, 3 * NT], F32)
            q.ACTB = sb("ACTB%d" % s, [128, NFF * NT], BF16)
            q.POOLH = [sb("POOLH%d_%d" % (s, j), [128, 8 * 16], F32) for j in range(2)]
            q.CONVH = sb("CONVH%d" % s, [128, 8 * HP], F32)
            q.WLR = sb("WLR%d" % s, [128, 32], F32)
            q.WLI = sb("WLI%d" % s, [128, 32], F32)
            q.SMQ = sb("SMQ%d" % s, [128, 64], F32)
            q.RSTD = sb("RSTD%d" % s, [128, NT], F32)
            q.X3 = v3(q.X, NT)
            q.HNB3 = v3(q.HNB, NT)
            q.BF3 = v3(q.BF, NT)
            q.TMP3 = v3(q.TMP, NT)
            q.ACTB3 = v3(q.ACTB, NT)
            B = PSB[4 * s:4 * s + 4]

            def hb(i, h, B=B):
                return B[i][:, h * NT:(h + 1) * NT]

            def khb(i, h, s=s):
                return 'PSB%dh%d' % (4 * s + i, h)
            q.PG = [hb(0, 0), hb(2, 0)]
            q.kPG = [khb(0, 0), khb(2, 0)]
            q.PU = [hb(1, 0), hb(3, 0)]
            q.kPU = [khb(1, 0), khb(3, 0)]
            q.PO = [hb(0, 1), hb(1, 1), hb(2, 1), hb(3, 1)]
            q.kPO = [khb(0, 1), khb(1, 1), khb(2, 1), khb(3, 1)]
            q.RING = [(hb(i, 0), hb(i, 1)) for i in range(4)]
            q.kRING = [(khb(i, 0), khb(i, 1)) for i in range(4)]
            q.PGF = B[0]
            q.kPGF = [khb(0, 0), khb(0, 1)]
            q.PUF = B[1]
            q.kPUF = [khb(1, 0), khb(1, 1)]
            q.POY = [hb(2, 0), hb(3, 0)]
            q.kPOY = [khb(2, 0), khb(3, 0)]
            streams.append(q)

        def pc(name, idx=0, n=1):
            o = PCOL[name] + idx
            return PR[:, o:o + n]

        def layer_wlist(L):
            lst = []
            kind = L % 3
            if kind == 0:
                lst.append(('poolw', L // 3))
            if kind == 1:
                for p in range(4):
                    lst.append(('cin', p))
                    lst.append(('cin', 4 + p))
                for c in range(8):
                    lst.append(('dg', c, 0))
                    lst.append(('dg', c, 1))
                for p in range(4):
                    lst.append(('cout', p))
            elif kind == 2:
                for p in range(4):
                    lst.append(('swa', p))
                    lst.append(('swb', p))
            for g in range(11):
                lst.append(('wg', L, g))
                lst.append(('wu', L, g))
            for mp in range(4):
                lst.append(('wd', L, mp, 0))
                lst.append(('wd', L, mp, 1))
            return lst

        def w_src(d):
            k = d[0]
            if k == 'wg':
                return wg_s[d[1], d[2]], w_gate[d[1]].rearrange("(kc k) (g m) -> g k kc m", k=128, m=256)[d[2]]
            if k == 'wu':
                return wu_s[d[1], d[2]], w_up[d[1]].rearrange("(kc k) (g m) -> g k kc m", k=128, m=256)[d[2]]
            if k == 'wd':
                src = w_down[d[1]].rearrange("(kh kc k) (mp m) -> mp kh k kc m", kh=2, kc=11, k=128, m=256)
                return wd_s[d[1], d[2], d[3]], src[d[2], d[3]]
            if k == 'cin':
                return cin_s[d[1]], conv_w_in.rearrange("(kc k) (g m) -> g k kc m", k=128, m=256)[d[1]]
            if k == 'cout':
                return cout_s[d[1]], conv_w_out.rearrange("(kc k) (g m) -> g k kc m", k=128, m=256)[d[1]]
            if k == 'swa':
                return swa_s[d[1]], ssm_wa.rearrange("(kc k) (g m) -> g k kc m", k=128, m=256)[d[1]]
            if k == 'dg':
                k0, k1 = (0, 16) if d[2] == 0 else (16, CONV_W)
                return dg_s[d[1]][:, k0:k1, :], None
            if k == 'poolw':
                return poolw_s[d[1]], pool_w[d[1]].rearrange("g (ki k) d -> k (g ki) d", k=128)
            if k == 'swb':
                return swb_s[d[1]], ssm_wb.rearrange("(kc k) (g m) -> g k kc m", k=128, m=256)[d[1]]
            raise KeyError(k)

        tile_wlist = []
        for L in range(depth_run):
            tile_wlist += layer_wlist(L)

        ssm_ready = [False]
        cast_done = set()
        cast_order = list(tile_wlist)
        cast_pos = [0]

        def ensure_cast(d):
            if S.dry or d in cast_done or d[0] == 'dg':
                return
            cast_done.add(d)
            scr, src = w_src(d)
            S.dma('pool', lambda e, scr=scr, src=src: e.dma_start(out=scr, in_=src), writes=['scr_' + '_'.join(map(str, d))])

        def cast_ahead(n):
            if S.dry:
                return
            while n > 0 and cast_pos[0] < len(cast_order):
                d = cast_order[cast_pos[0]]
                cast_pos[0] += 1
                if d not in cast_done:
                    ensure_cast(d)
                    n -= 1

        class WStream:
            def __init__(self):
                self.record = True
                self.descs = []
                self.issued = 0
                self.used = 0

            def _issue(self, i):
                d = self.descs[i]
                ensure_cast(d)
                scr, _ = w_src(d)
                a, m = scr.shape[1], scr.shape[2]
                dst = WS[i % NSLOT][:, 0:a * m].rearrange("p (a m) -> p a m", m=m)
                S.dma('sp', lambda e, dst=dst, scr=scr: e.dma_start(out=dst, in_=scr),
                      reads=['scr_' + '_'.join(map(str, d))], writes=['WS%d' % (i % NSLOT)])

            def next(self, d):
                if self.record:
                    self.descs.append(d)
                    k = len(self.descs) - 1
                else:
                    k = self.used
                    assert self.descs[k] == d, (self.descs[k], d)
                    lim = min(len(self.descs), k + NSLOT - 1)
                    while self.issued < lim:
                        self._issue(self.issued)
                        self.issued += 1
                    self.used += 1
                shp = w_src(d)[0].shape
                a, m = shp[1], shp[2]
                return WS[k % NSLOT][:, 0:a * m].rearrange("p (a m) -> p a m", m=m), 'WS%d' % (k % NSLOT)

        W = WStream()

        def setup():
            S.dma('sp', lambda e: e.dma_start(out=PR[:], in_=params), writes=['PR'])
            S.dma('pool', lambda e: e.dma_start(out=BRE[:].rearrange('p (a b) -> p a b', b=1024), in_=bre_l.rearrange('p (a b) -> p a b', b=1024)), writes=['BRE'])
            S.dma('pool', lambda e: e.dma_start(out=BIM[:].rearrange('p (a b) -> p a b', b=1024), in_=bim_l.rearrange('p (a b) -> p a b', b=1024)), writes=['BIM'])
            S.op('dve', lambda e: e.memset(ONESF[:], 1.0), writes=['ONESF'])
            S.op('dve', lambda e: e.memset(ONESB[:], 1.0 / D_MODEL), writes=['ONESB'])
            S.op('dve', lambda e: e.memset(CONST[:, 0:1], RMS_EPS), writes=['CONST'])
            S.op('dve', lambda e: e.memset(CONST[:, 1:2], LN_EPS), writes=['CONST'])
            for wi, Wn in enumerate(POOL_WINDOWS):
                S.op('dve', lambda e, wi=wi, Wn=Wn: e.memset(INVC[:, wi * 16:(wi + 1) * 16], 1.0 / Wn), writes=['INVC'])
                for t in range(Wn - 1):
                    S.op('dve', lambda e, wi=wi, t=t: e.memset(INVC[:, wi * 16 + t:wi * 16 + t + 1], 1.0 / (t + 1)), writes=['INVC'])
            S.op('dve', lambda e: e.tensor_tensor(out=CONST[:, 8:24], in0=pc('pool_b', 0, 16), in1=pc('pool_scale', 0, 16), op=ALU.mult),
                 reads=['PR'], writes=['PBS'])
            S.op('dve', lambda e: e.tensor_tensor(out=CONST[:, 24:32], in0=pc('mix_norm', 16, 8), in1=pc('ssm_d', 0, 8), op=ALU.mult),
                 reads=['PR'], writes=['DG'])
            if depth_run >= 2:
                IDT = ACCC[:, 0:128]
                S.dma('sp', lambda e: e.dma_start(out=IDT, in_=ident), writes=['ACCcident'])
                for c in range(8):
                    for k in range(CONV_W):
                        S.op('dve', lambda e, c=c, k=k: e.tensor_scalar(out=ACC[:, k * 128:(k + 1) * 128], in0=IDT, scalar1=pc('conv_dw', k * 8 + c), scalar2=None, op0=ALU.mult),
                             reads=['ACCcident', 'PR'], writes=['ACCsdiag'])
                    S.dma('pool', lambda e, c=c: e.dma_start(out=dg_s[c], in_=ACC[:, 0:CONV_W * 128].rearrange("p (k m) -> p k m", m=128), max_dma_last_dim=2048),
                          reads=['ACCsdiag'], writes=['scr_dg_%d_0' % c, 'scr_dg_%d_1' % c])

        def setup_ssm():
            S.rekey('ACCs', ['ACCsang', 'ACCskf', 'ACCsyy', 'ACCstau', 'ACCsta0', 'ACCsta1', 'ACCstb0', 'ACCstb1'])
            TAU = ACC[:, 4096:4096 + SB]
            S.dma('sp', lambda e: e.dma_start(out=TAU, in_=tau1), writes=['ACCstau'])
            f = lambda i: SM[:, i * 32:(i + 1) * 32]
            LRE, DT, AA, TH, R, T0, T1, T2, QRE, QIM, NQIM, DEN = [f(i) for i in range(12)]
            RC128 = f(12)
            RS128 = f(13)
            kSM = ['SM']
            S.op('dve', lambda e: e.tensor_scalar(out=LRE, in0=pc('lamre', 0, 32), scalar1=-1e-4, scalar2=None, op0=ALU.min), reads=['PR'], writes=kSM)
            S.op('act', lambda e: e.activation(out=DT, in_=pc('logdt', 0, 32), func=AF.Exp), reads=['PR'], writes=kSM)
            S.op('dve', lambda e: e.tensor_tensor(out=AA, in0=LRE, in1=DT, op=ALU.mult), reads=kSM, writes=kSM)
            S.op('dve', lambda e: e.tensor_tensor(out=TH, in0=pc('lamim', 0, 32), in1=DT, op=ALU.mult), reads=kSM + ['PR'], writes=kSM)
            S.op('act', lambda e: e.activation(out=R, in_=AA, func=AF.Exp), reads=kSM, writes=kSM)
            for st in range(32):
                S.op('dve', lambda e, st=st: e.tensor_scalar(out=RT[:, st * SB:(st + 1) * SB], in0=ONESF[:, 0:SB], scalar1=R[:, st:st + 1],
                                                             scalar2=None, op0=ALU.mult), reads=['ONESF'] + kSM, writes=['RT'])
            S.op('dve', lambda e: e.memset(RT3[:, :, 0], 0.0), writes=['RT'])
            MAGIC = 12582912.0
            C1 = 6.28125
            C2 = 2.0 * math.pi - 6.28125
            PI_LO = 3.1415925
            NQ = 8
            HALF = NQ * SB
            ANG = ACC[:, 0:HALF]
            KF = ACC[:, HALF:2 * HALF]
            YY = ACC[:, 2 * HALF:3 * HALF]
            kA, kK, kY = ['ACCsang'], ['ACCskf'], ['ACCsyy']
            for half in range(32 // NQ):
                for st in range(NQ):
                    S.op('dve', lambda e, st=st, half=half: e.tensor_scalar(out=ANG[:, st * SB:(st + 1) * SB], in0=TAU, scalar1=TH[:, half * NQ + st:half * NQ + st + 1],
                                                                          scalar2=None, op0=ALU.mult), reads=['ACCstau'] + kSM, writes=kA)
                for which, dst in (('sin', ST), ('cos', CT)):
                    if which == 'cos':
                        S.op('dve', lambda e: e.tensor_scalar(out=ANG, in0=ANG, scalar1=math.pi / 2, scalar2=None, op0=ALU.add), reads=kA, writes=kA)
                    S.op('dve', lambda e: e.tensor_scalar(out=KF, in0=ANG, scalar1=1.0 / (2 * math.pi), scalar2=MAGIC, op0=ALU.mult, op1=ALU.add), reads=kA, writes=kK)
                    S.op('dve', lambda e: e.tensor_scalar(out=KF, in0=KF, scalar1=-MAGIC, scalar2=None, op0=ALU.add), reads=kK, writes=kK)
                    S.op('dve', lambda e: e.scalar_tensor_tensor(out=YY, in0=KF, scalar=-C1, in1=ANG, op0=ALU.mult, op1=ALU.add), reads=kK + kA, writes=kY)
                    S.op('dve', lambda e: e.scalar_tensor_tensor(out=YY, in0=KF, scalar=-C2, in1=YY, op0=ALU.mult, op1=ALU.add), reads=kK + kY, writes=kY)
                    S.op('dve', lambda e: e.tensor_scalar(out=KF, in0=YY, scalar1=math.pi, scalar2=-2 * math.pi, op0=ALU.is_gt, op1=ALU.mult), reads=kY, writes=kK)
                    S.op('dve', lambda e: e.tensor_tensor(out=YY, in0=YY, in1=KF, op=ALU.add), reads=kY + kK, writes=kY)
                    S.op('dve', lambda e: e.tensor_scalar(out=KF, in0=YY, scalar1=-math.pi, scalar2=2 * math.pi, op0=ALU.is_lt, op1=ALU.mult), reads=kY, writes=kK)
                    S.op('dve', lambda e: e.tensor_tensor(out=YY, in0=YY, in1=KF, op=ALU.add), reads=kY + kK, writes=kY)
                    S.op('dve', lambda e: e.tensor_scalar(out=YY, in0=YY, scalar1=PI_LO, scalar2=-PI_LO, op0=ALU.min, op1=ALU.max), reads=kY, writes=kY)
                    S.op('act', lambda e, dst=dst, half=half: e.activation(out=dst[:, half * HALF:(half + 1) * HALF], in_=YY, func=AF.Sin),
                         reads=kY, writes=['CT' if which == 'cos' else 'ST'])
            C0 = CT3[:, :, 0]
            S0 = ST3[:, :, 0]
            CL = CT3[:, :, SB - 1]
            SL = ST3[:, :, SB - 1]
            tt = lambda o, a, b, op, rd=(), wr=kSM: S.op('dve', lambda e: e.tensor_tensor(out=o, in0=a, in1=b, op=op), reads=list(rd) + kSM, writes=wr)
            tt(RC128, R, CL, ALU.mult, rd=['CT'])
            tt(RS128, R, SL, ALU.mult, rd=['ST'])
            tt(T0, R, C0, ALU.mult, rd=['CT'])
            S.op('dve', lambda e: e.tensor_scalar(out=T0, in0=T0, scalar1=-1.0, scalar2=None, op0=ALU.add), reads=kSM, writes=kSM)
            tt(T1, R, S0, ALU.mult, rd=['ST'])
            LIM = pc('lamim', 0, 32)
            tt(T2, LRE, LRE, ALU.mult)
            tt(DEN, LIM, LIM, ALU.mult, rd=['PR'])
            tt(DEN, DEN, T2, ALU.add)
            S.op('dve', lambda e: e.reciprocal(out=DEN, in_=DEN), reads=kSM, writes=kSM)
            tt(QRE, T0, LRE, ALU.mult)
            tt(T2, T1, LIM, ALU.mult, rd=['PR'])
            tt(QRE, QRE, T2, ALU.add)
            tt(QRE, QRE, DEN, ALU.mult)
            tt(QIM, T1, LRE, ALU.mult)
            tt(T2, T0, LIM, ALU.mult, rd=['PR'])
            tt(QIM, QIM, T2, ALU.subtract)
            tt(QIM, QIM, DEN, ALU.mult)
            S.op('dve', lambda e: e.tensor_scalar(out=NQIM, in0=QIM, scalar1=-1.0, scalar2=None, op0=ALU.mult), reads=kSM, writes=kSM)
            S.dma('sp', lambda e: e.dma_start(out=ACC[:, 0:2048], in_=cre_l), writes=kA + kK + kY)
            S.dma('sp', lambda e: e.dma_start(out=ACC[:, 2048:4096], in_=cim_l), writes=kA + kK + kY + ['ACCstau'])
            CREL = ACC[:, 0:2048].rearrange("p (s h) -> p s h", h=64)
            CIML = ACC[:, 2048:4096].rearrange("p (s h) -> p s h", h=64)
            kT = kA + kK + kY
            for st in range(32):
                ta = ACC[:, 4224 + (st % 2) * 128:4224 + (st % 2) * 128 + 64]
                tb = ACC[:, 4224 + (st % 2) * 128 + 64:4224 + (st % 2) * 128 + 128]
                ka = ['ACCsta%d' % (st % 2)]
                kb = ['ACCstb%d' % (st % 2)]
                S.op('dve', lambda e, st=st, ta=ta: e.tensor_scalar(out=ta, in0=CIML[:, st, :], scalar1=QIM[:, st:st + 1], scalar2=None, op0=ALU.mult),
                     reads=kT + kSM, writes=ka)
                S.op('dve', lambda e, st=st, ta=ta: e.scalar_tensor_tensor(out=CRE3[:, st, :], in0=CREL[:, st, :], scalar=QRE[:, st:st + 1], in1=ta,
                                                                           op0=ALU.mult, op1=ALU.subtract), reads=kT + kSM + ka, writes=['CRE'])
                S.op('dve', lambda e, st=st, tb=tb: e.tensor_scalar(out=tb, in0=CIML[:, st, :], scalar1=QRE[:, st:st + 1], scalar2=-1.0, op0=ALU.mult, op1=ALU.mult),
                     reads=kT + kSM, writes=kb)
                S.op('dve', lambda e, st=st, tb=tb: e.scalar_tensor_tensor(out=NCIM3[:, st, :], in0=CREL[:, st, :], scalar=NQIM[:, st:st + 1], in1=tb,
                                                                           op0=ALU.mult, op1=ALU.add), reads=kT + kSM + kb, writes=['NCIM'])

        def rmsnorm(q, gname, gidx, dest, rstd_sbuf=False):
            p = q.p
            for c in range(8):
                S.op('act', lambda e, c=c: e.activation(out=q.BF3[:, c, :], in_=q.X3[:, c, :], func=AF.Square),
                     reads=[p + 'X%d' % c], writes=[p + 'BF%d' % c])
            for c in range(8):
                S.op('pe', lambda e, c=c: e.matmul(q.PO[2], ONESB[:], q.BF3[:, c, :], start=(c == 0), stop=(c == 7)),
                     reads=[p + 'BF%d' % c, 'ONESB'], writes=[q.kPO[2]])
            S.op('act', lambda e: e.activation(out=q.TMP3[:, 0, :], in_=q.PO[2], func=AF.Sqrt, bias=CONST[:, 0:1], scale=1.0),
                 reads=[q.kPO[2], 'CONST'], writes=[p + 'TMP0'])
            if rstd_sbuf:
                rs_ap, rs_k = q.RSTD[:], p + 'RSTD'
            else:
                rs_ap, rs_k = q.PO[2], q.kPO[2]
            S.op('dve', lambda e: e.reciprocal(out=rs_ap, in_=q.TMP3[:, 0, :]), reads=[p + 'TMP0'], writes=[rs_k])
            for c in range(8):
                if dest == 'bf':
                    o, wk = q.HNB3[:, c, :], p + 'HNB%d' % c
                else:
                    o, wk = HNF3[:, c, HP:HP + NT], 'HNF%d' % c
                S.op('dve', lambda e, c=c, o=o: e.scalar_tensor_tensor(out=o, in0=q.X3[:, c, :], scalar=pc(gname, gidx * 8 + c), in1=rs_ap,
                                                                      op0=ALU.mult, op1=ALU.mult),
                     reads=[p + 'X%d' % c, rs_k, 'PR'], writes=[wk])
            yield 4.0

        def mm_group(pst, kps, wv, kw, h, rhs3, rkeys, nk):
            for kc in range(nk):
                S.op('pe', lambda e, kc=kc: e.matmul(pst, wv[:, kc, h * 128:(h + 1) * 128], rhs3[:, kc, :], start=(kc == 0), stop=(kc == nk - 1)),
                     reads=[kw, rkeys[kc]], writes=[kps])

        def ffn(q, L):
            p = q.p
            yield from rmsnorm(q, 'ffn_norm', L, 'bf')
            kh = [p + 'HNB%d' % c for c in range(8)]
            for g in range(11):
                wg, kwg = W.next(('wg', L, g))
                wu, kwu = W.next(('wu', L, g))
                for h in range(2):
                    j = 2 * g + h
                    pb = j % 2
                    r4 = j % 4
                    pgt, kpg = q.RING[r4][0], q.kRING[r4][0]
                    put, kpu = q.RING[r4][1], q.kRING[r4][1]
                    mm_group(pgt, kpg, wg, kwg, h, q.HNB3, kh, 8)
                    mm_group(put, kpu, wu, kwu, h, q.HNB3, kh, 8)
                    S.op('act', lambda e, pb=pb, pgt=pgt: e.activation(out=q.TMP3[:, pb, :], in_=pgt, func=AF.Silu),
                         reads=[kpg], writes=[p + 'TMP%d' % pb])
                    S.op('act', lambda e, put=put: e.activation(out=q.TMP3[:, 2, :], in_=put, func=AF.Copy),
                         reads=[kpu], writes=[p + 'TMP2'])
                    S.op('pool', lambda e, pb=pb, j=j: e.tensor_tensor(out=q.ACTB3[:, j, :], in0=q.TMP3[:, pb, :], in1=q.TMP3[:, 2, :], op=ALU.mult),
                         reads=[p + 'TMP%d' % pb, p + 'TMP2'], writes=[p + 'ACTB%d' % j])
                yield 3.8
            for mp in range(4):
                wds = [W.next(('wd', L, mp, 0)), W.next(('wd', L, mp, 1))]
                pbase = (mp % 2) * 2
                for khalf, (wd, kwd) in enumerate(wds):
                    for h in range(2):
                        for kc in range(11):
                            S.op('pe', lambda e, wd=wd, h=h, kc=kc, khalf=khalf, pbase=pbase: e.matmul(
                                q.PO[pbase + h], wd[:, kc, h * 128:(h + 1) * 128], q.ACTB3[:, khalf * 11 + kc, :],
                                start=(khalf == 0 and kc == 0), stop=(khalf == 1 and kc == 10)),
                                reads=[kwd, p + 'ACTB%d' % (khalf * 11 + kc)], writes=[q.kPO[pbase + h]])
                for h in range(2):
                    c = 2 * mp + h
                    S.op('dve', lambda e, c=c, h=h, pbase=pbase: e.tensor_tensor(out=q.X3[:, c, :], in0=q.X3[:, c, :], in1=q.PO[pbase + h], op=ALU.add),
                         reads=[p + 'X%d' % c, q.kPO[pbase + h]], writes=[p + 'X%d' % c])
                yield 5.0

        def pool_mixer(q, L, first):
            p = q.p
            j = L // 3
            ACC3 = ACCC[:, 0:8 * AW].rearrange("p (c n) -> p c n", n=AW)
            kAC = keys('ACCcp', 8)
            S.rekey('ACCc', kAC)
            PH3 = v3(q.POOLH[j], 16)
            kph = p + 'POOLH%d' % j
            if first:
                S.op('dve', lambda e: e.memset(q.POOLH[j][:], 0.0), writes=[kph])
            yield from rmsnorm(q, 'mix_norm', L, 'f32')
            S.op('act', lambda e: e.activation(out=ACC3[:, :, 0:16], in_=PH3, func=AF.Copy), reads=[kph], writes=kAC)
            for c in range(8):
                S.op('dve', lambda e, c=c: e.tensor_tensor_scan(out=ACC3[:, c, 16:16 + NT], data0=ONESF[:, 0:NT], data1=HNF3[:, c, HP:HP + NT],
                                                                initial=ACC3[:, c, 15:16], op0=ALU.mult, op1=ALU.add),
                     reads=[kAC[c], 'ONESF', 'HNF%d' % c], writes=[kAC[c]])
            S.op('act', lambda e: e.activation(out=PH3, in_=ACC3[:, :, NT:NT + 16], func=AF.Copy), reads=kAC, writes=[kph])
            yield 5.0
            for c in range(8):
                wi = c // 2
                Wn = POOL_WINDOWS[wi]
                tb = c % 2
                S.op('dve', lambda e, c=c, Wn=Wn, tb=tb: e.tensor_tensor(out=q.TMP3[:, tb, :], in0=ACC3[:, c, 16:16 + NT], in1=ACC3[:, c, 16 - Wn:16 - Wn + NT],
                                                                         op=ALU.subtract), reads=[kAC[c]], writes=[p + 'TMP%d' % tb])
                S.op('dve', lambda e, c=c, Wn=Wn, tb=tb: e.scalar_tensor_tensor(out=q.BF3[:, c, :], in0=q.TMP3[:, tb, :], scalar=1.0 / Wn, in1=HNF3[:, c, HP:HP + NT],
                                                                                op0=ALU.mult, op1=ALU.subtract),
                     reads=[p + 'TMP%d' % tb, 'HNF%d' % c], writes=[p + 'BF%d' % c])
                if first:
                    t16 = q.SMQ[:, 32 + tb * 16:32 + tb * 16 + 16]
                    S.op('dve', lambda e, c=c, wi=wi, tb=tb, t16=t16: e.tensor_tensor(out=t16, in0=q.TMP3[:, tb, 0:16], in1=INVC[:, wi * 16:(wi + 1) * 16], op=ALU.mult),
                         reads=[p + 'TMP%d' % tb, 'INVC'], writes=[p + 'T16_%d' % tb])
                    S.op('dve', lambda e, c=c, t16=t16: e.tensor_tensor(out=q.BF3[:, c, 0:16], in0=t16, in1=HNF3[:, c, HP:HP + 16], op=ALU.subtract),
                         reads=[p + 'T16_%d' % tb, 'HNF%d' % c], writes=[p + 'BF%d' % c])
            yield 5.0
            PW, kpw = W.next(('poolw', j))
            for g in range(4):
                for mo in range(2):
                    c = 2 * g + mo
                    pb = c % 2
                    for ki in range(2):
                        S.op('pe', lambda e, g=g, mo=mo, ki=ki, pb=pb: e.matmul(q.PO[pb], PW[:, 2 * g + ki, mo * 128:(mo + 1) * 128], q.BF3[:, 2 * g + ki, :],
                                                                                start=(ki == 0), stop=(ki == 1)),
                             reads=[kpw, p + 'BF%d' % (2 * g + ki)], writes=[q.kPO[pb]])
                    S.op('act', lambda e, c=c, pb=pb: e.activation(out=q.TMP3[:, 2, :], in_=q.PO[pb], func=AF.Identity,
                                                                   bias=CONST[:, 8 + j * 8 + c:8 + j * 8 + c + 1], scale=pc('pool_scale', j * 8 + c)),
                         reads=[q.kPO[pb], 'PBS', 'PR'], writes=[p + 'TMP2'])
                    S.op('dve', lambda e, c=c: e.tensor_tensor(out=q.X3[:, c, :], in0=q.X3[:, c, :], in1=q.TMP3[:, 2, :], op=ALU.add),
                         reads=[p + 'X%d' % c, p + 'TMP2'], writes=[p + 'X%d' % c])
            yield 3.0

        def conv_mixer(q, L, first):
            p = q.p
            ACC3 = ACCC[:, 0:8 * NT].rearrange("p (c n) -> p c n", n=NT)
            kAC = keys('ACCcc', 8)
            S.rekey('ACCc', kAC)
            CH3 = v3(q.CONVH, HP)
            kch = p + 'CONVH'
            if first:
                S.op('dve', lambda e: e.memset(q.CONVH[:], 0.0), writes=[kch])
            yield from rmsnorm(q, 'mix_norm', L, 'bf')
            kh = [p + 'HNB%d' % c for c in range(8)]
            UB3 = HNF[:].bitcast(BF16)[:, 0:8 * HW].rearrange("p (c n) -> p c n", n=HW)
            kU = keys('HNF', 8)
            S.op('act', lambda e: e.activation(out=UB3[:, :, 0:HP], in_=CH3, func=AF.Copy), reads=[kch], writes=kU)
            for pp in range(4):
                wa, kwa = W.next(('cin', pp))
                wg, kwg = W.next(('cin', 4 + pp))
                for h in range(2):
                    c = 2 * pp + h
                    pb = c % 2
                    mm_group(q.PG[pb], q.kPG[pb], wa, kwa, h, q.HNB3, kh, 8)
                    mm_group(q.PU[pb], q.kPU[pb], wg, kwg, h, q.HNB3, kh, 8)
                    S.op('act', lambda e, c=c, pb=pb: e.activation(out=q.TMP3[:, 2, :], in_=q.PU[pb], func=AF.Sigmoid, bias=pc('conv_b_g', c), scale=1.0),
                         reads=[q.kPU[pb], 'PR'], writes=[p + 'TMP2'])
                    S.op('dve', lambda e, c=c, pb=pb: e.scalar_tensor_tensor(out=UB3[:, c, HP:HP + NT], in0=q.PG[pb], scalar=pc('conv_b_a', c), in1=q.TMP3[:, 2, :],
                                                                             op0=ALU.add, op1=ALU.mult),
                         reads=[q.kPG[pb], p + 'TMP2', 'PR'], writes=kU)
                yield 3.6
            S.op('act', lambda e: e.activation(out=CH3, in_=UB3[:, :, NT:NT + HP], func=AF.Copy), reads=kU, writes=[kch])
            o0 = HP - (CONV_W - 1)
            PSA = [q.PG[0], q.PU[0], q.PG[1], q.PU[1], q.PO[0], q.PO[1], q.PO[2], q.PO[3]]
            kPSA = [q.kPG[0], q.kPU[0], q.kPG[1], q.kPU[1], q.kPO[0], q.kPO[1], q.kPO[2], q.kPO[3]]
            for c in range(8):
                dga, kda = W.next(('dg', c, 0))
                dgb, kdb = W.next(('dg', c, 1))
                for k in range(CONV_W):
                    lh, kl = (dga[:, k, :], kda) if k < 16 else (dgb[:, k - 16, :], kdb)
                    S.op('pe', lambda e, c=c, k=k, lh=lh: e.matmul(PSA[c], lh, UB3[:, c, o0 + k:o0 + k + NT], start=(k == 0), stop=(k == CONV_W - 1)),
                         reads=[kl] + kU, writes=[kPSA[c]])
                S.op('act', lambda e, c=c: e.activation(out=ACC3[:, c, :], in_=PSA[c], func=AF.Identity, bias=pc('conv_dw_b', c), scale=1.0),
                     reads=[kPSA[c], 'PR'], writes=[kAC[c]])
                yield 4.4
            for c in range(8):
                S.op('act', lambda e, c=c: e.activation(out=q.BF3[:, c, :], in_=ACC3[:, c, :], func=AF.Copy), reads=[kAC[c]], writes=[p + 'BF%d' % c])
            for c in range(8):
                S.op('pe', lambda e, c=c: e.matmul(q.PO[2], ONESB[:], q.BF3[:, c, :], start=(c == 0), stop=(c == 7)), reads=[p + 'BF%d' % c, 'ONESB'], writes=[q.kPO[2]])
            for c in range(8):
                S.op('act', lambda e, c=c: e.activation(out=q.BF3[:, c, :], in_=ACC3[:, c, :], func=AF.Square), reads=[kAC[c]], writes=[p + 'BF%d' % c])
            for c in range(8):
                S.op('pe', lambda e, c=c: e.matmul(q.PO[3], ONESB[:], q.BF3[:, c, :], start=(c == 0), stop=(c == 7)), reads=[p + 'BF%d' % c, 'ONESB'], writes=[q.kPO[3]])
            S.op('act', lambda e: e.activation(out=q.TMP3[:, 0, :], in_=q.PO[2], func=AF.Square), reads=[q.kPO[2]], writes=[p + 'TMP0'])
            S.op('dve', lambda e: e.tensor_tensor(out=q.TMP3[:, 1, :], in0=q.PO[3], in1=q.TMP3[:, 0, :], op=ALU.subtract), reads=[q.kPO[3], p + 'TMP0'], writes=[p + 'TMP1'])
            S.op('act', lambda e: e.activation(out=q.TMP3[:, 0, :], in_=q.TMP3[:, 1, :], func=AF.Sqrt, bias=CONST[:, 1:2], scale=1.0), reads=[p + 'TMP1', 'CONST'], writes=[p + 'TMP0'])
            S.op('dve', lambda e: e.reciprocal(out=q.TMP3[:, 1, :], in_=q.TMP3[:, 0, :]), reads=[p + 'TMP0'], writes=[p + 'TMP1'])
            yield 4.0
            for c in range(8):
                S.op('dve', lambda e, c=c: e.tensor_tensor(out=ACC3[:, c, :], in0=ACC3[:, c, :], in1=q.PO[2], op=ALU.subtract),
                     reads=[kAC[c], q.kPO[2]], writes=[kAC[c]])
            for c in range(8):
                S.op('dve', lambda e, c=c: e.tensor_tensor(out=ACC3[:, c, :], in0=ACC3[:, c, :], in1=q.TMP3[:, 1, :], op=ALU.mult),
                     reads=[kAC[c], p + 'TMP1'], writes=[kAC[c]])
            for c in range(8):
                S.op('act', lambda e, c=c: e.activation(out=q.BF3[:, c, :], in_=ACC3[:, c, :], func=AF.Silu, bias=pc('conv_ln_b', c), scale=pc('conv_ln_g', c)),
                     reads=[kAC[c], 'PR'], writes=[p + 'BF%d' % c])
            yield 5.0
            kb = [p + 'BF%d' % c for c in range(8)]
            for pp in range(4):
                wo, kwo = W.next(('cout', pp))
                for h in range(2):
                    c = 2 * pp + h
                    pb = c % 2
                    mm_group(q.PO[pb], q.kPO[pb], wo, kwo, h, q.BF3, kb, 8)
                    S.op('dve', lambda e, c=c, pb=pb: e.tensor_tensor(out=q.X3[:, c, :], in0=q.X3[:, c, :], in1=q.PO[pb], op=ALU.add),
                         reads=[p + 'X%d' % c, q.kPO[pb]], writes=[p + 'X%d' % c])
                yield 1.8

        def ssm_mixer(q, L, first):
            p = q.p
            if not ssm_ready[0] and not S.dry:
                ssm_ready[0] = True
                setup_ssm()
            kAC = keys('ACCss', 9)
            S.rekey('ACCs', kAC)
            A = lambda i: ACC[:, i * 512:(i + 1) * 512]
            if first:
                S.op('dve', lambda e: e.memset(q.WLR[:], 0.0), writes=[p + 'WLR'])
                S.op('dve', lambda e: e.memset(q.WLI[:], 0.0), writes=[p + 'WLI'])
            yield from rmsnorm(q, 'mix_norm', L, 'bf', rstd_sbuf=True)
            tt = lambda eng, o, ko, a, ka, bb, kbb, op: S.op(eng, lambda e: e.tensor_tensor(out=o, in0=a, in1=bb, op=op), reads=list(ka) + list(kbb), writes=list(ko))
            it = 0
            pending = None

            def emit_cmm(c, b, xre, xim, kxr, kxi):
                pby = c % 2
                t0 = b * SB
                for j in range(4):
                    stt = 4 * c + j
                    S.op('pe', lambda e, j=j, stt=stt: e.matmul(q.POY[pby][64 * (j // 2):64 * (j // 2) + 64, t0:t0 + SB], CRE3[:, stt, :], xre[:, j * SB:(j + 1) * SB],
                                                                start=(j % 2 == 0), stop=False), reads=['CRE', kxr], writes=[q.kPOY[pby]])
                    S.op('pe', lambda e, j=j, stt=stt: e.matmul(q.POY[pby][64 * (j // 2):64 * (j // 2) + 64, t0:t0 + SB], NCIM3[:, stt, :], xim[:, j * SB:(j + 1) * SB],
                                                                start=False, stop=(j % 2 == 1)), reads=['NCIM', kxi], writes=[q.kPOY[pby]])

            def epilogue(c):
                pby = c % 2
                S.op('dve', lambda e: e.scalar_tensor_tensor(out=q.TMP3[:, 0, :], in0=q.X3[:, c, :], scalar=CONST[:, 24 + c:25 + c], in1=q.RSTD[:],
                                                             op0=ALU.mult, op1=ALU.mult), reads=[p + 'X%d' % c, 'DG', p + 'RSTD'], writes=[p + 'TMP0'])
                S.op('dve', lambda e: e.tensor_tensor(out=q.TMP3[:, 0, :], in0=q.TMP3[:, 0, :], in1=q.POY[pby], op=ALU.add),
                     reads=[p + 'TMP0', q.kPOY[pby]], writes=[p + 'TMP0'])
                S.op('act', lambda e: e.activation(out=q.TMP3[:, 1, :], in_=q.TMP3[:, 0, :], func=AF.Square), reads=[p + 'TMP0'], writes=[p + 'TMP1'])
                S.op('dve', lambda e: e.tensor_scalar(out=q.TMP3[:, 1, :], in0=q.TMP3[:, 1, :], scalar1=0.044715, scalar2=1.0, op0=ALU.mult, op1=ALU.add),
                     reads=[p + 'TMP1'], writes=[p + 'TMP1'])
                S.op('dve', lambda e: e.tensor_tensor(out=q.TMP3[:, 1, :], in0=q.TMP3[:, 1, :], in1=q.TMP3[:, 0, :], op=ALU.mult), reads=[p + 'TMP1', p + 'TMP0'], writes=[p + 'TMP1'])
                S.op('act', lambda e: e.activation(out=q.TMP3[:, 2, :], in_=q.TMP3[:, 1, :], func=AF.Sigmoid, scale=2.0 * math.sqrt(2.0 / math.pi)),
                     reads=[p + 'TMP1'], writes=[p + 'TMP2'])
                S.op('dve', lambda e: e.tensor_tensor(out=q.ACTB3[:, c, :], in0=q.TMP3[:, 0, :], in1=q.TMP3[:, 2, :], op=ALU.mult),
                     reads=[p + 'TMP0', p + 'TMP2'], writes=[p + 'ACTB%d' % c])

            for c in range(8):
                Ct = CT[:, c * 512:(c + 1) * 512]
                St = ST[:, c * 512:(c + 1) * 512]
                Rt = RT[:, c * 512:(c + 1) * 512]
                for b in range(NT // SB):
                    t0 = b * SB
                    par = it % 2
                    it += 1
                    for j in range(4):
                        S.op('pe', lambda e, j=j, c=c, t0=t0: e.matmul(q.PGF[:, j * SB:(j + 1) * SB], BRE3[:, c, j, :], q.HNB3[:, c, t0:t0 + SB], start=True, stop=True),
                             reads=['BRE', p + 'HNB%d' % c], writes=q.kPGF)
                    for j in range(4):
                        S.op('pe', lambda e, j=j, c=c, t0=t0: e.matmul(q.PUF[:, j * SB:(j + 1) * SB], BIM3[:, c, j, :], q.HNB3[:, c, t0:t0 + SB], start=True, stop=True),
                             reads=['BIM', p + 'HNB%d' % c], writes=q.kPUF)
                    tt('dve', A(0), [kAC[0]], Ct, ['CT'], q.PGF[:], q.kPGF, ALU.mult)
                    tt('dve', A(1), [kAC[1]], St, ['ST'], q.PUF[:], q.kPUF, ALU.mult)
                    tt('dve', A(2), [kAC[2]], Ct, ['CT'], q.PUF[:], q.kPUF, ALU.mult)
                    tt('dve', A(0), [kAC[0]], A(0), [kAC[0]], A(1), [kAC[1]], ALU.add)
                    tt('dve', A(1), [kAC[1]], St, ['ST'], q.PGF[:], q.kPGF, ALU.mult)
                    tt('dve', A(2), [kAC[2]], A(2), [kAC[2]], A(1), [kAC[1]], ALU.subtract)
                    yield 3.3
                    wlr = q.WLR[:, 4 * c:4 * c + 4]
                    wli = q.WLI[:, 4 * c:4 * c + 4]
                    rc = SM[:, 12 * 32 + 4 * c:12 * 32 + 4 * c + 4]
                    rs = SM[:, 13 * 32 + 4 * c:13 * 32 + 4 * c + 4]
                    sm = lambda i: q.SMQ[:, 4 * i:4 * i + 4]
                    ksm = lambda i: [p + 'SMi%d' % i]
                    v0r = A(0).rearrange("p (a b) -> p a b", b=SB)[:, :, 0]
                    v0i = A(2).rearrange("p (a b) -> p a b", b=SB)[:, :, 0]
                    tt('dve', sm(0), ksm(0), rc, ['SM'], wlr, [p + 'WLR'], ALU.mult)
                    tt('dve', sm(1), ksm(1), rs, ['SM'], wli, [p + 'WLI'], ALU.mult)
                    tt('dve', sm(2), ksm(2), rs, ['SM'], wlr, [p + 'WLR'], ALU.mult)
                    tt('dve', sm(3), ksm(3), rc, ['SM'], wli, [p + 'WLI'], ALU.mult)
                    tt('dve', sm(4), ksm(4), sm(0), ksm(0), sm(1), ksm(1), ALU.subtract)
                    tt('dve', sm(5), ksm(5), sm(2), ksm(2), sm(3), ksm(3), ALU.add)
                    tt('dve', v0r, [kAC[0]], v0r, [kAC[0]], sm(4), ksm(4), ALU.add)
                    tt('dve', v0i, [kAC[2]], v0i, [kAC[2]], sm(5), ksm(5), ALU.add)
                    wr_, wi_ = A(3 + par), A(5 + par)
                    kwr, kwi = kAC[3 + par], kAC[5 + par]
                    S.op('dve', lambda e, Rt=Rt, wr_=wr_: e.tensor_tensor_scan(out=wr_, data0=Rt, data1=A(0), initial=0.0, op0=ALU.mult, op1=ALU.add),
                         reads=['RT', kAC[0]], writes=[kwr])
                    S.op('dve', lambda e, Rt=Rt, wi_=wi_: e.tensor_tensor_scan(out=wi_, data0=Rt, data1=A(2), initial=0.0, op0=ALU.mult, op1=ALU.add),
                         reads=['RT', kAC[2]], writes=[kwi])
                    wlast_r = wr_.rearrange("p (a b) -> p a b", b=SB)[:, :, SB - 1]
                    wlast_i = wi_.rearrange("p (a b) -> p a b", b=SB)[:, :, SB - 1]
                    S.op('act', lambda e, wlr=wlr, wlast_r=wlast_r: e.activation(out=wlr, in_=wlast_r, func=AF.Copy), reads=[kwr], writes=[p + 'WLR'])
                    S.op('act', lambda e, wli=wli, wlast_i=wlast_i: e.activation(out=wli, in_=wlast_i, func=AF.Copy), reads=[kwi], writes=[p + 'WLI'])
                    xre = q.BF[:, par * 1024:par * 1024 + 512]
                    xim = q.BF[:, par * 1024 + 512:par * 1024 + 1024]
                    kxr = [p + 'BF%d' % (par * 4), p + 'BF%d' % (par * 4 + 1)]
                    kxi = [p + 'BF%d' % (par * 4 + 2), p + 'BF%d' % (par * 4 + 3)]
                    tt('pool', A(7), [kAC[7]], Ct, ['CT'], wr_, [kwr], ALU.mult)
                    tt('pool', A(8), [kAC[8]], St, ['ST'], wi_, [kwi], ALU.mult)
                    tt('pool', xre, kxr, A(7), [kAC[7]], A(8), [kAC[8]], ALU.subtract)
                    tt('pool', A(7), [kAC[7]], St, ['ST'], wr_, [kwr], ALU.mult)
                    tt('pool', A(8), [kAC[8]], Ct, ['CT'], wi_, [kwi], ALU.mult)
                    tt('pool', xim, kxi, A(7), [kAC[7]], A(8), [kAC[8]], ALU.add)
                    if pending is not None:
                        emit_cmm(*pending[:6])
                        if pending[6]:
                            epilogue(pending[0])
                    pending = (c, b, xre, xim, kxr[0], kxi[0], b == NT // SB - 1)
                    yield 3.2
            emit_cmm(*pending[:6])
            epilogue(pending[0])
            yield 3.0
            kg = [p + 'ACTB%d' % c for c in range(8)]
            for pp in range(4):
                wa, kwa = W.next(('swa', pp))
                wb, kwb = W.next(('swb', pp))
                for h in range(2):
                    c = 2 * pp + h
                    pb = c % 2
                    mm_group(q.PG[pb], q.kPG[pb], wa, kwa, h, q.ACTB3, kg, 8)
                    mm_group(q.PU[pb], q.kPU[pb], wb, kwb, h, q.ACTB3, kg, 8)
                    S.op('act', lambda e, pb=pb: e.activation(out=q.TMP3[:, 2, :], in_=q.PU[pb], func=AF.Sigmoid), reads=[q.kPU[pb]], writes=[p + 'TMP2'])
                    S.op('dve', lambda e, pb=pb: e.tensor_tensor(out=q.TMP3[:, 2, :], in0=q.TMP3[:, 2, :], in1=q.PG[pb], op=ALU.mult),
                         reads=[p + 'TMP2', q.kPG[pb]], writes=[p + 'TMP2'])
                    S.op('dve', lambda e, c=c: e.tensor_tensor(out=q.X3[:, c, :], in0=q.X3[:, c, :], in1=q.TMP3[:, 2, :], op=ALU.add),
                         reads=[p + 'X%d' % c, p + 'TMP2'], writes=[p + 'X%d' % c])
                yield 3.6

        xsrc = xT.rearrange("(c p) t -> p c t", p=128)
        odst = outT.rearrange("(c p) t -> p c t", p=128)

        def stream_prog(q):
            p = q.p
            for ti in range(NTILES):
                first = (ti == 0)
                t0 = q.s * SEQ + ti * NT
                if ti > 0:
                    S.dma('pool', lambda e, t0=t0: e.dma_start(out=q.X3, in_=xsrc[:, :, t0:t0 + NT]), writes=[p + 'X%d' % c for c in range(8)])
                for L in range(depth_run):
                    kind = L % 3
                    lk = 'S' if kind == 2 else 'C'
                    yield ('lock', lk)
                    if kind == 0:
                        yield from pool_mixer(q, L, first)
                    elif kind == 1:
                        yield from conv_mixer(q, L, first)
                    else:
                        yield from ssm_mixer(q, L, first)
                    yield ('unlock', lk)
                    yield from ffn(q, L)
                yield ('lock', 'C')
                if do_final:
                    yield from rmsnorm(q, 'final_norm', 0, 'f32')
                    S.dma('pool', lambda e, t0=t0: e.dma_start(out=odst[:, :, t0:t0 + NT], in_=HNF3[:, :, HP:HP + NT]), reads=keys('HNF', 8), writes=['outT'])
                else:
                    S.dma('pool', lambda e, t0=t0: e.dma_start(out=odst[:, :, t0:t0 + NT], in_=q.X3), reads=[p + 'X%d' % c for c in range(8)], writes=['outT'])
                yield ('unlock', 'C')

        def drive():
            gens = [stream_prog(q) for q in streams]
            t = [0.0, 0.5]
            done = [False, False]
            blocked = [None, None]
            locks = {'C': None, 'S': None}
            while not all(done):
                cand = [s for s in range(2) if not done[s] and blocked[s] is None]
                assert cand, "driver deadlock"
                s = min(cand, key=lambda i: t[i])
                try:
                    r = next(gens[s])
                except StopIteration:
                    done[s] = True
                    continue
                if isinstance(r, tuple) and r[0] == 'lock':
                    if locks[r[1]] is None:
                        locks[r[1]] = s
                    else:
                        blocked[s] = r[1]
                elif isinstance(r, tuple) and r[0] == 'unlock':
                    assert locks[r[1]] == s
                    locks[r[1]] = None
                    o = 1 - s
                    if blocked[o] == r[1]:
                        blocked[o] = None
                        locks[r[1]] = o
                        t[o] = max(t[o], t[s])
                else:
                    t[s] += r
                    cast_ahead(2)

        S.dry = True
        W.record = True
        drive()
        S.dry = False
        W.record = False
        for q in streams:
            t0 = q.s * SEQ
            S.dma('pool', lambda e, q=q, t0=t0: e.dma_start(out=q.X3, in_=xsrc[:, :, t0:t0 + NT]), writes=[q.p + 'X%d' % c for c in range(8)])
        setup()
        S.rekey('HNF', keys('HNF', 8))
        drive()
        assert ssm_ready[0] or depth_run < 3
        S.emit()
    return nc


def _chunked(v):
    return np.ascontiguousarray(np.asarray(v, np.float32).reshape(8, 128).T)


def _prep_shared(inp):
    P = np.zeros((128, NPAR), np.float32)

    def put(name, idx, arr):
        o = PCOL[name] + idx
        P[:, o:o + arr.shape[1]] = arr

    for L in range(DEPTH):
        put("mix_norm", L * 8, _chunked(inp["mix_norm"][L]))
        put("ffn_norm", L * 8, _chunked(inp["ffn_norm"][L]))
    put("final_norm", 0, _chunked(inp["final_norm"]))
    for j in range(2):
        put("pool_b", j * 8, _chunked(inp["pool_b"][j]))
        put("pool_scale", j * 8, _chunked(inp["pool_scale"][j]))
    put("conv_b_a", 0, _chunked(inp["conv_b_in"][0][:D_MODEL]))
    put("conv_b_g", 0, _chunked(inp["conv_b_in"][0][D_MODEL:]))
    for k in range(CONV_W):
        put("conv_dw", k * 8, _chunked(inp["conv_dw"][0][k]))
    put("conv_dw_b", 0, _chunked(inp["conv_dw_b"][0]))
    put("conv_ln_g", 0, _chunked(inp["conv_ln_g"][0]))
    put("conv_ln_b", 0, _chunked(inp["conv_ln_b"][0]))
    put("ssm_d", 0, _chunked(inp["ssm_d"][0]))
    put("lamre", 0, np.asarray(inp["ssm_lam_re"][0], np.float32).reshape(32, 128).T)
    put("lamim", 0, np.asarray(inp["ssm_lam_im"][0], np.float32).reshape(32, 128).T)
    put("logdt", 0, np.repeat(np.asarray(inp["ssm_log_dt"][0], np.float32), 64).reshape(32, 128).T)

    b_re = np.asarray(inp["ssm_b_re"][0], np.float32)
    b_im = np.asarray(inp["ssm_b_im"][0], np.float32)
    c_re = np.asarray(inp["ssm_c_re"][0], np.float32)
    c_im = np.asarray(inp["ssm_c_im"][0], np.float32)
    bre_l = np.zeros((128, 8, 4, 128), np.float32)
    bim_l = np.zeros((128, 8, 4, 128), np.float32)
    cre_l = np.zeros((128, 32, 64), np.float32)
    cim_l = np.zeros((128, 32, 64), np.float32)
    for st in range(32):
        q, j = st // 4, st % 4
        for gg in range(2):
            g = 2 * st + gg
            bre_l[32 * j + gg * 16:32 * j + gg * 16 + 16, q, j, gg * 64:(gg + 1) * 64] = b_re[g].T
            bim_l[32 * j + gg * 16:32 * j + gg * 16 + 16, q, j, gg * 64:(gg + 1) * 64] = b_im[g].T
            co = 32 * (j % 2) + gg * 16
            cre_l[gg * 64:(gg + 1) * 64, st, co:co + 16] = c_re[g].T
            cim_l[gg * 64:(gg + 1) * 64, st, co:co + 16] = c_im[g].T
    f = lambda a: np.ascontiguousarray(np.asarray(a, np.float32))
    shared = {
        "params": P,
        "tau1": np.ascontiguousarray(np.broadcast_to(np.arange(1, SB + 1, dtype=np.float32), (128, SB))),
        "ident": np.eye(128, dtype=np.float32),
        "w_gate": f(inp["w_gate"]), "w_up": f(inp["w_up"]), "w_down": f(inp["w_down"]),
        "pool_w": f(inp["pool_w"]),
        "conv_w_in": f(inp["conv_w_in"][0]), "conv_w_out": f(inp["conv_w_out"][0]),
        "ssm_wa": f(inp["ssm_w_glu_a"][0]), "ssm_wb": f(inp["ssm_w_glu_b"][0]),
        "bre_l": bre_l.reshape(128, 4096), "bim_l": bim_l.reshape(128, 4096),
        "cre_l": cre_l.reshape(128, 2048), "cim_l": cim_l.reshape(128, 2048),
    }
    return shared


_NC_CACHE = {}


def run(inputs, depth_run=DEPTH, do_final=True, cores=N_CORES, trace=False):
    key = (depth_run, do_final)
    if key not in _NC_CACHE:
        _NC_CACHE[key] = build_nc(depth_run, do_final)
    nc = _NC_CACHE[key]
    shared = _prep_shared(inputs)
    x = np.asarray(inputs["x"], np.float32)
    in_maps = []
    for i in range(cores):
        xt = np.ascontiguousarray(x[2 * i:2 * i + 2].reshape(TOK, D_MODEL).T)
        m = dict(shared)
        m["xT"] = xt
        in_maps.append(m)
    res = run_bass_kernel_spmd(nc, in_maps, core_ids=list(range(cores)), **({"trace": True} if trace else {}))
    outs = []
    for i in range(cores):
        o = np.asarray(res.results[i]["outT"], np.float32)
        outs.append(o.T.reshape(2, SEQ, D_MODEL))
    return np.concatenate(outs, axis=0), res


def kernel(**inputs):
    out, _ = run(inputs)
    return out.astype(np.float32)
```

```python
import math
import numpy as np
from contextlib import ExitStack
import concourse.bass as bass
import concourse.mybir as mybir
from concourse.bass_utils import run_bass_kernel_spmd

F32 = mybir.dt.float32
BF16 = mybir.dt.bfloat16
ALU = mybir.AluOpType
AF = mybir.ActivationFunctionType

D_MODEL = 1024
SEQ = 2048
DEPTH = 4
D_FF = 2816
NCH = 8
NFF = 22
POOL_WINDOWS = (2, 4, 8, 16)
CONV_W = 31
RMS_EPS = 1e-6
LN_EPS = 1e-5
N_CORES = 8
TOK = 4096
NT = 256
NTILES = SEQ // NT
TPS = SEQ // NT
HP = 32
SB = 128
NSLOT = 5
SLOT_ELEMS = 2816

PCOL = {}
_off = 0
for _name, _n in [("mix_norm", 32), ("ffn_norm", 32), ("final_norm", 8), ("pool_b", 16), ("pool_scale", 16),
                  ("conv_b_a", 8), ("conv_b_g", 8), ("conv_dw", 248), ("conv_dw_b", 8), ("conv_ln_g", 8),
                  ("conv_ln_b", 8), ("ssm_d", 8), ("lamre", 32), ("lamim", 32), ("logdt", 32)]:
    PCOL[_name] = _off
    _off += _n
NPAR = _off


class Sched:
    EPOCH = 4000

    def __init__(self, nc, es, n_dma_sems=6):
        self.nc = nc
        self.es = es
        self.h = {'pe': nc.tensor, 'dve': nc.vector, 'act': nc.scalar, 'pool': nc.gpsimd, 'sp': nc.sync}
        self.prog = {e: [] for e in self.h}
        self.cnt = {e: 0 for e in self.h}
        self.sems = {}
        self.waited = {}
        self.last_w = {}
        self.readers = {}
        self.n_dma_sems = n_dma_sems
        self.dma_rr = {e: 0 for e in self.h}
        self.dma_cnt = {}
        self.dry = False

    def rekey(self, prefix, newkeys):
        if self.dry:
            return
        toks = set()
        for k in [k for k in self.last_w if k.startswith(prefix)]:
            t = self.last_w.pop(k)
            if t is not None:
                toks.add(t)
        for k in [k for k in self.readers if k.startswith(prefix)]:
            toks.update(self.readers.pop(k))
        for k in newkeys:
            self.last_w[k] = None
            self.readers[k] = list(toks)

    def _sem(self, key):
        if key not in self.sems:
            self.sems[key] = self.es.enter_context(self.nc.semaphore("s_" + "_".join(str(k) for k in key)))
        return self.sems[key]

    def _deps(self, eng, reads, writes, extra=()):
        deps = set(extra)
        for k in reads:
            t = self.last_w.get(k)
            if t is not None:
                deps.add(t)
            if k.startswith('PSB') and eng != 'pe':
                t = self.last_w.get(k[:-1] + ('1' if k[-1] == '0' else '0'))
                if t is not None:
                    deps.add(t)
        for k in writes:
            t = self.last_w.get(k)
            if t is not None:
                deps.add(t)
            for t in self.readers.get(k, ()):
                deps.add(t)
            if k.startswith('PSB'):
                sib = k[:-1] + ('1' if k[-1] == '0' else '0')
                if eng == 'pe':
                    for t in self.readers.get(sib, ()):
                        deps.add(t)
                t = self.last_w.get(sib)
                if t is not None:
                    deps.add(t)
        need = {}
        for (s, v, e) in deps:
            if e == 'pe' and eng == 'pe':
                continue
            if need.get(s, 0) < v:
                need[s] = v
        waits = []
        for s, v in need.items():
            if self.waited.get((eng, s), 0) < v:
                self.waited[(eng, s)] = v
                waits.append((s, v))
        return waits

    def _commit(self, tok, reads, writes):
        for k in writes:
            self.last_w[k] = tok
            self.readers[k] = []
        for k in reads:
            if k not in writes:
                self.readers.setdefault(k, []).append(tok)

    def op(self, eng, fn, reads=(), writes=()):
        if self.dry:
            return None
        waits = self._deps(eng, reads, writes)
        self.cnt[eng] += 1
        c = self.cnt[eng]
        ep = (c - 1) // self.EPOCH
        skey = (eng, ep)
        self._sem(skey)
        tok = (skey, c - ep * self.EPOCH, eng)
        self.prog[eng].append((waits, fn, skey, 1))
        self._commit(tok, reads, writes)
        return tok

    def dma(self, q, fn, reads=(), writes=()):
        if self.dry:
            return None
        j = self.dma_rr[q]
        self.dma_rr[q] = (j + 1) % self.n_dma_sems
        skey = ('dma', q, j)
        self._sem(skey)
        n = self.dma_cnt.get(skey, 0)
        extra = [(skey, 16 * n, 'dma')] if n > 0 else []
        waits = self._deps(q, reads, writes, extra)
        self.dma_cnt[skey] = n + 1
        tok = (skey, 16 * (n + 1), 'dma')
        self.prog[q].append((waits, fn, skey, 16))
        self._commit(tok, reads, writes)
        return tok

    def emit(self, final_engine='sp'):
        finals = []
        for e in self.h:
            c = self.cnt[e]
            if c > 0:
                ep = (c - 1) // self.EPOCH
                finals.append(((e, ep), c - ep * self.EPOCH))
        for skey, n in self.dma_cnt.items():
            finals.append((skey, 16 * n))
        sems = self.sems
        prog = self.prog
        with self.nc.Block() as block:
            def mk(e):
                def body(engh):
                    for waits, fn, skey, inc in prog[e]:
                        for s, v in waits:
                            engh.wait_ge(sems[s], v)
                        fn(engh).then_inc(sems[skey], inc)
                    if e == final_engine:
                        for s, v in finals:
                            engh.wait_ge(sems[s], v)
                return body
            block.sync(mk('sp'))
            block.tensor(mk('pe'))
            block.vector(mk('dve'))
            block.scalar(mk('act'))
            block.gpsimd(mk('pool'))


def build_nc(depth_run=DEPTH, do_final=True):
    nc = bass.Bass("TRN2", target_bir_lowering=False)

    def D(name, shape, dt, kind="ExternalInput"):
        return nc.dram_tensor(name, list(shape), dt, kind=kind).ap()

    xT = D("xT", [D_MODEL, TOK], F32)
    outT = D("outT", [D_MODEL, TOK], F32, "ExternalOutput")
    params = D("params", [128, NPAR], F32)
    tau1 = D("tau1", [128, SB], F32)
    w_gate = D("w_gate", [DEPTH, D_MODEL, D_FF], F32)
    w_up = D("w_up", [DEPTH, D_MODEL, D_FF], F32)
    w_down = D("w_down", [DEPTH, D_FF, D_MODEL], F32)
    pool_w = D("pool_w", [2, 4, 256, 256], F32)
    conv_w_in = D("conv_w_in", [D_MODEL, 2 * D_MODEL], F32)
    conv_w_out = D("conv_w_out", [D_MODEL, D_MODEL], F32)
    ssm_wa = D("ssm_wa", [D_MODEL, D_MODEL], F32)
    ssm_wb = D("ssm_wb", [D_MODEL, D_MODEL], F32)
    bre_l = D("bre_l", [128, 8 * 4 * 128], F32)
    bim_l = D("bim_l", [128, 8 * 4 * 128], F32)
    cre_l = D("cre_l", [128, 32 * 64], F32)
    cim_l = D("cim_l", [128, 32 * 64], F32)
    ident = D("ident", [128, 128], F32)

    wg_s = D("wg_s", [DEPTH, 11, 128, 8, 256], BF16, "Internal")
    wu_s = D("wu_s", [DEPTH, 11, 128, 8, 256], BF16, "Internal")
    wd_s = D("wd_s", [DEPTH, 4, 2, 128, 11, 256], BF16, "Internal")
    cin_s = D("cin_s", [8, 128, 8, 256], BF16, "Internal")
    cout_s = D("cout_s", [4, 128, 8, 256], BF16, "Internal")
    swa_s = D("swa_s", [4, 128, 8, 256], BF16, "Internal")
    swb_s = D("swb_s", [4, 128, 8, 256], BF16, "Internal")
    poolw_s = D("poolw_s", [2, 128, 8, 256], BF16, "Internal")
    dg_s = D("dg_s", [8, 128, CONV_W, 128], BF16, "Internal")

    with ExitStack() as es:
        S = Sched(nc, es)

        def sb(name, shape, dt):
            return es.enter_context(nc.sbuf_tensor(name, list(shape), dt))

        def v3(t, n):
            return t[:].rearrange("p (c n) -> p c n", n=n)

        def keys(name, n):
            return ['%s%d' % (name, i) for i in range(n)]

        HW = HP + NT
        AW = 16 + NT
        HNF = sb("HNF", [128, 8 * HW], F32)
        ACC = sb("ACC", [128, 4608], F32)
        ACCC = sb("ACCC", [128, 8 * AW], F32)
        WS = [sb("WS%d" % i, [128, SLOT_ELEMS], BF16) for i in range(NSLOT)]
        CT = sb("CT", [128, 32 * SB], F32)
        ST = sb("ST", [128, 32 * SB], F32)
        RT = sb("RT", [128, 32 * SB], F32)
        PR = sb("PR", [128, NPAR], F32)
        ONESF = sb("ONESF", [128, NT], F32)
        ONESB = sb("ONESB", [128, 128], BF16)
        BRE = sb("BRE", [128, 8 * 4 * 128], BF16)
        BIM = sb("BIM", [128, 8 * 4 * 128], BF16)
        CRE = sb("CRE", [128, 32 * 64], BF16)
        NCIM = sb("NCIM", [128, 32 * 64], BF16)
        SM = sb("SM", [128, 512], F32)
        INVC = sb("INVC", [128, 4 * 16], F32)
        CONST = sb("CONST", [128, 32], F32)
        PSB = [es.enter_context(nc.psum_tensor("PSB%d" % i, [128, 512], F32)) for i in range(8)]
        HNF3 = v3(HNF, HW)
        BRE3 = BRE[:].rearrange("p (q jj s) -> p q jj s", jj=4, s=128)
        BIM3 = BIM[:].rearrange("p (q jj s) -> p q jj s", jj=4, s=128)
        CRE3 = v3(CRE, 64)
        NCIM3 = v3(NCIM, 64)
        CT3 = v3(CT, SB)
        ST3 = v3(ST, SB)
        RT3 = v3(RT, SB)

        class Q:
            pass

        streams = []
        for s in range(2):
            q = Q()
            q.s = s
            q.p = 'q%d_' % s
            q.X = sb("X%d" % s, [128, 8 * NT], F32)
            q.HNB = sb("HNB%d" % s, [128, 8 * NT], BF16)
            q.BF = sb("BF%d" % s, [128, 8 * NT], BF16)
            q.TMP = sb("TMP%d" % s, [128, 3 * NT], F32)
            q.ACTB = sb("ACTB%d" % s, [128, NFF * NT], BF16)
            q.POOLH = [sb("POOLH%d_%d" % (s, j), [128, 8 * 16], F32) for j in range(2)]
            q.CONVH = sb("CONVH%d" % s, [128, 8 * HP], F32)
            q.WLR = sb("WLR%d" % s, [128, 32], F32)
            q.WLI = sb("WLI%d" % s, [128, 32], F32)
            q.SMQ = sb("SMQ%d" % s, [128, 64], F32)
            q.RSTD = sb("RSTD%d" % s, [128, NT], F32)
            q.X3 = v3(q.X, NT)
            q.HNB3 = v3(q.HNB, NT)
            q.BF3 = v3(q.BF, NT)
            q.TMP3 = v3(q.TMP, NT)
            q.ACTB3 = v3(q.ACTB, NT)
            B = PSB[4 * s:4 * s + 4]

            def hb(i, h, B=B):
                return B[i][:, h * NT:(h + 1) * NT]

            def khb(i, h, s=s):
                return 'PSB%dh%d' % (4 * s + i, h)
            q.PG = [hb(0, 0), hb(2, 0)]
            q.kPG = [khb(0, 0), khb(2, 0)]
            q.PU = [hb(1, 0), hb(3, 0)]
            q.kPU = [khb(1, 0), khb(3, 0)]
            q.PO = [hb(0, 1), hb(1, 1), hb(2, 1), hb(3, 1)]
            q.kPO = [khb(0, 1), khb(1, 1), khb(2, 1), khb(3, 1)]
            q.RING = [(hb(i, 0), hb(i, 1)) for i in range(4)]
            q.kRING = [(khb(i, 0), khb(i, 1)) for i in range(4)]
            q.PGF = B[0]
            q.kPGF = [khb(0, 0), khb(0, 1)]
            q.PUF = B[1]
            q.kPUF = [khb(1, 0), khb(1, 1)]
            q.POY = [hb(2, 0), hb(3, 0)]
            q.kPOY = [khb(2, 0), khb(3, 0)]
            streams.append(q)

        def pc(name, idx=0, n=1):
            o = PCOL[name] + idx
            return PR[:, o:o + n]

        def layer_wlist(L):
            lst = []
            kind = L % 3
            if kind == 0:
                lst.append(('poolw', L // 3))
            if kind == 1:
                for p in range(4):
                    lst.append(('cin', p))
                    lst.append(('cin', 4 + p))
                for c in range(8):
                    lst.append(('dg', c, 0))
                    lst.append(('dg', c, 1))
                for p in range(4):
                    lst.append(('cout', p))
            elif kind == 2:
                for p in range(4):
                    lst.append(('swa', p))
                    lst.append(('swb', p))
            for g in range(11):
                lst.append(('wg', L, g))
                lst.append(('wu', L, g))
            for mp in range(4):
                lst.append(('wd', L, mp, 0))
                lst.append(('wd', L, mp, 1))
            return lst

        def w_src(d):
            k = d[0]
            if k == 'wg':
                return wg_s[d[1], d[2]], w_gate[d[1]].rearrange("(kc k) (g m) -> g k kc m", k=128, m=256)[d[2]]
            if k == 'wu':
                return wu_s[d[1], d[2]], w_up[d[1]].rearrange("(kc k) (g m) -> g k kc m", k=128, m=256)[d[2]]
            if k == 'wd':
                src = w_down[d[1]].rearrange("(kh kc k) (mp m) -> mp kh k kc m", kh=2, kc=11, k=128, m=256)
                return wd_s[d[1], d[2], d[3]], src[d[2], d[3]]
            if k == 'cin':
                return cin_s[d[1]], conv_w_in.rearrange("(kc k) (g m) -> g k kc m", k=128, m=256)[d[1]]
            if k == 'cout':
                return cout_s[d[1]], conv_w_out.rearrange("(kc k) (g m) -> g k kc m", k=128, m=256)[d[1]]
            if k == 'swa':
                return swa_s[d[1]], ssm_wa.rearrange("(kc k) (g m) -> g k kc m", k=128, m=256)[d[1]]
            if k == 'dg':
                k0, k1 = (0, 16) if d[2] == 0 else (16, CONV_W)
                return dg_s[d[1]][:, k0:k1, :], None
            if k == 'poolw':
                return poolw_s[d[1]], pool_w[d[1]].rearrange("g (ki k) d -> k (g ki) d", k=128)
            if k == 'swb':
                return swb_s[d[1]], ssm_wb.rearrange("(kc k) (g m) -> g k kc m", k=128, m=256)[d[1]]
            raise KeyError(k)

        tile_wlist = []
        for L in range(depth_run):
            tile_wlist += layer_wlist(L)

        ssm_ready = [False]
        cast_done = set()
        cast_order = list(tile_wlist)
        cast_pos = [0]

        def ensure_cast(d):
            if S.dry or d in cast_done or d[0] == 'dg':
                return
            cast_done.add(d)
            scr, src = w_src(d)
            S.dma('pool', lambda e, scr=scr, src=src: e.dma_start(out=scr, in_=src), writes=['scr_' + '_'.join(map(str, d))])

        def cast_ahead(n):
            if S.dry:
                return
            while n > 0 and cast_pos[0] < len(cast_order):
                d = cast_order[cast_pos[0]]
                cast_pos[0] += 1
                if d not in cast_done:
                    ensure_cast(d)
                    n -= 1

        class WStream:
            def __init__(self):
                self.record = True
                self.descs = []
                self.issued = 0
                self.used = 0

            def _issue(self, i):
                d = self.descs[i]
                ensure_cast(d)
                scr, _ = w_src(d)
                a, m = scr.shape[1], scr.shape[2]
                dst = WS[i % NSLOT][:, 0:a * m].rearrange("p (a m) -> p a m", m=m)
                S.dma('sp', lambda e, dst=dst, scr=scr: e.dma_start(out=dst, in_=scr),
                      reads=['scr_' + '_'.join(map(str, d))], writes=['WS%d' % (i % NSLOT)])

            def next(self, d):
                if self.record:
                    self.descs.append(d)
                    k = len(self.descs) - 1
                else:
                    k = self.used
                    assert self.descs[k] == d, (self.descs[k], d)
                    lim = min(len(self.descs), k + NSLOT - 1)
                    while self.issued < lim:
                        self._issue(self.issued)
                        self.issued += 1
                    self.used += 1
                shp = w_src(d)[0].shape
                a, m = shp[1], shp[2]
                return WS[k % NSLOT][:, 0:a * m].rearrange("p (a m) -> p a m", m=m), 'WS%d' % (k % NSLOT)

        W = WStream()

        def setup():
            S.dma('sp', lambda e: e.dma_start(out=PR[:], in_=params), writes=['PR'])
            S.dma('pool', lambda e: e.dma_start(out=BRE[:].rearrange('p (a b) -> p a b', b=1024), in_=bre_l.rearrange('p (a b) -> p a b', b=1024)), writes=['BRE'])
            S.dma('pool', lambda e: e.dma_start(out=BIM[:].rearrange('p (a b) -> p a b', b=1024), in_=bim_l.rearrange('p (a b) -> p a b', b=1024)), writes=['BIM'])
            S.op('dve', lambda e: e.memset(ONESF[:], 1.0), writes=['ONESF'])
            S.op('dve', lambda e: e.memset(ONESB[:], 1.0 / D_MODEL), writes=['ONESB'])
            S.op('dve', lambda e: e.memset(CONST[:, 0:1], RMS_EPS), writes=['CONST'])
            S.op('dve', lambda e: e.memset(CONST[:, 1:2], LN_EPS), writes=['CONST'])
            for wi, Wn in enumerate(POOL_WINDOWS):
                S.op('dve', lambda e, wi=wi, Wn=Wn: e.memset(INVC[:, wi * 16:(wi + 1) * 16], 1.0 / Wn), writes=['INVC'])
                for t in range(Wn - 1):
                    S.op('dve', lambda e, wi=wi, t=t: e.memset(INVC[:, wi * 16 + t:wi * 16 + t + 1], 1.0 / (t + 1)), writes=['INVC'])
            S.op('dve', lambda e: e.tensor_tensor(out=CONST[:, 8:24], in0=pc('pool_b', 0, 16), in1=pc('pool_scale', 0, 16), op=ALU.mult),
                 reads=['PR'], writes=['PBS'])
            S.op('dve', lambda e: e.tensor_tensor(out=CONST[:, 24:32], in0=pc('mix_norm', 16, 8), in1=pc('ssm_d', 0, 8), op=ALU.mult),
                 reads=['PR'], writes=['DG'])
            if depth_run >= 2:
                IDT = ACCC[:, 0:128]
                S.dma('sp', lambda e: e.dma_start(out=IDT, in_=ident), writes=['ACCcident'])
                for c in range(8):
                    for k in range(CONV_W):
                        S.op('dve', lambda e, c=c, k=k: e.tensor_scalar(out=ACC[:, k * 128:(k + 1) * 128], in0=IDT, scalar1=pc('conv_dw', k * 8 + c), scalar2=None, op0=ALU.mult),
                             reads=['ACCcident', 'PR'], writes=['ACCsdiag'])
                    S.dma('pool', lambda e, c=c: e.dma_start(out=dg_s[c], in_=ACC[:, 0:CONV_W * 128].rearrange("p (k m) -> p k m", m=128), max_dma_last_dim=2048),
                          reads=['ACCsdiag'], writes=['scr_dg_%d_0' % c, 'scr_dg_%d_1' % c])

        def setup_ssm():
            S.rekey('ACCs', ['ACCsang', 'ACCskf', 'ACCsyy', 'ACCstau', 'ACCsta0', 'ACCsta1', 'ACCstb0', 'ACCstb1'])
            TAU = ACC[:, 4096:4096 + SB]
            S.dma('sp', lambda e: e.dma_start(out=TAU, in_=tau1), writes=['ACCstau'])
            f = lambda i: SM[:, i * 32:(i + 1) * 32]
            LRE, DT, AA, TH, R, T0, T1, T2, QRE, QIM, NQIM, DEN = [f(i) for i in range(12)]
            RC128 = f(12)
            RS128 = f(13)
            kSM = ['SM']
            S.op('dve', lambda e: e.tensor_scalar(out=LRE, in0=pc('lamre', 0, 32), scalar1=-1e-4, scalar2=None, op0=ALU.min), reads=['PR'], writes=kSM)
            S.op('act', lambda e: e.activation(out=DT, in_=pc('logdt', 0, 32), func=AF.Exp), reads=['PR'], writes=kSM)
            S.op('dve', lambda e: e.tensor_tensor(out=AA, in0=LRE, in1=DT, op=ALU.mult), reads=kSM, writes=kSM)
            S.op('dve', lambda e: e.tensor_tensor(out=TH, in0=pc('lamim', 0, 32), in1=DT, op=ALU.mult), reads=kSM + ['PR'], writes=kSM)
            S.op('act', lambda e: e.activation(out=R, in_=AA, func=AF.Exp), reads=kSM, writes=kSM)
            for st in range(32):
                S.op('dve', lambda e, st=st: e.tensor_scalar(out=RT[:, st * SB:(st + 1) * SB], in0=ONESF[:, 0:SB], scalar1=R[:, st:st + 1],
                                                             scalar2=None, op0=ALU.mult), reads=['ONESF'] + kSM, writes=['RT'])
            S.op('dve', lambda e: e.memset(RT3[:, :, 0], 0.0), writes=['RT'])
            MAGIC = 12582912.0
            C1 = 6.28125
            C2 = 2.0 * math.pi - 6.28125
            PI_LO = 3.1415925
            NQ = 8
            HALF = NQ * SB
            ANG = ACC[:, 0:HALF]
            KF = ACC[:, HALF:2 * HALF]
            YY = ACC[:, 2 * HALF:3 * HALF]
            kA, kK, kY = ['ACCsang'], ['ACCskf'], ['ACCsyy']
            for half in range(32 // NQ):
                for st in range(NQ):
                    S.op('dve', lambda e, st=st, half=half: e.tensor_scalar(out=ANG[:, st * SB:(st + 1) * SB], in0=TAU, scalar1=TH[:, half * NQ + st:half * NQ + st + 1],
                                                                          scalar2=None, op0=ALU.mult), reads=['ACCstau'] + kSM, writes=kA)
                for which, dst in (('sin', ST), ('cos', CT)):
                    if which == 'cos':
                        S.op('dve', lambda e: e.tensor_scalar(out=ANG, in0=ANG, scalar1=math.pi / 2, scalar2=None, op0=ALU.add), reads=kA, writes=kA)
                    S.op('dve', lambda e: e.tensor_scalar(out=KF, in0=ANG, scalar1=1.0 / (2 * math.pi), scalar2=MAGIC, op0=ALU.mult, op1=ALU.add), reads=kA, writes=kK)
                    S.op('dve', lambda e: e.tensor_scalar(out=KF, in0=KF, scalar1=-MAGIC, scalar2=None, op0=ALU.add), reads=kK, writes=kK)
                    S.op('dve', lambda e: e.scalar_tensor_tensor(out=YY, in0=KF, scalar=-C1, in1=ANG, op0=ALU.mult, op1=ALU.add), reads=kK + kA, writes=kY)
                    S.op('dve', lambda e: e.scalar_tensor_tensor(out=YY, in0=KF, scalar=-C2, in1=YY, op0=ALU.mult, op1=ALU.add), reads=kK + kY, writes=kY)
                    S.op('dve', lambda e: e.tensor_scalar(out=KF, in0=YY, scalar1=math.pi, scalar2=-2 * math.pi, op0=ALU.is_gt, op1=ALU.mult), reads=kY, writes=kK)
                    S.op('dve', lambda e: e.tensor_tensor(out=YY, in0=YY, in1=KF, op=ALU.add), reads=kY + kK, writes=kY)
                    S.op('dve', lambda e: e.tensor_scalar(out=KF, in0=YY, scalar1=-math.pi, scalar2=2 * math.pi, op0=ALU.is_lt, op1=ALU.mult), reads=kY, writes=kK)
                    S.op('dve', lambda e: e.tensor_tensor(out=YY, in0=YY, in1=KF, op=ALU.add), reads=kY + kK, writes=kY)
                    S.op('dve', lambda e: e.tensor_scalar(out=YY, in0=YY, scalar1=PI_LO, scalar2=-PI_LO, op0=ALU.min, op1=ALU.max), reads=kY, writes=kY)
                    S.op('act', lambda e, dst=dst, half=half: e.activation(out=dst[:, half * HALF:(half + 1) * HALF], in_=YY, func=AF.Sin),
                         reads=kY, writes=['CT' if which == 'cos' else 'ST'])
            C0 = CT3[:, :, 0]
            S0 = ST3[:, :, 0]
            CL = CT3[:, :, SB - 1]
            SL = ST3[:, :, SB - 1]
            tt = lambda o, a, b, op, rd=(), wr=kSM: S.op('dve', lambda e: e.tensor_tensor(out=o, in0=a, in1=b, op=op), reads=list(rd) + kSM, writes=wr)
            tt(RC128, R, CL, ALU.mult, rd=['CT'])
            tt(RS128, R, SL, ALU.mult, rd=['ST'])
            tt(T0, R, C0, ALU.mult, rd=['CT'])
            S.op('dve', lambda e: e.tensor_scalar(out=T0, in0=T0, scalar1=-1.0, scalar2=None, op0=ALU.add), reads=kSM, writes=kSM)
            tt(T1, R, S0, ALU.mult, rd=['ST'])
            LIM = pc('lamim', 0, 32)
            tt(T2, LRE, LRE, ALU.mult)
            tt(DEN, LIM, LIM, ALU.mult, rd=['PR'])
            tt(DEN, DEN, T2, ALU.add)
            S.op('dve', lambda e: e.reciprocal(out=DEN, in_=DEN), reads=kSM, writes=kSM)
            tt(QRE, T0, LRE, ALU.mult)
            tt(T2, T1, LIM, ALU.mult, rd=['PR'])
            tt(QRE, QRE, T2, ALU.add)
            tt(QRE, QRE, DEN, ALU.mult)
            tt(QIM, T1, LRE, ALU.mult)
            tt(T2, T0, LIM, ALU.mult, rd=['PR'])
            tt(QIM, QIM, T2, ALU.subtract)
            tt(QIM, QIM, DEN, ALU.mult)
            S.op('dve', lambda e: e.tensor_scalar(out=NQIM, in0=QIM, scalar1=-1.0, scalar2=None, op0=ALU.mult), reads=kSM, writes=kSM)
            S.dma('sp', lambda e: e.dma_start(out=ACC[:, 0:2048], in_=cre_l), writes=kA + kK + kY)
            S.dma('sp', lambda e: e.dma_start(out=ACC[:, 2048:4096], in_=cim_l), writes=kA + kK + kY + ['ACCstau'])
            CREL = ACC[:, 0:2048].rearrange("p (s h) -> p s h", h=64)
            CIML = ACC[:, 2048:4096].rearrange("p (s h) -> p s h", h=64)
            kT = kA + kK + kY
            for st in range(32):
                ta = ACC[:, 4224 + (st % 2) * 128:4224 + (st % 2) * 128 + 64]
                tb = ACC[:, 4224 + (st % 2) * 128 + 64:4224 + (st % 2) * 128 + 128]
                ka = ['ACCsta%d' % (st % 2)]
                kb = ['ACCstb%d' % (st % 2)]
                S.op('dve', lambda e, st=st, ta=ta: e.tensor_scalar(out=ta, in0=CIML[:, st, :], scalar1=QIM[:, st:st + 1], scalar2=None, op0=ALU.mult),
                     reads=kT + kSM, writes=ka)
                S.op('dve', lambda e, st=st, ta=ta: e.scalar_tensor_tensor(out=CRE3[:, st, :], in0=CREL[:, st, :], scalar=QRE[:, st:st + 1], in1=ta,
                                                                           op0=ALU.mult, op1=ALU.subtract), reads=kT + kSM + ka, writes=['CRE'])
                S.op('dve', lambda e, st=st, tb=tb: e.tensor_scalar(out=tb, in0=CIML[:, st, :], scalar1=QRE[:, st:st + 1], scalar2=-1.0, op0=ALU.mult, op1=ALU.mult),
                     reads=kT + kSM, writes=kb)
                S.op('dve', lambda e, st=st, tb=tb: e.scalar_tensor_tensor(out=NCIM3[:, st, :], in0=CREL[:, st, :], scalar=NQIM[:, st:st + 1], in1=tb,
                                                                           op0=ALU.mult, op1=ALU.add), reads=kT + kSM + kb, writes=['NCIM'])

        def rmsnorm(q, gname, gidx, dest, rstd_sbuf=False):
            p = q.p
            for c in range(8):
                S.op('act', lambda e, c=c: e.activation(out=q.BF3[:, c, :], in_=q.X3[:, c, :], func=AF.Square),
                     reads=[p + 'X%d' % c], writes=[p + 'BF%d' % c])
            for c in range(8):
                S.op('pe', lambda e, c=c: e.matmul(q.PO[2], ONESB[:], q.BF3[:, c, :], start=(c == 0), stop=(c == 7)),
                     reads=[p + 'BF%d' % c, 'ONESB'], writes=[q.kPO[2]])
            S.op('act', lambda e: e.activation(out=q.TMP3[:, 0, :], in_=q.PO[2], func=AF.Sqrt, bias=CONST[:, 0:1], scale=1.0),
                 reads=[q.kPO[2], 'CONST'], writes=[p + 'TMP0'])
            if rstd_sbuf:
                rs_ap, rs_k = q.RSTD[:], p + 'RSTD'
            else:
                rs_ap, rs_k = q.PO[2], q.kPO[2]
            S.op('dve', lambda e: e.reciprocal(out=rs_ap, in_=q.TMP3[:, 0, :]), reads=[p + 'TMP0'], writes=[rs_k])
            for c in range(8):
                if dest == 'bf':
                    o, wk = q.HNB3[:, c, :], p + 'HNB%d' % c
                else:
                    o, wk = HNF3[:, c, HP:HP + NT], 'HNF%d' % c
                S.op('dve', lambda e, c=c, o=o: e.scalar_tensor_tensor(out=o, in0=q.X3[:, c, :], scalar=pc(gname, gidx * 8 + c), in1=rs_ap,
                                                                      op0=ALU.mult, op1=ALU.mult),
                     reads=[p + 'X%d' % c, rs_k, 'PR'], writes=[wk])
            yield 4.0

        def mm_group(pst, kps, wv, kw, h, rhs3, rkeys, nk):
            for kc in range(nk):
                S.op('pe', lambda e, kc=kc: e.matmul(pst, wv[:, kc, h * 128:(h + 1) * 128], rhs3[:, kc, :], start=(kc == 0), stop=(kc == nk - 1)),
                     reads=[kw, rkeys[kc]], writes=[kps])

        def ffn(q, L):
            p = q.p
            yield from rmsnorm(q, 'ffn_norm', L, 'bf')
            kh = [p + 'HNB%d' % c for c in range(8)]
            for g in range(11):
                wg, kwg = W.next(('wg', L, g))
                wu, kwu = W.next(('wu', L, g))
                for h in range(2):
                    j = 2 * g + h
                    pb = j % 2
                    r4 = j % 4
                    pgt, kpg = q.RING[r4][0], q.kRING[r4][0]
                    put, kpu = q.RING[r4][1], q.kRING[r4][1]
                    mm_group(pgt, kpg, wg, kwg, h, q.HNB3, kh, 8)
                    mm_group(put, kpu, wu, kwu, h, q.HNB3, kh, 8)
                    S.op('act', lambda e, pb=pb, pgt=pgt: e.activation(out=q.TMP3[:, pb, :], in_=pgt, func=AF.Silu),
                         reads=[kpg], writes=[p + 'TMP%d' % pb])
                    S.op('act', lambda e, put=put: e.activation(out=q.TMP3[:, 2, :], in_=put, func=AF.Copy),
                         reads=[kpu], writes=[p + 'TMP2'])
                    S.op('pool', lambda e, pb=pb, j=j: e.tensor_tensor(out=q.ACTB3[:, j, :], in0=q.TMP3[:, pb, :], in1=q.TMP3[:, 2, :], op=ALU.mult),
                         reads=[p + 'TMP%d' % pb, p + 'TMP2'], writes=[p + 'ACTB%d' % j])
                yield 3.8
            for mp in range(4):
                wds = [W.next(('wd', L, mp, 0)), W.next(('wd', L, mp, 1))]
                pbase = (mp % 2) * 2
                for khalf, (wd, kwd) in enumerate(wds):
                    for h in range(2):
                        for kc in range(11):
                            S.op('pe', lambda e, wd=wd, h=h, kc=kc, khalf=khalf, pbase=pbase: e.matmul(
                                q.PO[pbase + h], wd[:, kc, h * 128:(h + 1) * 128], q.ACTB3[:, khalf * 11 + kc, :],
                                start=(khalf == 0 and kc == 0), stop=(khalf == 1 and kc == 10)),
                                reads=[kwd, p + 'ACTB%d' % (khalf * 11 + kc)], writes=[q.kPO[pbase + h]])
                for h in range(2):
                    c = 2 * mp + h
                    S.op('act', lambda e, h=h, pbase=pbase: e.activation(out=q.TMP3[:, h, :], in_=q.PO[pbase + h], func=AF.Copy),
                         reads=[q.kPO[pbase + h]], writes=[p + 'TMP%d' % h])
                    S.op('pool', lambda e, c=c, h=h: e.tensor_tensor(out=q.X3[:, c, :], in0=q.X3[:, c, :], in1=q.TMP3[:, h, :], op=ALU.add),
                         reads=[p + 'X%d' % c, p + 'TMP%d' % h], writes=[p + 'X%d' % c])
                yield 5.0

        def pool_mixer(q, L, first):
            p = q.p
            j = L // 3
            ACC3 = ACCC[:, 0:8 * AW].rearrange("p (c n) -> p c n", n=AW)
            kAC = keys('ACCcp', 8)
            S.rekey('ACCc', kAC)
            PH3 = v3(q.POOLH[j], 16)
            kph = p + 'POOLH%d' % j
            if first:
                S.op('dve', lambda e: e.memset(q.POOLH[j][:], 0.0), writes=[kph])
            yield from rmsnorm(q, 'mix_norm', L, 'f32')
            S.op('act', lambda e: e.activation(out=ACC3[:, :, 0:16], in_=PH3, func=AF.Copy), reads=[kph], writes=kAC)
            for c in range(8):
                S.op('dve', lambda e, c=c: e.tensor_tensor_scan(out=ACC3[:, c, 16:16 + NT], data0=ONESF[:, 0:NT], data1=HNF3[:, c, HP:HP + NT],
                                                                initial=ACC3[:, c, 15:16], op0=ALU.mult, op1=ALU.add),
                     reads=[kAC[c], 'ONESF', 'HNF%d' % c], writes=[kAC[c]])
            S.op('act', lambda e: e.activation(out=PH3, in_=ACC3[:, :, NT:NT + 16], func=AF.Copy), reads=kAC, writes=[kph])
            yield 5.0
            for c in range(8):
                wi = c // 2
                Wn = POOL_WINDOWS[wi]
                tb = c % 2
                S.op('dve', lambda e, c=c, Wn=Wn, tb=tb: e.tensor_tensor(out=q.TMP3[:, tb, :], in0=ACC3[:, c, 16:16 + NT], in1=ACC3[:, c, 16 - Wn:16 - Wn + NT],
                                                                         op=ALU.subtract), reads=[kAC[c]], writes=[p + 'TMP%d' % tb])
                S.op('dve', lambda e, c=c, Wn=Wn, tb=tb: e.scalar_tensor_tensor(out=q.BF3[:, c, :], in0=q.TMP3[:, tb, :], scalar=1.0 / Wn, in1=HNF3[:, c, HP:HP + NT],
                                                                                op0=ALU.mult, op1=ALU.subtract),
                     reads=[p + 'TMP%d' % tb, 'HNF%d' % c], writes=[p + 'BF%d' % c])
                if first:
                    t16 = q.SMQ[:, 32 + tb * 16:32 + tb * 16 + 16]
                    S.op('dve', lambda e, c=c, wi=wi, tb=tb, t16=t16: e.tensor_tensor(out=t16, in0=q.TMP3[:, tb, 0:16], in1=INVC[:, wi * 16:(wi + 1) * 16], op=ALU.mult),
                         reads=[p + 'TMP%d' % tb, 'INVC'], writes=[p + 'T16_%d' % tb])
                    S.op('dve', lambda e, c=c, t16=t16: e.tensor_tensor(out=q.BF3[:, c, 0:16], in0=t16, in1=HNF3[:, c, HP:HP + 16], op=ALU.subtract),
                         reads=[p + 'T16_%d' % tb, 'HNF%d' % c], writes=[p + 'BF%d' % c])
            yield 5.0
            PW, kpw = W.next(('poolw', j))
            for g in range(4):
                for mo in range(2):
                    c = 2 * g + mo
                    pb = c % 2
                    for ki in range(2):
                        S.op('pe', lambda e, g=g, mo=mo, ki=ki, pb=pb: e.matmul(q.PO[pb], PW[:, 2 * g + ki, mo * 128:(mo + 1) * 128], q.BF3[:, 2 * g + ki, :],
                                                                                start=(ki == 0), stop=(ki == 1)),
                             reads=[kpw, p + 'BF%d' % (2 * g + ki)], writes=[q.kPO[pb]])
                    S.op('act', lambda e, c=c, pb=pb: e.activation(out=q.TMP3[:, 2, :], in_=q.PO[pb], func=AF.Identity,
                                                                   bias=CONST[:, 8 + j * 8 + c:8 + j * 8 + c + 1], scale=pc('pool_scale', j * 8 + c)),
                         reads=[q.kPO[pb], 'PBS', 'PR'], writes=[p + 'TMP2'])
                    S.op('dve', lambda e, c=c: e.tensor_tensor(out=q.X3[:, c, :], in0=q.X3[:, c, :], in1=q.TMP3[:, 2, :], op=ALU.add),
                         reads=[p + 'X%d' % c, p + 'TMP2'], writes=[p + 'X%d' % c])
            yield 3.0

        def conv_mixer(q, L, first):
            p = q.p
            ACC3 = ACCC[:, 0:8 * NT].rearrange("p (c n) -> p c n", n=NT)
            kAC = keys('ACCcc', 8)
            S.rekey('ACCc', kAC)
            CH3 = v3(q.CONVH, HP)
            kch = p + 'CONVH'
            if first:
                S.op('dve', lambda e: e.memset(q.CONVH[:], 0.0), writes=[kch])
            yield from rmsnorm(q, 'mix_norm', L, 'bf')
            kh = [p + 'HNB%d' % c for c in range(8)]
            UB3 = HNF[:].bitcast(BF16)[:, 0:8 * HW].rearrange("p (c n) -> p c n", n=HW)
            kU = keys('HNF', 8)
            S.op('act', lambda e: e.activation(out=UB3[:, :, 0:HP], in_=CH3, func=AF.Copy), reads=[kch], writes=kU)
            for pp in range(4):
                wa, kwa = W.next(('cin', pp))
                wg, kwg = W.next(('cin', 4 + pp))
                for h in range(2):
                    c = 2 * pp + h
                    pb = c % 2
                    mm_group(q.PG[pb], q.kPG[pb], wa, kwa, h, q.HNB3, kh, 8)
                    mm_group(q.PU[pb], q.kPU[pb], wg, kwg, h, q.HNB3, kh, 8)
                    S.op('act', lambda e, c=c, pb=pb: e.activation(out=q.TMP3[:, 2, :], in_=q.PU[pb], func=AF.Sigmoid, bias=pc('conv_b_g', c), scale=1.0),
                         reads=[q.kPU[pb], 'PR'], writes=[p + 'TMP2'])
                    S.op('dve', lambda e, c=c, pb=pb: e.scalar_tensor_tensor(out=UB3[:, c, HP:HP + NT], in0=q.PG[pb], scalar=pc('conv_b_a', c), in1=q.TMP3[:, 2, :],
                                                                             op0=ALU.add, op1=ALU.mult),
                         reads=[q.kPG[pb], p + 'TMP2', 'PR'], writes=kU)
                yield 3.6
            S.op('act', lambda e: e.activation(out=CH3, in_=UB3[:, :, NT:NT + HP], func=AF.Copy), reads=kU, writes=[kch])
            o0 = HP - (CONV_W - 1)
            PSA = [q.PG[0], q.PU[0], q.PG[1], q.PU[1], q.PO[0], q.PO[1], q.PO[2], q.PO[3]]
            kPSA = [q.kPG[0], q.kPU[0], q.kPG[1], q.kPU[1], q.kPO[0], q.kPO[1], q.kPO[2], q.kPO[3]]
            for c in range(8):
                dga, kda = W.next(('dg', c, 0))
                dgb, kdb = W.next(('dg', c, 1))
                for k in range(CONV_W):
                    lh, kl = (dga[:, k, :], kda) if k < 16 else (dgb[:, k - 16, :], kdb)
                    S.op('pe', lambda e, c=c, k=k, lh=lh: e.matmul(PSA[c], lh, UB3[:, c, o0 + k:o0 + k + NT], start=(k == 0), stop=(k == CONV_W - 1)),
                         reads=[kl] + kU, writes=[kPSA[c]])
                S.op('act', lambda e, c=c: e.activation(out=ACC3[:, c, :], in_=PSA[c], func=AF.Identity, bias=pc('conv_dw_b', c), scale=1.0),
                     reads=[kPSA[c], 'PR'], writes=[kAC[c]])
                yield 4.4
            for c in range(8):
                S.op('act', lambda e, c=c: e.activation(out=q.BF3[:, c, :], in_=ACC3[:, c, :], func=AF.Copy), reads=[kAC[c]], writes=[p + 'BF%d' % c])
            for c in range(8):
                S.op('pe', lambda e, c=c: e.matmul(q.PO[2], ONESB[:], q.BF3[:, c, :], start=(c == 0), stop=(c == 7)), reads=[p + 'BF%d' % c, 'ONESB'], writes=[q.kPO[2]])
            for c in range(8):
                S.op('act', lambda e, c=c: e.activation(out=q.BF3[:, c, :], in_=ACC3[:, c, :], func=AF.Square), reads=[kAC[c]], writes=[p + 'BF%d' % c])
            for c in range(8):
                S.op('pe', lambda e, c=c: e.matmul(q.PO[3], ONESB[:], q.BF3[:, c, :], start=(c == 0), stop=(c == 7)), reads=[p + 'BF%d' % c, 'ONESB'], writes=[q.kPO[3]])
            S.op('act', lambda e: e.activation(out=q.TMP3[:, 0, :], in_=q.PO[2], func=AF.Square), reads=[q.kPO[2]], writes=[p + 'TMP0'])
            S.op('dve', lambda e: e.tensor_tensor(out=q.TMP3[:, 1, :], in0=q.PO[3], in1=q.TMP3[:, 0, :], op=ALU.subtract), reads=[q.kPO[3], p + 'TMP0'], writes=[p + 'TMP1'])
            S.op('act', lambda e: e.activation(out=q.TMP3[:, 0, :], in_=q.TMP3[:, 1, :], func=AF.Sqrt, bias=CONST[:, 1:2], scale=1.0), reads=[p + 'TMP1', 'CONST'], writes=[p + 'TMP0'])
            S.op('dve', lambda e: e.reciprocal(out=q.TMP3[:, 1, :], in_=q.TMP3[:, 0, :]), reads=[p + 'TMP0'], writes=[p + 'TMP1'])
            yield 4.0
            for c in range(8):
                S.op('dve', lambda e, c=c: e.tensor_tensor(out=ACC3[:, c, :], in0=ACC3[:, c, :], in1=q.PO[2], op=ALU.subtract),
                     reads=[kAC[c], q.kPO[2]], writes=[kAC[c]])
            for c in range(8):
                S.op('dve', lambda e, c=c: e.tensor_tensor(out=ACC3[:, c, :], in0=ACC3[:, c, :], in1=q.TMP3[:, 1, :], op=ALU.mult),
                     reads=[kAC[c], p + 'TMP1'], writes=[kAC[c]])
            for c in range(8):
                S.op('act', lambda e, c=c: e.activation(out=q.BF3[:, c, :], in_=ACC3[:, c, :], func=AF.Silu, bias=pc('conv_ln_b', c), scale=pc('conv_ln_g', c)),
                     reads=[kAC[c], 'PR'], writes=[p + 'BF%d' % c])
            yield 5.0
            kb = [p + 'BF%d' % c for c in range(8)]
            for pp in range(4):
                wo, kwo = W.next(('cout', pp))
                for h in range(2):
                    c = 2 * pp + h
                    pb = c % 2
                    mm_group(q.PO[pb], q.kPO[pb], wo, kwo, h, q.BF3, kb, 8)
                    S.op('dve', lambda e, c=c, pb=pb: e.tensor_tensor(out=q.X3[:, c, :], in0=q.X3[:, c, :], in1=q.PO[pb], op=ALU.add),
                         reads=[p + 'X%d' % c, q.kPO[pb]], writes=[p + 'X%d' % c])
                yield 1.8

        def ssm_mixer(q, L, first):
            p = q.p
            if not ssm_ready[0] and not S.dry:
                ssm_ready[0] = True
                setup_ssm()
            kAC = keys('ACCss', 9)
            S.rekey('ACCs', kAC)
            A = lambda i: ACC[:, i * 512:(i + 1) * 512]
            if first:
                S.op('dve', lambda e: e.memset(q.WLR[:], 0.0), writes=[p + 'WLR'])
                S.op('dve', lambda e: e.memset(q.WLI[:], 0.0), writes=[p + 'WLI'])
            yield from rmsnorm(q, 'mix_norm', L, 'bf', rstd_sbuf=True)
            tt = lambda eng, o, ko, a, ka, bb, kbb, op: S.op(eng, lambda e: e.tensor_tensor(out=o, in0=a, in1=bb, op=op), reads=list(ka) + list(kbb), writes=list(ko))
            it = 0
            pending = None

            def emit_cmm(c, b, xre, xim, kxr, kxi):
                pby = c % 2
                t0 = b * SB
                for j in range(4):
                    stt = 4 * c + j
                    S.op('pe', lambda e, j=j, stt=stt: e.matmul(q.POY[pby][64 * (j // 2):64 * (j // 2) + 64, t0:t0 + SB], CRE3[:, stt, :], xre[:, j * SB:(j + 1) * SB],
                                                                start=(j % 2 == 0), stop=False), reads=['CRE', kxr], writes=[q.kPOY[pby]])
                    S.op('pe', lambda e, j=j, stt=stt: e.matmul(q.POY[pby][64 * (j // 2):64 * (j // 2) + 64, t0:t0 + SB], NCIM3[:, stt, :], xim[:, j * SB:(j + 1) * SB],
                                                                start=False, stop=(j % 2 == 1)), reads=['NCIM', kxi], writes=[q.kPOY[pby]])

            def epilogue(c):
                pby = c % 2
                S.op('dve', lambda e: e.scalar_tensor_tensor(out=q.TMP3[:, 0, :], in0=q.X3[:, c, :], scalar=CONST[:, 24 + c:25 + c], in1=q.RSTD[:],
                                                             op0=ALU.mult, op1=ALU.mult), reads=[p + 'X%d' % c, 'DG', p + 'RSTD'], writes=[p + 'TMP0'])
                S.op('dve', lambda e: e.tensor_tensor(out=q.TMP3[:, 0, :], in0=q.TMP3[:, 0, :], in1=q.POY[pby], op=ALU.add),
                     reads=[p + 'TMP0', q.kPOY[pby]], writes=[p + 'TMP0'])
                S.op('act', lambda e: e.activation(out=q.TMP3[:, 1, :], in_=q.TMP3[:, 0, :], func=AF.Square), reads=[p + 'TMP0'], writes=[p + 'TMP1'])
                S.op('dve', lambda e: e.tensor_scalar(out=q.TMP3[:, 1, :], in0=q.TMP3[:, 1, :], scalar1=0.044715, scalar2=1.0, op0=ALU.mult, op1=ALU.add),
                     reads=[p + 'TMP1'], writes=[p + 'TMP1'])
                S.op('dve', lambda e: e.tensor_tensor(out=q.TMP3[:, 1, :], in0=q.TMP3[:, 1, :], in1=q.TMP3[:, 0, :], op=ALU.mult), reads=[p + 'TMP1', p + 'TMP0'], writes=[p + 'TMP1'])
                S.op('act', lambda e: e.activation(out=q.TMP3[:, 2, :], in_=q.TMP3[:, 1, :], func=AF.Sigmoid, scale=2.0 * math.sqrt(2.0 / math.pi)),
                     reads=[p + 'TMP1'], writes=[p + 'TMP2'])
                S.op('dve', lambda e: e.tensor_tensor(out=q.ACTB3[:, c, :], in0=q.TMP3[:, 0, :], in1=q.TMP3[:, 2, :], op=ALU.mult),
                     reads=[p + 'TMP0', p + 'TMP2'], writes=[p + 'ACTB%d' % c])

            for c in range(8):
                Ct = CT[:, c * 512:(c + 1) * 512]
                St = ST[:, c * 512:(c + 1) * 512]
                Rt = RT[:, c * 512:(c + 1) * 512]
                for b in range(NT // SB):
                    t0 = b * SB
                    par = it % 2
                    it += 1
                    for j in range(4):
                        S.op('pe', lambda e, j=j, c=c, t0=t0: e.matmul(q.PGF[:, j * SB:(j + 1) * SB], BRE3[:, c, j, :], q.HNB3[:, c, t0:t0 + SB], start=True, stop=True),
                             reads=['BRE', p + 'HNB%d' % c], writes=q.kPGF)
                    for j in range(4):
                        S.op('pe', lambda e, j=j, c=c, t0=t0: e.matmul(q.PUF[:, j * SB:(j + 1) * SB], BIM3[:, c, j, :], q.HNB3[:, c, t0:t0 + SB], start=True, stop=True),
                             reads=['BIM', p + 'HNB%d' % c], writes=q.kPUF)
                    tt('dve', A(0), [kAC[0]], Ct, ['CT'], q.PGF[:], q.kPGF, ALU.mult)
                    tt('dve', A(1), [kAC[1]], St, ['ST'], q.PUF[:], q.kPUF, ALU.mult)
                    tt('dve', A(2), [kAC[2]], Ct, ['CT'], q.PUF[:], q.kPUF, ALU.mult)
                    tt('dve', A(0), [kAC[0]], A(0), [kAC[0]], A(1), [kAC[1]], ALU.add)
                    tt('dve', A(1), [kAC[1]], St, ['ST'], q.PGF[:], q.kPGF, ALU.mult)
                    tt('dve', A(2), [kAC[2]], A(2), [kAC[2]], A(1), [kAC[1]], ALU.subtract)
                    yield 3.3
                    wlr = q.WLR[:, 4 * c:4 * c + 4]
                    wli = q.WLI[:, 4 * c:4 * c + 4]
                    rc = SM[:, 12 * 32 + 4 * c:12 * 32 + 4 * c + 4]
                    rs = SM[:, 13 * 32 + 4 * c:13 * 32 + 4 * c + 4]
                    sm = lambda i: q.SMQ[:, 4 * i:4 * i + 4]
                    ksm = lambda i: [p + 'SMi%d' % i]
                    v0r = A(0).rearrange("p (a b) -> p a b", b=SB)[:, :, 0]
                    v0i = A(2).rearrange("p (a b) -> p a b", b=SB)[:, :, 0]
                    tt('dve', sm(0), ksm(0), rc, ['SM'], wlr, [p + 'WLR'], ALU.mult)
                    tt('dve', sm(1), ksm(1), rs, ['SM'], wli, [p + 'WLI'], ALU.mult)
                    tt('dve', sm(2), ksm(2), rs, ['SM'], wlr, [p + 'WLR'], ALU.mult)
                    tt('dve', sm(3), ksm(3), rc, ['SM'], wli, [p + 'WLI'], ALU.mult)
                    tt('dve', sm(4), ksm(4), sm(0), ksm(0), sm(1), ksm(1), ALU.subtract)
                    tt('dve', sm(5), ksm(5), sm(2), ksm(2), sm(3), ksm(3), ALU.add)
                    tt('dve', v0r, [kAC[0]], v0r, [kAC[0]], sm(4), ksm(4), ALU.add)
                    tt('dve', v0i, [kAC[2]], v0i, [kAC[2]], sm(5), ksm(5), ALU.add)
                    wr_, wi_ = A(3 + par), A(5 + par)
                    kwr, kwi = kAC[3 + par], kAC[5 + par]
                    S.op('dve', lambda e, Rt=Rt, wr_=wr_: e.tensor_tensor_scan(out=wr_, data0=Rt, data1=A(0), initial=0.0, op0=ALU.mult, op1=ALU.add),
                         reads=['RT', kAC[0]], writes=[kwr])
                    S.op('dve', lambda e, Rt=Rt, wi_=wi_: e.tensor_tensor_scan(out=wi_, data0=Rt, data1=A(2), initial=0.0, op0=ALU.mult, op1=ALU.add),
                         reads=['RT', kAC[2]], writes=[kwi])
                    wlast_r = wr_.rearrange("p (a b) -> p a b", b=SB)[:, :, SB - 1]
                    wlast_i = wi_.rearrange("p (a b) -> p a b", b=SB)[:, :, SB - 1]
                    S.op('act', lambda e, wlr=wlr, wlast_r=wlast_r: e.activation(out=wlr, in_=wlast_r, func=AF.Copy), reads=[kwr], writes=[p + 'WLR'])
                    S.op('act', lambda e, wli=wli, wlast_i=wlast_i: e.activation(out=wli, in_=wlast_i, func=AF.Copy), reads=[kwi], writes=[p + 'WLI'])
                    xre = q.BF[:, par * 1024:par * 1024 + 512]
                    xim = q.BF[:, par * 1024 + 512:par * 1024 + 1024]
                    kxr = [p + 'BF%d' % (par * 4), p + 'BF%d' % (par * 4 + 1)]
                    kxi = [p + 'BF%d' % (par * 4 + 2), p + 'BF%d' % (par * 4 + 3)]
                    tt('pool', A(7), [kAC[7]], Ct, ['CT'], wr_, [kwr], ALU.mult)
                    tt('pool', A(8), [kAC[8]], St, ['ST'], wi_, [kwi], ALU.mult)
                    tt('pool', xre, kxr, A(7), [kAC[7]], A(8), [kAC[8]], ALU.subtract)
                    tt('pool', A(7), [kAC[7]], St, ['ST'], wr_, [kwr], ALU.mult)
                    tt('pool', A(8), [kAC[8]], Ct, ['CT'], wi_, [kwi], ALU.mult)
                    tt('pool', xim, kxi, A(7), [kAC[7]], A(8), [kAC[8]], ALU.add)
                    if pending is not None:
                        emit_cmm(*pending[:6])
                        if pending[6]:
                            epilogue(pending[0])
                    pending = (c, b, xre, xim, kxr[0], kxi[0], b == NT // SB - 1)
                    yield 3.2
            emit_cmm(*pending[:6])
            epilogue(pending[0])
            yield 3.0
            kg = [p + 'ACTB%d' % c for c in range(8)]
            for pp in range(4):
                wa, kwa = W.next(('swa', pp))
                wb, kwb = W.next(('swb', pp))
                for h in range(2):
                    c = 2 * pp + h
                    pb = c % 2
                    mm_group(q.PG[pb], q.kPG[pb], wa, kwa, h, q.ACTB3, kg, 8)
                    mm_group(q.PU[pb], q.kPU[pb], wb, kwb, h, q.ACTB3, kg, 8)
                    S.op('act', lambda e, pb=pb: e.activation(out=q.TMP3[:, 2, :], in_=q.PU[pb], func=AF.Sigmoid), reads=[q.kPU[pb]], writes=[p + 'TMP2'])
                    S.op('dve', lambda e, pb=pb: e.tensor_tensor(out=q.TMP3[:, 2, :], in0=q.TMP3[:, 2, :], in1=q.PG[pb], op=ALU.mult),
                         reads=[p + 'TMP2', q.kPG[pb]], writes=[p + 'TMP2'])
                    S.op('dve', lambda e, c=c: e.tensor_tensor(out=q.X3[:, c, :], in0=q.X3[:, c, :], in1=q.TMP3[:, 2, :], op=ALU.add),
                         reads=[p + 'X%d' % c, p + 'TMP2'], writes=[p + 'X%d' % c])
                yield 3.6

        xsrc = xT.rearrange("(c p) t -> p c t", p=128)
        odst = outT.rearrange("(c p) t -> p c t", p=128)

        def stream_prog(q):
            p = q.p
            for ti in range(NTILES):
                first = (ti == 0)
                t0 = q.s * SEQ + ti * NT
                if ti > 0:
                    S.dma('sp', lambda e, t0=t0: e.dma_start(out=q.X3, in_=xsrc[:, :, t0:t0 + NT]), writes=[p + 'X%d' % c for c in range(8)])
                for L in range(depth_run):
                    kind = L % 3
                    lk = 'S' if kind == 2 else 'C'
                    yield ('lock', lk)
                    if kind == 0:
                        yield from pool_mixer(q, L, first)
                    elif kind == 1:
                        yield from conv_mixer(q, L, first)
                    else:
                        yield from ssm_mixer(q, L, first)
                    yield ('unlock', lk)
                    yield from ffn(q, L)
                yield ('lock', 'C')
                if do_final:
                    yield from rmsnorm(q, 'final_norm', 0, 'f32')
                    S.dma('sp', lambda e, t0=t0: e.dma_start(out=odst[:, :, t0:t0 + NT], in_=HNF3[:, :, HP:HP + NT]), reads=keys('HNF', 8), writes=['outT'])
                else:
                    S.dma('sp', lambda e, t0=t0: e.dma_start(out=odst[:, :, t0:t0 + NT], in_=q.X3), reads=[p + 'X%d' % c for c in range(8)], writes=['outT'])
                yield ('unlock', 'C')

        def drive():
            gens = [stream_prog(q) for q in streams]
            t = [0.0, 0.5]
            done = [False, False]
            blocked = [None, None]
            locks = {'C': None, 'S': None}
            while not all(done):
                cand = [s for s in range(2) if not done[s] and blocked[s] is None]
                assert cand, "driver deadlock"
                s = min(cand, key=lambda i: t[i])
                try:
                    r = next(gens[s])
                except StopIteration:
                    done[s] = True
                    continue
                if isinstance(r, tuple) and r[0] == 'lock':
                    if locks[r[1]] is None:
                        locks[r[1]] = s
                    else:
                        blocked[s] = r[1]
                elif isinstance(r, tuple) and r[0] == 'unlock':
                    assert locks[r[1]] == s
                    locks[r[1]] = None
                    o = 1 - s
                    if blocked[o] == r[1]:
                        blocked[o] = None
                        locks[r[1]] = o
                        t[o] = max(t[o], t[s])
                else:
                    t[s] += r
                    cast_ahead(2)

        S.dry = True
        W.record = True
        drive()
        S.dry = False
        W.record = False
        for q in streams:
            t0 = q.s * SEQ
            S.dma('pool', lambda e, q=q, t0=t0: e.dma_start(out=q.X3, in_=xsrc[:, :, t0:t0 + NT]), writes=[q.p + 'X%d' % c for c in range(8)])
        setup()
        S.rekey('HNF', keys('HNF', 8))
        drive()
        assert ssm_ready[0] or depth_run < 3
        S.emit()
    return nc


def _chunked(v):
    return np.ascontiguousarray(np.asarray(v, np.float32).reshape(8, 128).T)


def _prep_shared(inp):
    P = np.zeros((128, NPAR), np.float32)

    def put(name, idx, arr):
        o = PCOL[name] + idx
        P[:, o:o + arr.shape[1]] = arr

    for L in range(DEPTH):
        put("mix_norm", L * 8, _chunked(inp["mix_norm"][L]))
        put("ffn_norm", L * 8, _chunked(inp["ffn_norm"][L]))
    put("final_norm", 0, _chunked(inp["final_norm"]))
    for j in range(2):
        put("pool_b", j * 8, _chunked(inp["pool_b"][j]))
        put("pool_scale", j * 8, _chunked(inp["pool_scale"][j]))
    put("conv_b_a", 0, _chunked(inp["conv_b_in"][0][:D_MODEL]))
    put("conv_b_g", 0, _chunked(inp["conv_b_in"][0][D_MODEL:]))
    for k in range(CONV_W):
        put("conv_dw", k * 8, _chunked(inp["conv_dw"][0][k]))
    put("conv_dw_b", 0, _chunked(inp["conv_dw_b"][0]))
    put("conv_ln_g", 0, _chunked(inp["conv_ln_g"][0]))
    put("conv_ln_b", 0, _chunked(inp["conv_ln_b"][0]))
    put("ssm_d", 0, _chunked(inp["ssm_d"][0]))
    put("lamre", 0, np.asarray(inp["ssm_lam_re"][0], np.float32).reshape(32, 128).T)
    put("lamim", 0, np.asarray(inp["ssm_lam_im"][0], np.float32).reshape(32, 128).T)
    put("logdt", 0, np.repeat(np.asarray(inp["ssm_log_dt"][0], np.float32), 64).reshape(32, 128).T)

    b_re = np.asarray(inp["ssm_b_re"][0], np.float32)
    b_im = np.asarray(inp["ssm_b_im"][0], np.float32)
    c_re = np.asarray(inp["ssm_c_re"][0], np.float32)
    c_im = np.asarray(inp["ssm_c_im"][0], np.float32)
    bre_l = np.zeros((128, 8, 4, 128), np.float32)
    bim_l = np.zeros((128, 8, 4, 128), np.float32)
    cre_l = np.zeros((128, 32, 64), np.float32)
    cim_l = np.zeros((128, 32, 64), np.float32)
    for st in range(32):
        q, j = st // 4, st % 4
        for gg in range(2):
            g = 2 * st + gg
            bre_l[32 * j + gg * 16:32 * j + gg * 16 + 16, q, j, gg * 64:(gg + 1) * 64] = b_re[g].T
            bim_l[32 * j + gg * 16:32 * j + gg * 16 + 16, q, j, gg * 64:(gg + 1) * 64] = b_im[g].T
            co = 32 * (j % 2) + gg * 16
            cre_l[gg * 64:(gg + 1) * 64, st, co:co + 16] = c_re[g].T
            cim_l[gg * 64:(gg + 1) * 64, st, co:co + 16] = c_im[g].T
    f = lambda a: np.ascontiguousarray(np.asarray(a, np.float32))
    shared = {
        "params": P,
        "tau1": np.ascontiguousarray(np.broadcast_to(np.arange(1, SB + 1, dtype=np.float32), (128, SB))),
        "ident": np.eye(128, dtype=np.float32),
        "w_gate": f(inp["w_gate"]), "w_up": f(inp["w_up"]), "w_down": f(inp["w_down"]),
        "pool_w": f(inp["pool_w"]),
        "conv_w_in": f(inp["conv_w_in"][0]), "conv_w_out": f(inp["conv_w_out"][0]),
        "ssm_wa": f(inp["ssm_w_glu_a"][0]), "ssm_wb": f(inp["ssm_w_glu_b"][0]),
        "bre_l": bre_l.reshape(128, 4096), "bim_l": bim_l.reshape(128, 4096),
        "cre_l": cre_l.reshape(128, 2048), "cim_l": cim_l.reshape(128, 2048),
    }
    return shared


_NC_CACHE = {}


def run(inputs, depth_run=DEPTH, do_final=True, cores=N_CORES, trace=False):
    key = (depth_run, do_final)
    if key not in _NC_CACHE:
        _NC_CACHE[key] = build_nc(depth_run, do_final)
    nc = _NC_CACHE[key]
    shared = _prep_shared(inputs)
    x = np.asarray(inputs["x"], np.float32)
    in_maps = []
    for i in range(cores):
        xt = np.ascontiguousarray(x[2 * i:2 * i + 2].reshape(TOK, D_MODEL).T)
        m = dict(shared)
        m["xT"] = xt
        in_maps.append(m)
    res = run_bass_kernel_spmd(nc, in_maps, core_ids=list(range(cores)), **({"trace": True} if trace else {}))
    outs = []
    for i in range(cores):
        o = np.asarray(res.results[i]["outT"], np.float32)
        outs.append(o.T.reshape(2, SEQ, D_MODEL))
    return np.concatenate(outs, axis=0), res


def kernel(**inputs):
    out, _ = run(inputs)
    return out.astype(np.float32)
```

```python
import math
import numpy as np
from contextlib import ExitStack
import concourse.bass as bass
import concourse.mybir as mybir
from concourse.bass_utils import run_bass_kernel_spmd

F32 = mybir.dt.float32
BF16 = mybir.dt.bfloat16
ALU = mybir.AluOpType
AF = mybir.ActivationFunctionType

D_MODEL = 1024
SEQ = 2048
DEPTH = 4
D_FF = 2816
NCH = 8
NFF = 22
POOL_WINDOWS = (2, 4, 8, 16)
CONV_W = 31
RMS_EPS = 1e-6
LN_EPS = 1e-5
N_CORES = 8
TOK = 4096
NT = 256
NTILES = SEQ // NT
TPS = SEQ // NT
HP = 32
SB = 128
NSLOT = 5
SLOT_ELEMS = 2816

PCOL = {}
_off = 0
for _name, _n in [("mix_norm", 32), ("ffn_norm", 32), ("final_norm", 8), ("pool_b", 16), ("pool_scale", 16),
                  ("conv_b_a", 8), ("conv_b_g", 8), ("conv_dw", 248), ("conv_dw_b", 8), ("conv_ln_g", 8),
                  ("conv_ln_b", 8), ("ssm_d", 8), ("lamre", 32), ("lamim", 32), ("logdt", 32)]:
    PCOL[_name] = _off
    _off += _n
NPAR = _off


class Sched:
    EPOCH = 4000

    def __init__(self, nc, es, n_dma_sems=6):
        self.nc = nc
        self.es = es
        self.h = {'pe': nc.tensor, 'dve': nc.vector, 'act': nc.scalar, 'pool': nc.gpsimd, 'sp': nc.sync}
        self.prog = {e: [] for e in self.h}
        self.cnt = {e: 0 for e in self.h}
        self.sems = {}
        self.waited = {}
        self.last_w = {}
        self.readers = {}
        self.n_dma_sems = n_dma_sems
        self.dma_rr = {e: 0 for e in self.h}
        self.dma_cnt = {}
        self.dry = False

    def rekey(self, prefix, newkeys):
        if self.dry:
            return
        toks = set()
        for k in [k for k in self.last_w if k.startswith(prefix)]:
            t = self.last_w.pop(k)
            if t is not None:
                toks.add(t)
        for k in [k for k in self.readers if k.startswith(prefix)]:
            toks.update(self.readers.pop(k))
        for k in newkeys:
            self.last_w[k] = None
            self.readers[k] = list(toks)

    def _sem(self, key):
        if key not in self.sems:
            self.sems[key] = self.es.enter_context(self.nc.semaphore("s_" + "_".join(str(k) for k in key)))
        return self.sems[key]

    def _deps(self, eng, reads, writes, extra=()):
        deps = set(extra)
        for k in reads:
            t = self.last_w.get(k)
            if t is not None:
                deps.add(t)
            if k.startswith('PSB') and eng != 'pe':
                t = self.last_w.get(k[:-1] + ('1' if k[-1] == '0' else '0'))
                if t is not None:
                    deps.add(t)
        for k in writes:
            t = self.last_w.get(k)
            if t is not None:
                deps.add(t)
            for t in self.readers.get(k, ()):
                deps.add(t)
            if k.startswith('PSB'):
                sib = k[:-1] + ('1' if k[-1] == '0' else '0')
                if eng == 'pe':
                    for t in self.readers.get(sib, ()):
                        deps.add(t)
                t = self.last_w.get(sib)
                if t is not None:
                    deps.add(t)
        need = {}
        for (s, v, e) in deps:
            if e == 'pe' and eng == 'pe':
                continue
            if need.get(s, 0) < v:
                need[s] = v
        waits = []
        for s, v in need.items():
            if self.waited.get((eng, s), 0) < v:
                self.waited[(eng, s)] = v
                waits.append((s, v))
        return waits

    def _commit(self, tok, reads, writes):
        for k in writes:
            self.last_w[k] = tok
            self.readers[k] = []
        for k in reads:
            if k not in writes:
                self.readers.setdefault(k, []).append(tok)

    def op(self, eng, fn, reads=(), writes=()):
        if self.dry:
            return None
        waits = self._deps(eng, reads, writes)
        self.cnt[eng] += 1
        c = self.cnt[eng]
        ep = (c - 1) // self.EPOCH
        skey = (eng, ep)
        self._sem(skey)
        tok = (skey, c - ep * self.EPOCH, eng)
        self.prog[eng].append((waits, fn, skey, 1))
        self._commit(tok, reads, writes)
        return tok

    def dma(self, q, fn, reads=(), writes=()):
        if self.dry:
            return None
        j = self.dma_rr[q]
        self.dma_rr[q] = (j + 1) % self.n_dma_sems
        skey = ('dma', q, j)
        self._sem(skey)
        n = self.dma_cnt.get(skey, 0)
        extra = [(skey, 16 * n, 'dma')] if n > 0 else []
        waits = self._deps(q, reads, writes, extra)
        self.dma_cnt[skey] = n + 1
        tok = (skey, 16 * (n + 1), 'dma')
        self.prog[q].append((waits, fn, skey, 16))
        self._commit(tok, reads, writes)
        return tok

    def emit(self, final_engine='sp'):
        finals = []
        for e in self.h:
            c = self.cnt[e]
            if c > 0:
                ep = (c - 1) // self.EPOCH
                finals.append(((e, ep), c - ep * self.EPOCH))
        for skey, n in self.dma_cnt.items():
            finals.append((skey, 16 * n))
        sems = self.sems
        prog = self.prog
        with self.nc.Block() as block:
            def mk(e):
                def body(engh):
                    for waits, fn, skey, inc in prog[e]:
                        for s, v in waits:
                            engh.wait_ge(sems[s], v)
                        fn(engh).then_inc(sems[skey], inc)
                    if e == final_engine:
                        for s, v in finals:
                            engh.wait_ge(sems[s], v)
                return body
            block.sync(mk('sp'))
            block.tensor(mk('pe'))
            block.vector(mk('dve'))
            block.scalar(mk('act'))
            block.gpsimd(mk('pool'))


def build_nc(depth_run=DEPTH, do_final=True):
    nc = bass.Bass("TRN2", target_bir_lowering=False)

    def D(name, shape, dt, kind="ExternalInput"):
        return nc.dram_tensor(name, list(shape), dt, kind=kind).ap()

    xT = D("xT", [D_MODEL, TOK], F32)
    outT = D("outT", [D_MODEL, TOK], F32, "ExternalOutput")
    params = D("params", [128, NPAR], F32)
    tau1 = D("tau1", [128, SB], F32)
    w_gate = D("w_gate", [DEPTH, D_MODEL, D_FF], F32)
    w_up = D("w_up", [DEPTH, D_MODEL, D_FF], F32)
    w_down = D("w_down", [DEPTH, D_FF, D_MODEL], F32)
    pool_w = D("pool_w", [2, 4, 256, 256], F32)
    conv_w_in = D("conv_w_in", [D_MODEL, 2 * D_MODEL], F32)
    conv_w_out = D("conv_w_out", [D_MODEL, D_MODEL], F32)
    ssm_wa = D("ssm_wa", [D_MODEL, D_MODEL], F32)
    ssm_wb = D("ssm_wb", [D_MODEL, D_MODEL], F32)
    bre_l = D("bre_l", [128, 8 * 4 * 128], F32)
    bim_l = D("bim_l", [128, 8 * 4 * 128], F32)
    cre_l = D("cre_l", [128, 32 * 64], F32)
    cim_l = D("cim_l", [128, 32 * 64], F32)
    ident = D("ident", [128, 128], F32)

    wg_s = D("wg_s", [DEPTH, 11, 128, 8, 256], BF16, "Internal")
    wu_s = D("wu_s", [DEPTH, 11, 128, 8, 256], BF16, "Internal")
    wd_s = D("wd_s", [DEPTH, 4, 2, 128, 11, 256], BF16, "Internal")
    cin_s = D("cin_s", [8, 128, 8, 256], BF16, "Internal")
    cout_s = D("cout_s", [4, 128, 8, 256], BF16, "Internal")
    swa_s = D("swa_s", [4, 128, 8, 256], BF16, "Internal")
    swb_s = D("swb_s", [4, 128, 8, 256], BF16, "Internal")
    poolw_s = D("poolw_s", [2, 128, 8, 256], BF16, "Internal")
    dg_s = D("dg_s", [8, 128, CONV_W, 128], BF16, "Internal")

    with ExitStack() as es:
        S = Sched(nc, es)

        def sb(name, shape, dt):
            return es.enter_context(nc.sbuf_tensor(name, list(shape), dt))

        def v3(t, n):
            return t[:].rearrange("p (c n) -> p c n", n=n)

        def keys(name, n):
            return ['%s%d' % (name, i) for i in range(n)]

        HW = HP + NT
        AW = 16 + NT
        HNF = sb("HNF", [128, 8 * HW], F32)
        ACC = sb("ACC", [128, 4608], F32)
        ACCC = sb("ACCC", [128, 8 * AW], F32)
        WS = [sb("WS%d" % i, [128, SLOT_ELEMS], BF16) for i in range(NSLOT)]
        CT = sb("CT", [128, 32 * SB], F32)
        ST = sb("ST", [128, 32 * SB], F32)
        RT = sb("RT", [128, 32 * SB], F32)
        PR = sb("PR", [128, NPAR], F32)
        ONESF = sb("ONESF", [128, NT], F32)
        ONESB = sb("ONESB", [128, 128], BF16)
        BRE = sb("BRE", [128, 8 * 4 * 128], BF16)
        BIM = sb("BIM", [128, 8 * 4 * 128], BF16)
        CRE = sb("CRE", [128, 32 * 64], BF16)
        NCIM = sb("NCIM", [128, 32 * 64], BF16)
        SM = sb("SM", [128, 512], F32)
        INVC = sb("INVC", [128, 4 * 16], F32)
        CONST = sb("CONST", [128, 32], F32)
        PSB = [es.enter_context(nc.psum_tensor("PSB%d" % i, [128, 512], F32)) for i in range(8)]
        HNF3 = v3(HNF, HW)
        BRE3 = BRE[:].rearrange("p (q jj s) -> p q jj s", jj=4, s=128)
        BIM3 = BIM[:].rearrange("p (q jj s) -> p q jj s", jj=4, s=128)
        CRE3 = v3(CRE, 64)
        NCIM3 = v3(NCIM, 64)
        CT3 = v3(CT, SB)
        ST3 = v3(ST, SB)
        RT3 = v3(RT, SB)

        class Q:
            pass

        streams = []
        for s in range(2):
            q = Q()
            q.s = s
            q.p = 'q%d_' % s
            q.X = sb("X%d" % s, [128, 8 * NT], F32)
            q.HNB = sb("HNB%d" % s, [128, 8 * NT], BF16)
            q.BF = sb("BF%d" % s, [128, 8 * NT], BF16)
            q.TMP = sb("TMP%d" % s, [128, 3 * NT], F32)
            q.ACTB = sb("ACTB%d" % s, [128, NFF * NT], BF16)
            q.POOLH = [sb("POOLH%d_%d" % (s, j), [128, 8 * 16], F32) for j in range(2)]
            q.CONVH = sb("CONVH%d" % s, [128, 8 * HP], F32)
            q.WLR = sb("WLR%d" % s, [128, 32], F32)
            q.WLI = sb("WLI%d" % s, [128, 32], F32)
            q.SMQ = sb("SMQ%d" % s, [128, 64], F32)
            q.RSTD = sb("RSTD%d" % s, [128, NT], F32)
            q.X3 = v3(q.X, NT)
            q.HNB3 = v3(q.HNB, NT)
            q.BF3 = v3(q.BF, NT)
            q.TMP3 = v3(q.TMP, NT)
            q.ACTB3 = v3(q.ACTB, NT)
            B = PSB[4 * s:4 * s + 4]

            def hb(i, h, B=B):
                return B[i][:, h * NT:(h + 1) * NT]

            def khb(i, h, s=s):
                return 'PSB%dh%d' % (4 * s + i, h)
            q.PG = [hb(0, 0), hb(2, 0)]
            q.kPG = [khb(0, 0), khb(2, 0)]
            q.PU = [hb(1, 0), hb(3, 0)]
            q.kPU = [khb(1, 0), khb(3, 0)]
            q.PO = [hb(0, 1), hb(1, 1), hb(2, 1), hb(3, 1)]
            q.kPO = [khb(0, 1), khb(1, 1), khb(2, 1), khb(3, 1)]
            q.RING = [(hb(i, 0), hb(i, 1)) for i in range(4)]
            q.kRING = [(khb(i, 0), khb(i, 1)) for i in range(4)]
            q.PGF = B[0]
            q.kPGF = [khb(0, 0), khb(0, 1)]
            q.PUF = B[1]
            q.kPUF = [khb(1, 0), khb(1, 1)]
            q.POY = [hb(2, 0), hb(3, 0)]
            q.kPOY = [khb(2, 0), khb(3, 0)]
            streams.append(q)

        def pc(name, idx=0, n=1):
            o = PCOL[name] + idx
            return PR[:, o:o + n]

        def layer_wlist(L):
            lst = []
            kind = L % 3
            if kind == 0:
                lst.append(('poolw', L // 3))
            if kind == 1:
                for p in range(4):
                    lst.append(('cin', p))
                    lst.append(('cin', 4 + p))
                for c in range(8):
                    lst.append(('dg', c, 0))
                    lst.append(('dg', c, 1))
                for p in range(4):
                    lst.append(('cout', p))
            elif kind == 2:
                for p in range(4):
                    lst.append(('swa', p))
                    lst.append(('swb', p))
            for g in range(11):
                lst.append(('wg', L, g))
                lst.append(('wu', L, g))
            for mp in range(4):
                lst.append(('wd', L, mp, 0))
                lst.append(('wd', L, mp, 1))
            return lst

        def w_src(d):
            k = d[0]
            if k == 'wg':
                return wg_s[d[1], d[2]], w_gate[d[1]].rearrange("(kc k) (g m) -> g k kc m", k=128, m=256)[d[2]]
            if k == 'wu':
                return wu_s[d[1], d[2]], w_up[d[1]].rearrange("(kc k) (g m) -> g k kc m", k=128, m=256)[d[2]]
            if k == 'wd':
                src = w_down[d[1]].rearrange("(kh kc k) (mp m) -> mp kh k kc m", kh=2, kc=11, k=128, m=256)
                return wd_s[d[1], d[2], d[3]], src[d[2], d[3]]
            if k == 'cin':
                return cin_s[d[1]], conv_w_in.rearrange("(kc k) (g m) -> g k kc m", k=128, m=256)[d[1]]
            if k == 'cout':
                return cout_s[d[1]], conv_w_out.rearrange("(kc k) (g m) -> g k kc m", k=128, m=256)[d[1]]
            if k == 'swa':
                return swa_s[d[1]], ssm_wa.rearrange("(kc k) (g m) -> g k kc m", k=128, m=256)[d[1]]
            if k == 'dg':
                k0, k1 = (0, 16) if d[2] == 0 else (16, CONV_W)
                return dg_s[d[1]][:, k0:k1, :], None
            if k == 'poolw':
                return poolw_s[d[1]], pool_w[d[1]].rearrange("g (ki k) d -> k (g ki) d", k=128)
            if k == 'swb':
                return swb_s[d[1]], ssm_wb.rearrange("(kc k) (g m) -> g k kc m", k=128, m=256)[d[1]]
            raise KeyError(k)

        tile_wlist = []
        for L in range(depth_run):
            tile_wlist += layer_wlist(L)

        ssm_ready = [False]
        cast_done = set()
        cast_order = list(tile_wlist)
        cast_pos = [0]

        def ensure_cast(d):
            if S.dry or d in cast_done or d[0] == 'dg':
                return
            cast_done.add(d)
            scr, src = w_src(d)
            S.dma('pool', lambda e, scr=scr, src=src: e.dma_start(out=scr, in_=src), writes=['scr_' + '_'.join(map(str, d))])

        def cast_ahead(n):
            if S.dry:
                return
            while n > 0 and cast_pos[0] < len(cast_order):
                d = cast_order[cast_pos[0]]
                cast_pos[0] += 1
                if d not in cast_done:
                    ensure_cast(d)
                    n -= 1

        class WStream:
            def __init__(self):
                self.record = True
                self.descs = []
                self.issued = 0
                self.used = 0

            def _issue(self, i):
                d = self.descs[i]
                ensure_cast(d)
                scr, _ = w_src(d)
                a, m = scr.shape[1], scr.shape[2]
                dst = WS[i % NSLOT][:, 0:a * m].rearrange("p (a m) -> p a m", m=m)
                S.dma('sp', lambda e, dst=dst, scr=scr: e.dma_start(out=dst, in_=scr),
                      reads=['scr_' + '_'.join(map(str, d))], writes=['WS%d' % (i % NSLOT)])

            def next(self, d):
                if self.record:
                    self.descs.append(d)
                    k = len(self.descs) - 1
                else:
                    k = self.used
                    assert self.descs[k] == d, (self.descs[k], d)
                    lim = min(len(self.descs), k + NSLOT - 1)
                    while self.issued < lim:
                        self._issue(self.issued)
                        self.issued += 1
                    self.used += 1
                shp = w_src(d)[0].shape
                a, m = shp[1], shp[2]
                return WS[k % NSLOT][:, 0:a * m].rearrange("p (a m) -> p a m", m=m), 'WS%d' % (k % NSLOT)

        W = WStream()

        def setup():
            S.dma('sp', lambda e: e.dma_start(out=PR[:], in_=params), writes=['PR'])
            S.dma('pool', lambda e: e.dma_start(out=BRE[:].rearrange('p (a b) -> p a b', b=1024), in_=bre_l.rearrange('p (a b) -> p a b', b=1024)), writes=['BRE'])
            S.dma('pool', lambda e: e.dma_start(out=BIM[:].rearrange('p (a b) -> p a b', b=1024), in_=bim_l.rearrange('p (a b) -> p a b', b=1024)), writes=['BIM'])
            S.op('dve', lambda e: e.memset(ONESF[:], 1.0), writes=['ONESF'])
            S.op('dve', lambda e: e.memset(ONESB[:], 1.0 / D_MODEL), writes=['ONESB'])
            S.op('dve', lambda e: e.memset(CONST[:, 0:1], RMS_EPS), writes=['CONST'])
            S.op('dve', lambda e: e.memset(CONST[:, 1:2], LN_EPS), writes=['CONST'])
            for wi, Wn in enumerate(POOL_WINDOWS):
                S.op('dve', lambda e, wi=wi, Wn=Wn: e.memset(INVC[:, wi * 16:(wi + 1) * 16], 1.0 / Wn), writes=['INVC'])
                for t in range(Wn - 1):
                    S.op('dve', lambda e, wi=wi, t=t: e.memset(INVC[:, wi * 16 + t:wi * 16 + t + 1], 1.0 / (t + 1)), writes=['INVC'])
            S.op('dve', lambda e: e.tensor_tensor(out=CONST[:, 8:24], in0=pc('pool_b', 0, 16), in1=pc('pool_scale', 0, 16), op=ALU.mult),
                 reads=['PR'], writes=['PBS'])
            S.op('dve', lambda e: e.tensor_tensor(out=CONST[:, 24:32], in0=pc('mix_norm', 16, 8), in1=pc('ssm_d', 0, 8), op=ALU.mult),
                 reads=['PR'], writes=['DG'])
            if depth_run >= 2:
                IDT = ACCC[:, 0:128]
                S.dma('sp', lambda e: e.dma_start(out=IDT, in_=ident), writes=['ACCcident'])
                for c in range(8):
                    for k in range(CONV_W):
                        S.op('dve', lambda e, c=c, k=k: e.tensor_scalar(out=ACC[:, k * 128:(k + 1) * 128], in0=IDT, scalar1=pc('conv_dw', k * 8 + c), scalar2=None, op0=ALU.mult),
                             reads=['ACCcident', 'PR'], writes=['ACCsdiag'])
                    S.dma('pool', lambda e, c=c: e.dma_start(out=dg_s[c], in_=ACC[:, 0:CONV_W * 128].rearrange("p (k m) -> p k m", m=128), max_dma_last_dim=2048),
                          reads=['ACCsdiag'], writes=['scr_dg_%d_0' % c, 'scr_dg_%d_1' % c])

        def setup_ssm():
            S.rekey('ACCs', ['ACCsang', 'ACCskf', 'ACCsyy', 'ACCstau', 'ACCsta0', 'ACCsta1', 'ACCstb0', 'ACCstb1'])
            TAU = ACC[:, 4096:4096 + SB]
            S.dma('sp', lambda e: e.dma_start(out=TAU, in_=tau1), writes=['ACCstau'])
            f = lambda i: SM[:, i * 32:(i + 1) * 32]
            LRE, DT, AA, TH, R, T0, T1, T2, QRE, QIM, NQIM, DEN = [f(i) for i in range(12)]
            RC128 = f(12)
            RS128 = f(13)
            kSM = ['SM']
            S.op('dve', lambda e: e.tensor_scalar(out=LRE, in0=pc('lamre', 0, 32), scalar1=-1e-4, scalar2=None, op0=ALU.min), reads=['PR'], writes=kSM)
            S.op('act', lambda e: e.activation(out=DT, in_=pc('logdt', 0, 32), func=AF.Exp), reads=['PR'], writes=kSM)
            S.op('dve', lambda e: e.tensor_tensor(out=AA, in0=LRE, in1=DT, op=ALU.mult), reads=kSM, writes=kSM)
            S.op('dve', lambda e: e.tensor_tensor(out=TH, in0=pc('lamim', 0, 32), in1=DT, op=ALU.mult), reads=kSM + ['PR'], writes=kSM)
            S.op('act', lambda e: e.activation(out=R, in_=AA, func=AF.Exp), reads=kSM, writes=kSM)
            for st in range(32):
                S.op('dve', lambda e, st=st: e.tensor_scalar(out=RT[:, st * SB:(st + 1) * SB], in0=ONESF[:, 0:SB], scalar1=R[:, st:st + 1],
                                                             scalar2=None, op0=ALU.mult), reads=['ONESF'] + kSM, writes=['RT'])
            S.op('dve', lambda e: e.memset(RT3[:, :, 0], 0.0), writes=['RT'])
            MAGIC = 12582912.0
            C1 = 6.28125
            C2 = 2.0 * math.pi - 6.28125
            PI_LO = 3.1415925
            NQ = 8
            HALF = NQ * SB
            ANG = ACC[:, 0:HALF]
            KF = ACC[:, HALF:2 * HALF]
            YY = ACC[:, 2 * HALF:3 * HALF]
            kA, kK, kY = ['ACCsang'], ['ACCskf'], ['ACCsyy']
            for half in range(32 // NQ):
                for st in range(NQ):
                    S.op('dve', lambda e, st=st, half=half: e.tensor_scalar(out=ANG[:, st * SB:(st + 1) * SB], in0=TAU, scalar1=TH[:, half * NQ + st:half * NQ + st + 1],
                                                                          scalar2=None, op0=ALU.mult), reads=['ACCstau'] + kSM, writes=kA)
                for which, dst in (('sin', ST), ('cos', CT)):
                    if which == 'cos':
                        S.op('dve', lambda e: e.tensor_scalar(out=ANG, in0=ANG, scalar1=math.pi / 2, scalar2=None, op0=ALU.add), reads=kA, writes=kA)
                    S.op('dve', lambda e: e.tensor_scalar(out=KF, in0=ANG, scalar1=1.0 / (2 * math.pi), scalar2=MAGIC, op0=ALU.mult, op1=ALU.add), reads=kA, writes=kK)
                    S.op('dve', lambda e: e.tensor_scalar(out=KF, in0=KF, scalar1=-MAGIC, scalar2=None, op0=ALU.add), reads=kK, writes=kK)
                    S.op('dve', lambda e: e.scalar_tensor_tensor(out=YY, in0=KF, scalar=-C1, in1=ANG, op0=ALU.mult, op1=ALU.add), reads=kK + kA, writes=kY)
                    S.op('dve', lambda e: e.scalar_tensor_tensor(out=YY, in0=KF, scalar=-C2, in1=YY, op0=ALU.mult, op1=ALU.add), reads=kK + kY, writes=kY)
                    S.op('dve', lambda e: e.tensor_scalar(out=KF, in0=YY, scalar1=math.pi, scalar2=-2 * math.pi, op0=ALU.is_gt, op1=ALU.mult), reads=kY, writes=kK)
                    S.op('dve', lambda e: e.tensor_tensor(out=YY, in0=YY, in1=KF, op=ALU.add), reads=kY + kK, writes=kY)
                    S.op('dve', lambda e: e.tensor_scalar(out=KF, in0=YY, scalar1=-math.pi, scalar2=2 * math.pi, op0=ALU.is_lt, op1=ALU.mult), reads=kY, writes=kK)
                    S.op('dve', lambda e: e.tensor_tensor(out=YY, in0=YY, in1=KF, op=ALU.add), reads=kY + kK, writes=kY)
                    S.op('dve', lambda e: e.tensor_scalar(out=YY, in0=YY, scalar1=PI_LO, scalar2=-PI_LO, op0=ALU.min, op1=ALU.max), reads=kY, writes=kY)
                    S.op('act', lambda e, dst=dst, half=half: e.activation(out=dst[:, half * HALF:(half + 1) * HALF], in_=YY, func=AF.Sin),
                         reads=kY, writes=['CT' if which == 'cos' else 'ST'])
            C0 = CT3[:, :, 0]
            S0 = ST3[:, :, 0]
            CL = CT3[:, :, SB - 1]
            SL = ST3[:, :, SB - 1]
            tt = lambda o, a, b, op, rd=(), wr=kSM: S.op('dve', lambda e: e.tensor_tensor(out=o, in0=a, in1=b, op=op), reads=list(rd) + kSM, writes=wr)
            tt(RC128, R, CL, ALU.mult, rd=['CT'])
            tt(RS128, R, SL, ALU.mult, rd=['ST'])
            tt(T0, R, C0, ALU.mult, rd=['CT'])
            S.op('dve', lambda e: e.tensor_scalar(out=T0, in0=T0, scalar1=-1.0, scalar2=None, op0=ALU.add), reads=kSM, writes=kSM)
            tt(T1, R, S0, ALU.mult, rd=['ST'])
            LIM = pc('lamim', 0, 32)
            tt(T2, LRE, LRE, ALU.mult)
            tt(DEN, LIM, LIM, ALU.mult, rd=['PR'])
            tt(DEN, DEN, T2, ALU.add)
            S.op('dve', lambda e: e.reciprocal(out=DEN, in_=DEN), reads=kSM, writes=kSM)
            tt(QRE, T0, LRE, ALU.mult)
            tt(T2, T1, LIM, ALU.mult, rd=['PR'])
            tt(QRE, QRE, T2, ALU.add)
            tt(QRE, QRE, DEN, ALU.mult)
            tt(QIM, T1, LRE, ALU.mult)
            tt(T2, T0, LIM, ALU.mult, rd=['PR'])
            tt(QIM, QIM, T2, ALU.subtract)
            tt(QIM, QIM, DEN, ALU.mult)
            S.op('dve', lambda e: e.tensor_scalar(out=NQIM, in0=QIM, scalar1=-1.0, scalar2=None, op0=ALU.mult), reads=kSM, writes=kSM)
            S.dma('sp', lambda e: e.dma_start(out=ACC[:, 0:2048], in_=cre_l), writes=kA + kK + kY)
            S.dma('sp', lambda e: e.dma_start(out=ACC[:, 2048:4096], in_=cim_l), writes=kA + kK + kY + ['ACCstau'])
            CREL = ACC[:, 0:2048].rearrange("p (s h) -> p s h", h=64)
            CIML = ACC[:, 2048:4096].rearrange("p (s h) -> p s h", h=64)
            kT = kA + kK + kY
            for st in range(32):
                ta = ACC[:, 4224 + (st % 2) * 128:4224 + (st % 2) * 128 + 64]
                tb = ACC[:, 4224 + (st % 2) * 128 + 64:4224 + (st % 2) * 128 + 128]
                ka = ['ACCsta%d' % (st % 2)]
                kb = ['ACCstb%d' % (st % 2)]
                S.op('dve', lambda e, st=st, ta=ta: e.tensor_scalar(out=ta, in0=CIML[:, st, :], scalar1=QIM[:, st:st + 1], scalar2=None, op0=ALU.mult),
                     reads=kT + kSM, writes=ka)
                S.op('dve', lambda e, st=st, ta=ta: e.scalar_tensor_tensor(out=CRE3[:, st, :], in0=CREL[:, st, :], scalar=QRE[:, st:st + 1], in1=ta,
                                                                           op0=ALU.mult, op1=ALU.subtract), reads=kT + kSM + ka, writes=['CRE'])
                S.op('dve', lambda e, st=st, tb=tb: e.tensor_scalar(out=tb, in0=CIML[:, st, :], scalar1=QRE[:, st:st + 1], scalar2=-1.0, op0=ALU.mult, op1=ALU.mult),
                     reads=kT + kSM, writes=kb)
                S.op('dve', lambda e, st=st, tb=tb: e.scalar_tensor_tensor(out=NCIM3[:, st, :], in0=CREL[:, st, :], scalar=NQIM[:, st:st + 1], in1=tb,
                                                                           op0=ALU.mult, op1=ALU.add), reads=kT + kSM + kb, writes=['NCIM'])

        def rmsnorm(q, gname, gidx, dest, rstd_sbuf=False):
            p = q.p
            for c in range(8):
                S.op('act', lambda e, c=c: e.activation(out=q.BF3[:, c, :], in_=q.X3[:, c, :], func=AF.Square),
                     reads=[p + 'X%d' % c], writes=[p + 'BF%d' % c])
            for c in range(8):
                S.op('pe', lambda e, c=c: e.matmul(q.PO[2], ONESB[:], q.BF3[:, c, :], start=(c == 0), stop=(c == 7)),
                     reads=[p + 'BF%d' % c, 'ONESB'], writes=[q.kPO[2]])
            S.op('act', lambda e: e.activation(out=q.TMP3[:, 0, :], in_=q.PO[2], func=AF.Sqrt, bias=CONST[:, 0:1], scale=1.0),
                 reads=[q.kPO[2], 'CONST'], writes=[p + 'TMP0'])
            if rstd_sbuf:
                rs_ap, rs_k = q.RSTD[:], p + 'RSTD'
            else:
                rs_ap, rs_k = q.PO[2], q.kPO[2]
            S.op('dve', lambda e: e.reciprocal(out=rs_ap, in_=q.TMP3[:, 0, :]), reads=[p + 'TMP0'], writes=[rs_k])
            for c in range(8):
                if dest == 'bf':
                    o, wk = q.HNB3[:, c, :], p + 'HNB%d' % c
                else:
                    o, wk = HNF3[:, c, HP:HP + NT], 'HNF%d' % c
                S.op('dve', lambda e, c=c, o=o: e.scalar_tensor_tensor(out=o, in0=q.X3[:, c, :], scalar=pc(gname, gidx * 8 + c), in1=rs_ap,
                                                                      op0=ALU.mult, op1=ALU.mult),
                     reads=[p + 'X%d' % c, rs_k, 'PR'], writes=[wk])
            yield 4.0

        def mm_group(pst, kps, wv, kw, h, rhs3, rkeys, nk):
            for kc in range(nk):
                S.op('pe', lambda e, kc=kc: e.matmul(pst, wv[:, kc, h * 128:(h + 1) * 128], rhs3[:, kc, :], start=(kc == 0), stop=(kc == nk - 1)),
                     reads=[kw, rkeys[kc]], writes=[kps])

        def ffn(q, L):
            p = q.p
            yield from rmsnorm(q, 'ffn_norm', L, 'bf')
            kh = [p + 'HNB%d' % c for c in range(8)]
            for g in range(11):
                wg, kwg = W.next(('wg', L, g))
                wu, kwu = W.next(('wu', L, g))
                for h in range(2):
                    j = 2 * g + h
                    pb = j % 2
                    r4 = j % 4
                    pgt, kpg = q.RING[r4][0], q.kRING[r4][0]
                    put, kpu = q.RING[r4][1], q.kRING[r4][1]
                    mm_group(pgt, kpg, wg, kwg, h, q.HNB3, kh, 8)
                    mm_group(put, kpu, wu, kwu, h, q.HNB3, kh, 8)
                    S.op('act', lambda e, pb=pb, pgt=pgt: e.activation(out=q.TMP3[:, pb, :], in_=pgt, func=AF.Silu),
                         reads=[kpg], writes=[p + 'TMP%d' % pb])
                    if j % 2 == 0:
                        S.op('act', lambda e, put=put: e.activation(out=q.TMP3[:, 2, :], in_=put, func=AF.Copy),
                             reads=[kpu], writes=[p + 'TMP2'])
                        S.op('pool', lambda e, pb=pb, j=j: e.tensor_tensor(out=q.ACTB3[:, j, :], in0=q.TMP3[:, pb, :], in1=q.TMP3[:, 2, :], op=ALU.mult),
                             reads=[p + 'TMP%d' % pb, p + 'TMP2'], writes=[p + 'ACTB%d' % j])
                    else:
                        S.op('dve', lambda e, pb=pb, j=j, put=put: e.tensor_tensor(out=q.ACTB3[:, j, :], in0=q.TMP3[:, pb, :], in1=put, op=ALU.mult),
                             reads=[p + 'TMP%d' % pb, kpu], writes=[p + 'ACTB%d' % j])
                yield 3.8
            for mp in range(4):
                wds = [W.next(('wd', L, mp, 0)), W.next(('wd', L, mp, 1))]
                pbase = (mp % 2) * 2
                for khalf, (wd, kwd) in enumerate(wds):
                    for h in range(2):
                        for kc in range(11):
                            S.op('pe', lambda e, wd=wd, h=h, kc=kc, khalf=khalf, pbase=pbase: e.matmul(
                                q.PO[pbase + h], wd[:, kc, h * 128:(h + 1) * 128], q.ACTB3[:, khalf * 11 + kc, :],
                                start=(khalf == 0 and kc == 0), stop=(khalf == 1 and kc == 10)),
                                reads=[kwd, p + 'ACTB%d' % (khalf * 11 + kc)], writes=[q.kPO[pbase + h]])
                for h in range(2):
                    c = 2 * mp + h
                    S.op('act', lambda e, h=h, pbase=pbase: e.activation(out=q.TMP3[:, h, :], in_=q.PO[pbase + h], func=AF.Copy),
                         reads=[q.kPO[pbase + h]], writes=[p + 'TMP%d' % h])
                    S.op('pool', lambda e, c=c, h=h: e.tensor_tensor(out=q.X3[:, c, :], in0=q.X3[:, c, :], in1=q.TMP3[:, h, :], op=ALU.add),
                         reads=[p + 'X%d' % c, p + 'TMP%d' % h], writes=[p + 'X%d' % c])
                yield 5.0

        def pool_mixer(q, L, first):
            p = q.p
            j = L // 3
            ACC3 = ACCC[:, 0:8 * AW].rearrange("p (c n) -> p c n", n=AW)
            kAC = keys('ACCcp', 8)
            S.rekey('ACCc', kAC)
            PH3 = v3(q.POOLH[j], 16)
            kph = p + 'POOLH%d' % j
            if first:
                S.op('dve', lambda e: e.memset(q.POOLH[j][:], 0.0), writes=[kph])
            yield from rmsnorm(q, 'mix_norm', L, 'f32')
            S.op('act', lambda e: e.activation(out=ACC3[:, :, 0:16], in_=PH3, func=AF.Copy), reads=[kph], writes=kAC)
            for c in range(8):
                S.op('dve', lambda e, c=c: e.tensor_tensor_scan(out=ACC3[:, c, 16:16 + NT], data0=ONESF[:, 0:NT], data1=HNF3[:, c, HP:HP + NT],
                                                                initial=ACC3[:, c, 15:16], op0=ALU.mult, op1=ALU.add),
                     reads=[kAC[c], 'ONESF', 'HNF%d' % c], writes=[kAC[c]])
            S.op('act', lambda e: e.activation(out=PH3, in_=ACC3[:, :, NT:NT + 16], func=AF.Copy), reads=kAC, writes=[kph])
            yield 5.0
            for c in range(8):
                wi = c // 2
                Wn = POOL_WINDOWS[wi]
                tb = c % 2
                S.op('dve', lambda e, c=c, Wn=Wn, tb=tb: e.tensor_tensor(out=q.TMP3[:, tb, :], in0=ACC3[:, c, 16:16 + NT], in1=ACC3[:, c, 16 - Wn:16 - Wn + NT],
                                                                         op=ALU.subtract), reads=[kAC[c]], writes=[p + 'TMP%d' % tb])
                S.op('dve', lambda e, c=c, Wn=Wn, tb=tb: e.scalar_tensor_tensor(out=q.BF3[:, c, :], in0=q.TMP3[:, tb, :], scalar=1.0 / Wn, in1=HNF3[:, c, HP:HP + NT],
                                                                                op0=ALU.mult, op1=ALU.subtract),
                     reads=[p + 'TMP%d' % tb, 'HNF%d' % c], writes=[p + 'BF%d' % c])
                if first:
                    t16 = q.SMQ[:, 32 + tb * 16:32 + tb * 16 + 16]
                    S.op('dve', lambda e, c=c, wi=wi, tb=tb, t16=t16: e.tensor_tensor(out=t16, in0=q.TMP3[:, tb, 0:16], in1=INVC[:, wi * 16:(wi + 1) * 16], op=ALU.mult),
                         reads=[p + 'TMP%d' % tb, 'INVC'], writes=[p + 'T16_%d' % tb])
                    S.op('dve', lambda e, c=c, t16=t16: e.tensor_tensor(out=q.BF3[:, c, 0:16], in0=t16, in1=HNF3[:, c, HP:HP + 16], op=ALU.subtract),
                         reads=[p + 'T16_%d' % tb, 'HNF%d' % c], writes=[p + 'BF%d' % c])
            yield 5.0
            PW, kpw = W.next(('poolw', j))
            for g in range(4):
                for mo in range(2):
                    c = 2 * g + mo
                    pb = c % 2
                    for ki in range(2):
                        S.op('pe', lambda e, g=g, mo=mo, ki=ki, pb=pb: e.matmul(q.PO[pb], PW[:, 2 * g + ki, mo * 128:(mo + 1) * 128], q.BF3[:, 2 * g + ki, :],
                                                                                start=(ki == 0), stop=(ki == 1)),
                             reads=[kpw, p + 'BF%d' % (2 * g + ki)], writes=[q.kPO[pb]])
                    S.op('act', lambda e, c=c, pb=pb: e.activation(out=q.TMP3[:, 2, :], in_=q.PO[pb], func=AF.Identity,
                                                                   bias=CONST[:, 8 + j * 8 + c:8 + j * 8 + c + 1], scale=pc('pool_scale', j * 8 + c)),
                         reads=[q.kPO[pb], 'PBS', 'PR'], writes=[p + 'TMP2'])
                    S.op('dve', lambda e, c=c: e.tensor_tensor(out=q.X3[:, c, :], in0=q.X3[:, c, :], in1=q.TMP3[:, 2, :], op=ALU.add),
                         reads=[p + 'X%d' % c, p + 'TMP2'], writes=[p + 'X%d' % c])
            yield 3.0

        def conv_mixer(q, L, first):
            p = q.p
            ACC3 = ACCC[:, 0:8 * NT].rearrange("p (c n) -> p c n", n=NT)
            kAC = keys('ACCcc', 8)
            S.rekey('ACCc', kAC)
            CH3 = v3(q.CONVH, HP)
            kch = p + 'CONVH'
            if first:
                S.op('dve', lambda e: e.memset(q.CONVH[:], 0.0), writes=[kch])
            yield from rmsnorm(q, 'mix_norm', L, 'bf')
            kh = [p + 'HNB%d' % c for c in range(8)]
            UB3 = HNF[:].bitcast(BF16)[:, 0:8 * HW].rearrange("p (c n) -> p c n", n=HW)
            kU = keys('HNF', 8)
            S.op('act', lambda e: e.activation(out=UB3[:, :, 0:HP], in_=CH3, func=AF.Copy), reads=[kch], writes=kU)
            for pp in range(4):
                wa, kwa = W.next(('cin', pp))
                wg, kwg = W.next(('cin', 4 + pp))
                for h in range(2):
                    c = 2 * pp + h
                    pb = c % 2
                    mm_group(q.PG[pb], q.kPG[pb], wa, kwa, h, q.HNB3, kh, 8)
                    mm_group(q.PU[pb], q.kPU[pb], wg, kwg, h, q.HNB3, kh, 8)
                    S.op('act', lambda e, c=c, pb=pb: e.activation(out=q.TMP3[:, 2, :], in_=q.PU[pb], func=AF.Sigmoid, bias=pc('conv_b_g', c), scale=1.0),
                         reads=[q.kPU[pb], 'PR'], writes=[p + 'TMP2'])
                    S.op('dve', lambda e, c=c, pb=pb: e.scalar_tensor_tensor(out=UB3[:, c, HP:HP + NT], in0=q.PG[pb], scalar=pc('conv_b_a', c), in1=q.TMP3[:, 2, :],
                                                                             op0=ALU.add, op1=ALU.mult),
                         reads=[q.kPG[pb], p + 'TMP2', 'PR'], writes=kU)
                yield 3.6
            S.op('act', lambda e: e.activation(out=CH3, in_=UB3[:, :, NT:NT + HP], func=AF.Copy), reads=kU, writes=[kch])
            o0 = HP - (CONV_W - 1)
            PSA = [q.PG[0], q.PU[0], q.PG[1], q.PU[1], q.PO[0], q.PO[1], q.PO[2], q.PO[3]]
            kPSA = [q.kPG[0], q.kPU[0], q.kPG[1], q.kPU[1], q.kPO[0], q.kPO[1], q.kPO[2], q.kPO[3]]
            for c in range(8):
                dga, kda = W.next(('dg', c, 0))
                dgb, kdb = W.next(('dg', c, 1))
                for k in range(CONV_W):
                    lh, kl = (dga[:, k, :], kda) if k < 16 else (dgb[:, k - 16, :], kdb)
                    S.op('pe', lambda e, c=c, k=k, lh=lh: e.matmul(PSA[c], lh, UB3[:, c, o0 + k:o0 + k + NT], start=(k == 0), stop=(k == CONV_W - 1)),
                         reads=[kl] + kU, writes=[kPSA[c]])
                S.op('act', lambda e, c=c: e.activation(out=ACC3[:, c, :], in_=PSA[c], func=AF.Identity, bias=pc('conv_dw_b', c), scale=1.0),
                     reads=[kPSA[c], 'PR'], writes=[kAC[c]])
                yield 4.4
            for c in range(8):
                S.op('act', lambda e, c=c: e.activation(out=q.BF3[:, c, :], in_=ACC3[:, c, :], func=AF.Copy), reads=[kAC[c]], writes=[p + 'BF%d' % c])
            for c in range(8):
                S.op('pe', lambda e, c=c: e.matmul(q.PO[2], ONESB[:], q.BF3[:, c, :], start=(c == 0), stop=(c == 7)), reads=[p + 'BF%d' % c, 'ONESB'], writes=[q.kPO[2]])
            for c in range(8):
                S.op('act', lambda e, c=c: e.activation(out=q.BF3[:, c, :], in_=ACC3[:, c, :], func=AF.Square), reads=[kAC[c]], writes=[p + 'BF%d' % c])
            for c in range(8):
                S.op('pe', lambda e, c=c: e.matmul(q.PO[3], ONESB[:], q.BF3[:, c, :], start=(c == 0), stop=(c == 7)), reads=[p + 'BF%d' % c, 'ONESB'], writes=[q.kPO[3]])
            S.op('act', lambda e: e.activation(out=q.TMP3[:, 0, :], in_=q.PO[2], func=AF.Square), reads=[q.kPO[2]], writes=[p + 'TMP0'])
            S.op('dve', lambda e: e.tensor_tensor(out=q.TMP3[:, 1, :], in0=q.PO[3], in1=q.TMP3[:, 0, :], op=ALU.subtract), reads=[q.kPO[3], p + 'TMP0'], writes=[p + 'TMP1'])
            S.op('act', lambda e: e.activation(out=q.TMP3[:, 0, :], in_=q.TMP3[:, 1, :], func=AF.Sqrt, bias=CONST[:, 1:2], scale=1.0), reads=[p + 'TMP1', 'CONST'], writes=[p + 'TMP0'])
            S.op('dve', lambda e: e.reciprocal(out=q.TMP3[:, 1, :], in_=q.TMP3[:, 0, :]), reads=[p + 'TMP0'], writes=[p + 'TMP1'])
            yield 4.0
            for c in range(8):
                S.op('dve', lambda e, c=c: e.tensor_tensor(out=ACC3[:, c, :], in0=ACC3[:, c, :], in1=q.PO[2], op=ALU.subtract),
                     reads=[kAC[c], q.kPO[2]], writes=[kAC[c]])
            for c in range(8):
                S.op('dve', lambda e, c=c: e.tensor_tensor(out=ACC3[:, c, :], in0=ACC3[:, c, :], in1=q.TMP3[:, 1, :], op=ALU.mult),
                     reads=[kAC[c], p + 'TMP1'], writes=[kAC[c]])
            for c in range(8):
                S.op('act', lambda e, c=c: e.activation(out=q.BF3[:, c, :], in_=ACC3[:, c, :], func=AF.Silu, bias=pc('conv_ln_b', c), scale=pc('conv_ln_g', c)),
                     reads=[kAC[c], 'PR'], writes=[p + 'BF%d' % c])
            yield 5.0
            kb = [p + 'BF%d' % c for c in range(8)]
            for pp in range(4):
                wo, kwo = W.next(('cout', pp))
                for h in range(2):
                    c = 2 * pp + h
                    pb = c % 2
                    mm_group(q.PO[pb], q.kPO[pb], wo, kwo, h, q.BF3, kb, 8)
                    S.op('dve', lambda e, c=c, pb=pb: e.tensor_tensor(out=q.X3[:, c, :], in0=q.X3[:, c, :], in1=q.PO[pb], op=ALU.add),
                         reads=[p + 'X%d' % c, q.kPO[pb]], writes=[p + 'X%d' % c])
                yield 1.8

        def ssm_mixer(q, L, first):
            p = q.p
            if not ssm_ready[0] and not S.dry:
                ssm_ready[0] = True
                setup_ssm()
            kAC = keys('ACCss', 9)
            S.rekey('ACCs', kAC)
            A = lambda i: ACC[:, i * 512:(i + 1) * 512]
            if first:
                S.op('dve', lambda e: e.memset(q.WLR[:], 0.0), writes=[p + 'WLR'])
                S.op('dve', lambda e: e.memset(q.WLI[:], 0.0), writes=[p + 'WLI'])
            yield from rmsnorm(q, 'mix_norm', L, 'bf', rstd_sbuf=True)
            tt = lambda eng, o, ko, a, ka, bb, kbb, op: S.op(eng, lambda e: e.tensor_tensor(out=o, in0=a, in1=bb, op=op), reads=list(ka) + list(kbb), writes=list(ko))
            it = 0
            pending = None

            def emit_cmm(c, b, xre, xim, kxr, kxi):
                pby = c % 2
                t0 = b * SB
                for j in range(4):
                    stt = 4 * c + j
                    S.op('pe', lambda e, j=j, stt=stt: e.matmul(q.POY[pby][64 * (j // 2):64 * (j // 2) + 64, t0:t0 + SB], CRE3[:, stt, :], xre[:, j * SB:(j + 1) * SB],
                                                                start=(j % 2 == 0), stop=False), reads=['CRE', kxr], writes=[q.kPOY[pby]])
                    S.op('pe', lambda e, j=j, stt=stt: e.matmul(q.POY[pby][64 * (j // 2):64 * (j // 2) + 64, t0:t0 + SB], NCIM3[:, stt, :], xim[:, j * SB:(j + 1) * SB],
                                                                start=False, stop=(j % 2 == 1)), reads=['NCIM', kxi], writes=[q.kPOY[pby]])

            def epilogue(c):
                pby = c % 2
                S.op('dve', lambda e: e.scalar_tensor_tensor(out=q.TMP3[:, 0, :], in0=q.X3[:, c, :], scalar=CONST[:, 24 + c:25 + c], in1=q.RSTD[:],
                                                             op0=ALU.mult, op1=ALU.mult), reads=[p + 'X%d' % c, 'DG', p + 'RSTD'], writes=[p + 'TMP0'])
                S.op('dve', lambda e: e.tensor_tensor(out=q.TMP3[:, 0, :], in0=q.TMP3[:, 0, :], in1=q.POY[pby], op=ALU.add),
                     reads=[p + 'TMP0', q.kPOY[pby]], writes=[p + 'TMP0'])
                S.op('act', lambda e: e.activation(out=q.TMP3[:, 1, :], in_=q.TMP3[:, 0, :], func=AF.Square), reads=[p + 'TMP0'], writes=[p + 'TMP1'])
                S.op('dve', lambda e: e.tensor_scalar(out=q.TMP3[:, 1, :], in0=q.TMP3[:, 1, :], scalar1=0.044715, scalar2=1.0, op0=ALU.mult, op1=ALU.add),
                     reads=[p + 'TMP1'], writes=[p + 'TMP1'])
                S.op('dve', lambda e: e.tensor_tensor(out=q.TMP3[:, 1, :], in0=q.TMP3[:, 1, :], in1=q.TMP3[:, 0, :], op=ALU.mult), reads=[p + 'TMP1', p + 'TMP0'], writes=[p + 'TMP1'])
                S.op('act', lambda e: e.activation(out=q.TMP3[:, 2, :], in_=q.TMP3[:, 1, :], func=AF.Sigmoid, scale=2.0 * math.sqrt(2.0 / math.pi)),
                     reads=[p + 'TMP1'], writes=[p + 'TMP2'])
                S.op('dve', lambda e: e.tensor_tensor(out=q.ACTB3[:, c, :], in0=q.TMP3[:, 0, :], in1=q.TMP3[:, 2, :], op=ALU.mult),
                     reads=[p + 'TMP0', p + 'TMP2'], writes=[p + 'ACTB%d' % c])

            for c in range(8):
                Ct = CT[:, c * 512:(c + 1) * 512]
                St = ST[:, c * 512:(c + 1) * 512]
                Rt = RT[:, c * 512:(c + 1) * 512]
                for b in range(NT // SB):
                    t0 = b * SB
                    par = it % 2
                    it += 1
                    for j in range(4):
                        S.op('pe', lambda e, j=j, c=c, t0=t0: e.matmul(q.PGF[:, j * SB:(j + 1) * SB], BRE3[:, c, j, :], q.HNB3[:, c, t0:t0 + SB], start=True, stop=True),
                             reads=['BRE', p + 'HNB%d' % c], writes=q.kPGF)
                    for j in range(4):
                        S.op('pe', lambda e, j=j, c=c, t0=t0: e.matmul(q.PUF[:, j * SB:(j + 1) * SB], BIM3[:, c, j, :], q.HNB3[:, c, t0:t0 + SB], start=True, stop=True),
                             reads=['BIM', p + 'HNB%d' % c], writes=q.kPUF)
                    tt('dve', A(0), [kAC[0]], Ct, ['CT'], q.PGF[:], q.kPGF, ALU.mult)
                    tt('dve', A(1), [kAC[1]], St, ['ST'], q.PUF[:], q.kPUF, ALU.mult)
                    tt('dve', A(2), [kAC[2]], Ct, ['CT'], q.PUF[:], q.kPUF, ALU.mult)
                    tt('dve', A(0), [kAC[0]], A(0), [kAC[0]], A(1), [kAC[1]], ALU.add)
                    tt('dve', A(1), [kAC[1]], St, ['ST'], q.PGF[:], q.kPGF, ALU.mult)
                    tt('dve', A(2), [kAC[2]], A(2), [kAC[2]], A(1), [kAC[1]], ALU.subtract)
                    yield 3.3
                    wlr = q.WLR[:, 4 * c:4 * c + 4]
                    wli = q.WLI[:, 4 * c:4 * c + 4]
                    rc = SM[:, 12 * 32 + 4 * c:12 * 32 + 4 * c + 4]
                    rs = SM[:, 13 * 32 + 4 * c:13 * 32 + 4 * c + 4]
                    sm = lambda i: q.SMQ[:, 4 * i:4 * i + 4]
                    ksm = lambda i: [p + 'SMi%d' % i]
                    v0r = A(0).rearrange("p (a b) -> p a b", b=SB)[:, :, 0]
                    v0i = A(2).rearrange("p (a b) -> p a b", b=SB)[:, :, 0]
                    tt('dve', sm(0), ksm(0), rc, ['SM'], wlr, [p + 'WLR'], ALU.mult)
                    tt('dve', sm(1), ksm(1), rs, ['SM'], wli, [p + 'WLI'], ALU.mult)
                    tt('dve', sm(2), ksm(2), rs, ['SM'], wlr, [p + 'WLR'], ALU.mult)
                    tt('dve', sm(3), ksm(3), rc, ['SM'], wli, [p + 'WLI'], ALU.mult)
                    tt('dve', sm(4), ksm(4), sm(0), ksm(0), sm(1), ksm(1), ALU.subtract)
                    tt('dve', sm(5), ksm(5), sm(2), ksm(2), sm(3), ksm(3), ALU.add)
                    tt('dve', v0r, [kAC[0]], v0r, [kAC[0]], sm(4), ksm(4), ALU.add)
                    tt('dve', v0i, [kAC[2]], v0i, [kAC[2]], sm(5), ksm(5), ALU.add)
                    wr_, wi_ = A(3 + par), A(5 + par)
                    kwr, kwi = kAC[3 + par], kAC[5 + par]
                    S.op('dve', lambda e, Rt=Rt, wr_=wr_: e.tensor_tensor_scan(out=wr_, data0=Rt, data1=A(0), initial=0.0, op0=ALU.mult, op1=ALU.add),
                         reads=['RT', kAC[0]], writes=[kwr])
                    S.op('dve', lambda e, Rt=Rt, wi_=wi_: e.tensor_tensor_scan(out=wi_, data0=Rt, data1=A(2), initial=0.0, op0=ALU.mult, op1=ALU.add),
                         reads=['RT', kAC[2]], writes=[kwi])
                    wlast_r = wr_.rearrange("p (a b) -> p a b", b=SB)[:, :, SB - 1]
                    wlast_i = wi_.rearrange("p (a b) -> p a b", b=SB)[:, :, SB - 1]
                    S.op('act', lambda e, wlr=wlr, wlast_r=wlast_r: e.activation(out=wlr, in_=wlast_r, func=AF.Copy), reads=[kwr], writes=[p + 'WLR'])
                    S.op('act', lambda e, wli=wli, wlast_i=wlast_i: e.activation(out=wli, in_=wlast_i, func=AF.Copy), reads=[kwi], writes=[p + 'WLI'])
                    xre = q.BF[:, par * 1024:par * 1024 + 512]
                    xim = q.BF[:, par * 1024 + 512:par * 1024 + 1024]
                    kxr = [p + 'BF%d' % (par * 4), p + 'BF%d' % (par * 4 + 1)]
                    kxi = [p + 'BF%d' % (par * 4 + 2), p + 'BF%d' % (par * 4 + 3)]
                    tt('pool', A(7), [kAC[7]], Ct, ['CT'], wr_, [kwr], ALU.mult)
                    tt('pool', A(8), [kAC[8]], St, ['ST'], wi_, [kwi], ALU.mult)
                    tt('pool', xre, kxr, A(7), [kAC[7]], A(8), [kAC[8]], ALU.subtract)
                    tt('pool', A(7), [kAC[7]], St, ['ST'], wr_, [kwr], ALU.mult)
                    tt('pool', A(8), [kAC[8]], Ct, ['CT'], wi_, [kwi], ALU.mult)
                    tt('pool', xim, kxi, A(7), [kAC[7]], A(8), [kAC[8]], ALU.add)
                    if pending is not None:
                        emit_cmm(*pending[:6])
                        if pending[6]:
                            epilogue(pending[0])
                    pending = (c, b, xre, xim, kxr[0], kxi[0], b == NT // SB - 1)
                    yield 3.2
            emit_cmm(*pending[:6])
            epilogue(pending[0])
            yield 3.0
            kg = [p + 'ACTB%d' % c for c in range(8)]
            for pp in range(4):
                wa, kwa = W.next(('swa', pp))
                wb, kwb = W.next(('swb', pp))
                for h in range(2):
                    c = 2 * pp + h
                    pb = c % 2
                    mm_group(q.PG[pb], q.kPG[pb], wa, kwa, h, q.ACTB3, kg, 8)
                    mm_group(q.PU[pb], q.kPU[pb], wb, kwb, h, q.ACTB3, kg, 8)
                    S.op('act', lambda e, pb=pb: e.activation(out=q.TMP3[:, 2, :], in_=q.PU[pb], func=AF.Sigmoid), reads=[q.kPU[pb]], writes=[p + 'TMP2'])
                    S.op('dve', lambda e, pb=pb: e.tensor_tensor(out=q.TMP3[:, 2, :], in0=q.TMP3[:, 2, :], in1=q.PG[pb], op=ALU.mult),
                         reads=[p + 'TMP2', q.kPG[pb]], writes=[p + 'TMP2'])
                    S.op('dve', lambda e, c=c: e.tensor_tensor(out=q.X3[:, c, :], in0=q.X3[:, c, :], in1=q.TMP3[:, 2, :], op=ALU.add),
                         reads=[p + 'X%d' % c, p + 'TMP2'], writes=[p + 'X%d' % c])
                yield 3.6

        xsrc = xT.rearrange("(c p) t -> p c t", p=128)
        odst = outT.rearrange("(c p) t -> p c t", p=128)

        def stream_prog(q):
            p = q.p
            for ti in range(NTILES):
                first = (ti == 0)
                t0 = q.s * SEQ + ti * NT
                if ti > 0:
                    S.dma('pool', lambda e, t0=t0: e.dma_start(out=q.X3, in_=xsrc[:, :, t0:t0 + NT]), writes=[p + 'X%d' % c for c in range(8)])
                for L in range(depth_run):
                    kind = L % 3
                    lk = 'S' if kind == 2 else 'C'
                    yield ('lock', lk)
                    if kind == 0:
                        yield from pool_mixer(q, L, first)
                    elif kind == 1:
                        yield from conv_mixer(q, L, first)
                    else:
                        yield from ssm_mixer(q, L, first)
                    yield ('unlock', lk)
                    yield from ffn(q, L)
                yield ('lock', 'C')
                if do_final:
                    yield from rmsnorm(q, 'final_norm', 0, 'f32')
                    S.dma('pool', lambda e, t0=t0: e.dma_start(out=odst[:, :, t0:t0 + NT], in_=HNF3[:, :, HP:HP + NT]), reads=keys('HNF', 8), writes=['outT'])
                else:
                    S.dma('pool', lambda e, t0=t0: e.dma_start(out=odst[:, :, t0:t0 + NT], in_=q.X3), reads=[p + 'X%d' % c for c in range(8)], writes=['outT'])
                yield ('unlock', 'C')

        def drive():
            gens = [stream_prog(q) for q in streams]
            t = [0.0, 0.5]
            done = [False, False]
            blocked = [None, None]
            locks = {'C': None, 'S': None}
            while not all(done):
                cand = [s for s in range(2) if not done[s] and blocked[s] is None]
                assert cand, "driver deadlock"
                s = min(cand, key=lambda i: t[i])
                try:
                    r = next(gens[s])
                except StopIteration:
                    done[s] = True
                    continue
                if isinstance(r, tuple) and r[0] == 'lock':
                    if locks[r[1]] is None:
                        locks[r[1]] = s
                    else:
                        blocked[s] = r[1]
                elif isinstance(r, tuple) and r[0] == 'unlock':
                    assert locks[r[1]] == s
                    locks[r[1]] = None
                    o = 1 - s
                    if blocked[o] == r[1]:
                        blocked[o] = None
                        locks[r[1]] = o
                        t[o] = max(t[o], t[s])
                else:
                    t[s] += r
                    cast_ahead(2)

        S.dry = True
        W.record = True
        drive()
        S.dry = False
        W.record = False
        for q in streams:
            t0 = q.s * SEQ
            S.dma('pool', lambda e, q=q, t0=t0: e.dma_start(out=q.X3, in_=xsrc[:, :, t0:t0 + NT]), writes=[q.p + 'X%d' % c for c in range(8)])
        setup()
        S.rekey('HNF', keys('HNF', 8))
        drive()
        assert ssm_ready[0] or depth_run < 3
        S.emit()
    return nc


def _chunked(v):
    return np.ascontiguousarray(np.asarray(v, np.float32).reshape(8, 128).T)


def _prep_shared(inp):
    P = np.zeros((128, NPAR), np.float32)

    def put(name, idx, arr):
        o = PCOL[name] + idx
        P[:, o:o + arr.shape[1]] = arr

    for L in range(DEPTH):
        put("mix_norm", L * 8, _chunked(inp["mix_norm"][L]))
        put("ffn_norm", L * 8, _chunked(inp["ffn_norm"][L]))
    put("final_norm", 0, _chunked(inp["final_norm"]))
    for j in range(2):
        put("pool_b", j * 8, _chunked(inp["pool_b"][j]))
        put("pool_scale", j * 8, _chunked(inp["pool_scale"][j]))
    put("conv_b_a", 0, _chunked(inp["conv_b_in"][0][:D_MODEL]))
    put("conv_b_g", 0, _chunked(inp["conv_b_in"][0][D_MODEL:]))
    for k in range(CONV_W):
        put("conv_dw", k * 8, _chunked(inp["conv_dw"][0][k]))
    put("conv_dw_b", 0, _chunked(inp["conv_dw_b"][0]))
    put("conv_ln_g", 0, _chunked(inp["conv_ln_g"][0]))
    put("conv_ln_b", 0, _chunked(inp["conv_ln_b"][0]))
    put("ssm_d", 0, _chunked(inp["ssm_d"][0]))
    put("lamre", 0, np.asarray(inp["ssm_lam_re"][0], np.float32).reshape(32, 128).T)
    put("lamim", 0, np.asarray(inp["ssm_lam_im"][0], np.float32).reshape(32, 128).T)
    put("logdt", 0, np.repeat(np.asarray(inp["ssm_log_dt"][0], np.float32), 64).reshape(32, 128).T)

    b_re = np.asarray(inp["ssm_b_re"][0], np.float32)
    b_im = np.asarray(inp["ssm_b_im"][0], np.float32)
    c_re = np.asarray(inp["ssm_c_re"][0], np.float32)
    c_im = np.asarray(inp["ssm_c_im"][0], np.float32)
    bre_l = np.zeros((128, 8, 4, 128), np.float32)
    bim_l = np.zeros((128, 8, 4, 128), np.float32)
    cre_l = np.zeros((128, 32, 64), np.float32)
    cim_l = np.zeros((128, 32, 64), np.float32)
    for st in range(32):
        q, j = st // 4, st % 4
        for gg in range(2):
            g = 2 * st + gg
            bre_l[32 * j + gg * 16:32 * j + gg * 16 + 16, q, j, gg * 64:(gg + 1) * 64] = b_re[g].T
            bim_l[32 * j + gg * 16:32 * j + gg * 16 + 16, q, j, gg * 64:(gg + 1) * 64] = b_im[g].T
            co = 32 * (j % 2) + gg * 16
            cre_l[gg * 64:(gg + 1) * 64, st, co:co + 16] = c_re[g].T
            cim_l[gg * 64:(gg + 1) * 64, st, co:co + 16] = c_im[g].T
    f = lambda a: np.ascontiguousarray(np.asarray(a, np.float32))
    shared = {
        "params": P,
        "tau1": np.ascontiguousarray(np.broadcast_to(np.arange(1, SB + 1, dtype=np.float32), (128, SB))),
        "ident": np.eye(128, dtype=np.float32),
        "w_gate": f(inp["w_gate"]), "w_up": f(inp["w_up"]), "w_down": f(inp["w_down"]),
        "pool_w": f(inp["pool_w"]),
        "conv_w_in": f(inp["conv_w_in"][0]), "conv_w_out": f(inp["conv_w_out"][0]),
        "ssm_wa": f(inp["ssm_w_glu_a"][0]), "ssm_wb": f(inp["ssm_w_glu_b"][0]),
        "bre_l": bre_l.reshape(128, 4096), "bim_l": bim_l.reshape(128, 4096),
        "cre_l": cre_l.reshape(128, 2048), "cim_l": cim_l.reshape(128, 2048),
    }
    return shared


_NC_CACHE = {}


def run(inputs, depth_run=DEPTH, do_final=True, cores=N_CORES, trace=False):
    key = (depth_run, do_final)
    if key not in _NC_CACHE:
        _NC_CACHE[key] = build_nc(depth_run, do_final)
    nc = _NC_CACHE[key]
    shared = _prep_shared(inputs)
    x = np.asarray(inputs["x"], np.float32)
    in_maps = []
    for i in range(cores):
        xt = np.ascontiguousarray(x[2 * i:2 * i + 2].reshape(TOK, D_MODEL).T)
        m = dict(shared)
        m["xT"] = xt
        in_maps.append(m)
    res = run_bass_kernel_spmd(nc, in_maps, core_ids=list(range(cores)), **({"trace": True} if trace else {}))
    outs = []
    for i in range(cores):
        o = np.asarray(res.results[i]["outT"], np.float32)
        outs.append(o.T.reshape(2, SEQ, D_MODEL))
    return np.concatenate(outs, axis=0), res


def kernel(**inputs):
    out, _ = run(inputs)
    return out.astype(np.float32)
```

```python
import math
import numpy as np
from contextlib import ExitStack
import concourse.bass as bass
import concourse.mybir as mybir
from concourse.bass_utils import run_bass_kernel_spmd

F32 = mybir.dt.float32
BF16 = mybir.dt.bfloat16
ALU = mybir.AluOpType
AF = mybir.ActivationFunctionType

D_MODEL = 1024
SEQ = 2048
DEPTH = 4
D_FF = 2816
NCH = 8
NFF = 22
POOL_WINDOWS = (2, 4, 8, 16)
CONV_W = 31
RMS_EPS = 1e-6
LN_EPS = 1e-5
N_CORES = 8
TOK = 4096
NT = 256
NTILES = SEQ // NT
TPS = SEQ // NT
HP = 32
SB = 128
NSLOT = 5
SLOT_ELEMS = 2816

PCOL = {}
_off = 0
for _name, _n in [("mix_norm", 32), ("ffn_norm", 32), ("final_norm", 8), ("pool_b", 16), ("pool_scale", 16),
                  ("conv_b_a", 8), ("conv_b_g", 8), ("conv_dw", 248), ("conv_dw_b", 8), ("conv_ln_g", 8),
                  ("conv_ln_b", 8), ("ssm_d", 8), ("lamre", 32), ("lamim", 32), ("logdt", 32)]:
    PCOL[_name] = _off
    _off += _n
NPAR = _off


class Sched:
    EPOCH = 4000

    def __init__(self, nc, es, n_dma_sems=6):
        self.nc = nc
        self.es = es
        self.h = {'pe': nc.tensor, 'dve': nc.vector, 'act': nc.scalar, 'pool': nc.gpsimd, 'sp': nc.sync}
        self.prog = {e: [] for e in self.h}
        self.cnt = {e: 0 for e in self.h}
        self.sems = {}
        self.waited = {}
        self.last_w = {}
        self.readers = {}
        self.n_dma_sems = n_dma_sems
        self.dma_rr = {e: 0 for e in self.h}
        self.dma_cnt = {}
        self.dry = False

    def rekey(self, prefix, newkeys):
        if self.dry:
            return
        toks = set()
        for k in [k for k in self.last_w if k.startswith(prefix)]:
            t = self.last_w.pop(k)
            if t is not None:
                toks.add(t)
        for k in [k for k in self.readers if k.startswith(prefix)]:
            toks.update(self.readers.pop(k))
        for k in newkeys:
            self.last_w[k] = None
            self.readers[k] = list(toks)

    def _sem(self, key):
        if key not in self.sems:
            self.sems[key] = self.es.enter_context(self.nc.semaphore("s_" + "_".join(str(k) for k in key)))
        return self.sems[key]

    def _deps(self, eng, reads, writes, extra=()):
        deps = set(extra)
        for k in reads:
            t = self.last_w.get(k)
            if t is not None:
                deps.add(t)
            if k.startswith('PSB') and eng != 'pe':
                t = self.last_w.get(k[:-1] + ('1' if k[-1] == '0' else '0'))
                if t is not None:
                    deps.add(t)
        for k in writes:
            t = self.last_w.get(k)
            if t is not None:
                deps.add(t)
            for t in self.readers.get(k, ()):
                deps.add(t)
            if k.startswith('PSB'):
                sib = k[:-1] + ('1' if k[-1] == '0' else '0')
                if eng == 'pe':
                    for t in self.readers.get(sib, ()):
                        deps.add(t)
                t = self.last_w.get(sib)
                if t is not None:
                    deps.add(t)
        need = {}
        for (s, v, e) in deps:
            if e == 'pe' and eng == 'pe':
                continue
            if need.get(s, 0) < v:
                need[s] = v
        waits = []
        for s, v in need.items():
            if self.waited.get((eng, s), 0) < v:
                self.waited[(eng, s)] = v
                waits.append((s, v))
        return waits

    def _commit(self, tok, reads, writes):
        for k in writes:
            self.last_w[k] = tok
            self.readers[k] = []
        for k in reads:
            if k not in writes:
                self.readers.setdefault(k, []).append(tok)

    def op(self, eng, fn, reads=(), writes=()):
        if self.dry:
            return None
        waits = self._deps(eng, reads, writes)
        self.cnt[eng] += 1
        c = self.cnt[eng]
        ep = (c - 1) // self.EPOCH
        skey = (eng, ep)
        self._sem(skey)
        tok = (skey, c - ep * self.EPOCH, eng)
        self.prog[eng].append((waits, fn, skey, 1))
        self._commit(tok, reads, writes)
        return tok

    def dma(self, q, fn, reads=(), writes=()):
        if self.dry:
            return None
        j = self.dma_rr[q]
        self.dma_rr[q] = (j + 1) % self.n_dma_sems
        skey = ('dma', q, j)
        self._sem(skey)
        n = self.dma_cnt.get(skey, 0)
        extra = [(skey, 16 * n, 'dma')] if n > 0 else []
        waits = self._deps(q, reads, writes, extra)
        self.dma_cnt[skey] = n + 1
        tok = (skey, 16 * (n + 1), 'dma')
        self.prog[q].append((waits, fn, skey, 16))
        self._commit(tok, reads, writes)
        return tok

    def emit(self, final_engine='sp'):
        finals = []
        for e in self.h:
            c = self.cnt[e]
            if c > 0:
                ep = (c - 1) // self.EPOCH
                finals.append(((e, ep), c - ep * self.EPOCH))
        for skey, n in self.dma_cnt.items():
            finals.append((skey, 16 * n))
        sems = self.sems
        prog = self.prog
        with self.nc.Block() as block:
            def mk(e):
                def body(engh):
                    for waits, fn, skey, inc in prog[e]:
                        for s, v in waits:
                            engh.wait_ge(sems[s], v)
                        fn(engh).then_inc(sems[skey], inc)
                    if e == final_engine:
                        for s, v in finals:
                            engh.wait_ge(sems[s], v)
                return body
            block.sync(mk('sp'))
            block.tensor(mk('pe'))
            block.vector(mk('dve'))
            block.scalar(mk('act'))
            block.gpsimd(mk('pool'))


def build_nc(depth_run=DEPTH, do_final=True):
    nc = bass.Bass("TRN2", target_bir_lowering=False)

    def D(name, shape, dt, kind="ExternalInput"):
        return nc.dram_tensor(name, list(shape), dt, kind=kind).ap()

    xT = D("xT", [D_MODEL, TOK], F32)
    outT = D("outT", [D_MODEL, TOK], F32, "ExternalOutput")
    params = D("params", [128, NPAR], F32)
    tau1 = D("tau1", [128, SB], F32)
    w_gate = D("w_gate", [DEPTH, D_MODEL, D_FF], F32)
    w_up = D("w_up", [DEPTH, D_MODEL, D_FF], F32)
    w_down = D("w_down", [DEPTH, D_FF, D_MODEL], F32)
    pool_w = D("pool_w", [2, 4, 256, 256], F32)
    conv_w_in = D("conv_w_in", [D_MODEL, 2 * D_MODEL], F32)
    conv_w_out = D("conv_w_out", [D_MODEL, D_MODEL], F32)
    ssm_wa = D("ssm_wa", [D_MODEL, D_MODEL], F32)
    ssm_wb = D("ssm_wb", [D_MODEL, D_MODEL], F32)
    bre_l = D("bre_l", [128, 8 * 4 * 128], F32)
    bim_l = D("bim_l", [128, 8 * 4 * 128], F32)
    cre_l = D("cre_l", [128, 32 * 64], F32)
    cim_l = D("cim_l", [128, 32 * 64], F32)
    ident = D("ident", [128, 128], F32)

    wg_s = D("wg_s", [DEPTH, 11, 128, 8, 256], BF16, "Internal")
    wu_s = D("wu_s", [DEPTH, 11, 128, 8, 256], BF16, "Internal")
    wd_s = D("wd_s", [DEPTH, 4, 2, 128, 11, 256], BF16, "Internal")
    cin_s = D("cin_s", [8, 128, 8, 256], BF16, "Internal")
    cout_s = D("cout_s", [4, 128, 8, 256], BF16, "Internal")
    swa_s = D("swa_s", [4, 128, 8, 256], BF16, "Internal")
    swb_s = D("swb_s", [4, 128, 8, 256], BF16, "Internal")
    poolw_s = D("poolw_s", [2, 128, 8, 256], BF16, "Internal")
    dg_s = D("dg_s", [8, 128, CONV_W, 128], BF16, "Internal")

    with ExitStack() as es:
        S = Sched(nc, es)

        def sb(name, shape, dt):
            return es.enter_context(nc.sbuf_tensor(name, list(shape), dt))

        def v3(t, n):
            return t[:].rearrange("p (c n) -> p c n", n=n)

        def keys(name, n):
            return ['%s%d' % (name, i) for i in range(n)]

        HW = HP + NT
        AW = 16 + NT
        HNF = sb("HNF", [128, 8 * HW], F32)
        ACC = sb("ACC", [128, 4608], F32)
        ACCC = sb("ACCC", [128, 8 * AW], F32)
        WS = [sb("WS%d" % i, [128, SLOT_ELEMS], BF16) for i in range(NSLOT)]
        CT = sb("CT", [128, 32 * SB], F32)
        ST = sb("ST", [128, 32 * SB], F32)
        RT = sb("RT", [128, 32 * SB], F32)
        PR = sb("PR", [128, NPAR], F32)
        ONESF = sb("ONESF", [128, NT], F32)
        ONESB = sb("ONESB", [128, 128], BF16)
        BRE = sb("BRE", [128, 8 * 4 * 128], BF16)
        BIM = sb("BIM", [128, 8 * 4 * 128], BF16)
        CRE = sb("CRE", [128, 32 * 64], BF16)
        NCIM = sb("NCIM", [128, 32 * 64], BF16)
        SM = sb("SM", [128, 512], F32)
        INVC = sb("INVC", [128, 4 * 16], F32)
        CONST = sb("CONST", [128, 32], F32)
        PSB = [es.enter_context(nc.psum_tensor("PSB%d" % i, [128, 512], F32)) for i in range(8)]
        HNF3 = v3(HNF, HW)
        BRE3 = BRE[:].rearrange("p (q jj s) -> p q jj s", jj=4, s=128)
        BIM3 = BIM[:].rearrange("p (q jj s) -> p q jj s", jj=4, s=128)
        CRE3 = v3(CRE, 64)
        NCIM3 = v3(NCIM, 64)
        CT3 = v3(CT, SB)
        ST3 = v3(ST, SB)
        RT3 = v3(RT, SB)

        class Q:
            pass

        streams = []
        for s in range(2):
            q = Q()
            q.s = s
            q.p = 'q%d_' % s
            q.X = sb("X%d" % s, [128, 8 * NT], F32)
            q.HNB = sb("HNB%d" % s, [128, 8 * NT], BF16)
            q.BF = sb("BF%d" % s, [128, 8 * NT], BF16)
            q.TMP = sb("TMP%d" % s, [128, 3 * NT], F32)
            q.ACTB = sb("ACTB%d" % s, [128, NFF * NT], BF16)
            q.POOLH = [sb("POOLH%d_%d" % (s, j), [128, 8 * 16], F32) for j in range(2)]
            q.CONVH = sb("CONVH%d" % s, [128, 8 * HP], F32)
            q.WLR = sb("WLR%d" % s, [128, 32], F32)
            q.WLI = sb("WLI%d" % s, [128, 32], F32)
            q.SMQ = sb("SMQ%d" % s, [128, 64], F32)
            q.RSTD = sb("RSTD%d" % s, [128, NT], F32)
            q.X3 = v3(q.X, NT)
            q.HNB3 = v3(q.HNB, NT)
            q.BF3 = v3(q.BF, NT)
            q.TMP3 = v3(q.TMP, NT)
            q.ACTB3 = v3(q.ACTB, NT)
            B = PSB[4 * s:4 * s + 4]

            def hb(i, h, B=B):
                return B[i][:, h * NT:(h + 1) * NT]

            def khb(i, h, s=s):
                return 'PSB%dh%d' % (4 * s + i, h)
            q.PG = [hb(0, 0), hb(2, 0)]
            q.kPG = [khb(0, 0), khb(2, 0)]
            q.PU = [hb(1, 0), hb(3, 0)]
            q.kPU = [khb(1, 0), khb(3, 0)]
            q.PO = [hb(0, 1), hb(1, 1), hb(2, 1), hb(3, 1)]
            q.kPO = [khb(0, 1), khb(1, 1), khb(2, 1), khb(3, 1)]
            q.RING = [(hb(i, 0), hb(i, 1)) for i in range(4)]
            q.kRING = [(khb(i, 0), khb(i, 1)) for i in range(4)]
            q.PGF = B[0]
            q.kPGF = [khb(0, 0), khb(0, 1)]
            q.PUF = B[1]
            q.kPUF = [khb(1, 0), khb(1, 1)]
            q.POY = [hb(2, 0), hb(3, 0)]
            q.kPOY = [khb(2, 0), khb(3, 0)]
            streams.append(q)

        def pc(name, idx=0, n=1):
            o = PCOL[name] + idx
            return PR[:, o:o + n]

        def layer_wlist(L):
            lst = []
            kind = L % 3
            if kind == 0:
                lst.append(('poolw', L // 3))
            if kind == 1:
                for p in range(4):
                    lst.append(('cin', p))
                    lst.append(('cin', 4 + p))
                for c in range(8):
                    lst.append(('dg', c, 0))
                    lst.append(('dg', c, 1))
                for p in range(4):
                    lst.append(('cout', p))
            elif kind == 2:
                for p in range(4):
                    lst.append(('swa', p))
                    lst.append(('swb', p))
            for g in range(11):
                lst.append(('wg', L, g))
                lst.append(('wu', L, g))
            for mp in range(4):
                lst.append(('wd', L, mp, 0))
                lst.append(('wd', L, mp, 1))
            return lst

        def w_src(d):
            k = d[0]
            if k == 'wg':
                return wg_s[d[1], d[2]], w_gate[d[1]].rearrange("(kc k) (g m) -> g k kc m", k=128, m=256)[d[2]]
            if k == 'wu':
                return wu_s[d[1], d[2]], w_up[d[1]].rearrange("(kc k) (g m) -> g k kc m", k=128, m=256)[d[2]]
            if k == 'wd':
                src = w_down[d[1]].rearrange("(kh kc k) (mp m) -> mp kh k kc m", kh=2, kc=11, k=128, m=256)
                return wd_s[d[1], d[2], d[3]], src[d[2], d[3]]
            if k == 'cin':
                return cin_s[d[1]], conv_w_in.rearrange("(kc k) (g m) -> g k kc m", k=128, m=256)[d[1]]
            if k == 'cout':
                return cout_s[d[1]], conv_w_out.rearrange("(kc k) (g m) -> g k kc m", k=128, m=256)[d[1]]
            if k == 'swa':
                return swa_s[d[1]], ssm_wa.rearrange("(kc k) (g m) -> g k kc m", k=128, m=256)[d[1]]
            if k == 'dg':
                k0, k1 = (0, 16) if d[2] == 0 else (16, CONV_W)
                return dg_s[d[1]][:, k0:k1, :], None
            if k == 'poolw':
                return poolw_s[d[1]], pool_w[d[1]].rearrange("g (ki k) d -> k (g ki) d", k=128)
            if k == 'swb':
                return swb_s[d[1]], ssm_wb.rearrange("(kc k) (g m) -> g k kc m", k=128, m=256)[d[1]]
            raise KeyError(k)

        tile_wlist = []
        for L in range(depth_run):
            tile_wlist += layer_wlist(L)

        ssm_ready = [False]
        cast_done = set()
        cast_order = list(tile_wlist)
        cast_pos = [0]

        def ensure_cast(d):
            if S.dry or d in cast_done or d[0] == 'dg':
                return
            cast_done.add(d)
            scr, src = w_src(d)
            S.dma('pool', lambda e, scr=scr, src=src: e.dma_start(out=scr, in_=src), writes=['scr_' + '_'.join(map(str, d))])

        def cast_ahead(n):
            if S.dry:
                return
            while n > 0 and cast_pos[0] < len(cast_order):
                d = cast_order[cast_pos[0]]
                cast_pos[0] += 1
                if d not in cast_done:
                    ensure_cast(d)
                    n -= 1

        class WStream:
            def __init__(self):
                self.record = True
                self.descs = []
                self.issued = 0
                self.used = 0

            def _issue(self, i):
                d = self.descs[i]
                ensure_cast(d)
                scr, _ = w_src(d)
                a, m = scr.shape[1], scr.shape[2]
                dst = WS[i % NSLOT][:, 0:a * m].rearrange("p (a m) -> p a m", m=m)
                S.dma('sp', lambda e, dst=dst, scr=scr: e.dma_start(out=dst, in_=scr),
                      reads=['scr_' + '_'.join(map(str, d))], writes=['WS%d' % (i % NSLOT)])

            def next(self, d):
                if self.record:
                    self.descs.append(d)
                    k = len(self.descs) - 1
                else:
                    k = self.used
                    assert self.descs[k] == d, (self.descs[k], d)
                    lim = min(len(self.descs), k + NSLOT - 1)
                    while self.issued < lim:
                        self._issue(self.issued)
                        self.issued += 1
                    self.used += 1
                shp = w_src(d)[0].shape
                a, m = shp[1], shp[2]
                return WS[k % NSLOT][:, 0:a * m].rearrange("p (a m) -> p a m", m=m), 'WS%d' % (k % NSLOT)

        W = WStream()

        def setup():
            S.dma('sp', lambda e: e.dma_start(out=PR[:], in_=params), writes=['PR'])
            S.dma('pool', lambda e: e.dma_start(out=BRE[:].rearrange('p (a b) -> p a b', b=1024), in_=bre_l.rearrange('p (a b) -> p a b', b=1024)), writes=['BRE'])
            S.dma('pool', lambda e: e.dma_start(out=BIM[:].rearrange('p (a b) -> p a b', b=1024), in_=bim_l.rearrange('p (a b) -> p a b', b=1024)), writes=['BIM'])
            S.op('dve', lambda e: e.memset(ONESF[:], 1.0), writes=['ONESF'])
            S.op('dve', lambda e: e.memset(ONESB[:], 1.0 / D_MODEL), writes=['ONESB'])
            S.op('dve', lambda e: e.memset(CONST[:, 0:1], RMS_EPS), writes=['CONST'])
            S.op('dve', lambda e: e.memset(CONST[:, 1:2], LN_EPS), writes=['CONST'])
            for wi, Wn in enumerate(POOL_WINDOWS):
                S.op('dve', lambda e, wi=wi, Wn=Wn: e.memset(INVC[:, wi * 16:(wi + 1) * 16], 1.0 / Wn), writes=['INVC'])
                for t in range(Wn - 1):
                    S.op('dve', lambda e, wi=wi, t=t: e.memset(INVC[:, wi * 16 + t:wi * 16 + t + 1], 1.0 / (t + 1)), writes=['INVC'])
            S.op('dve', lambda e: e.tensor_tensor(out=CONST[:, 8:24], in0=pc('pool_b', 0, 16), in1=pc('pool_scale', 0, 16), op=ALU.mult),
                 reads=['PR'], writes=['PBS'])
            S.op('dve', lambda e: e.tensor_tensor(out=CONST[:, 24:32], in0=pc('mix_norm', 16, 8), in1=pc('ssm_d', 0, 8), op=ALU.mult),
                 reads=['PR'], writes=['DG'])
            if depth_run >= 2:
                IDT = ACCC[:, 0:128]
                S.dma('sp', lambda e: e.dma_start(out=IDT, in_=ident), writes=['ACCcident'])
                for c in range(8):
                    for k in range(CONV_W):
                        S.op('dve', lambda e, c=c, k=k: e.tensor_scalar(out=ACC[:, k * 128:(k + 1) * 128], in0=IDT, scalar1=pc('conv_dw', k * 8 + c), scalar2=None, op0=ALU.mult),
                             reads=['ACCcident', 'PR'], writes=['ACCsdiag'])
                    S.dma('pool', lambda e, c=c: e.dma_start(out=dg_s[c], in_=ACC[:, 0:CONV_W * 128].rearrange("p (k m) -> p k m", m=128), max_dma_last_dim=2048),
                          reads=['ACCsdiag'], writes=['scr_dg_%d_0' % c, 'scr_dg_%d_1' % c])

        def setup_ssm():
            S.rekey('ACCs', ['ACCsang', 'ACCskf', 'ACCsyy', 'ACCstau', 'ACCsta0', 'ACCsta1', 'ACCstb0', 'ACCstb1'])
            TAU = ACC[:, 4096:4096 + SB]
            S.dma('sp', lambda e: e.dma_start(out=TAU, in_=tau1), writes=['ACCstau'])
            f = lambda i: SM[:, i * 32:(i + 1) * 32]
            LRE, DT, AA, TH, R, T0, T1, T2, QRE, QIM, NQIM, DEN = [f(i) for i in range(12)]
            RC128 = f(12)
            RS128 = f(13)
            kSM = ['SM']
            S.op('dve', lambda e: e.tensor_scalar(out=LRE, in0=pc('lamre', 0, 32), scalar1=-1e-4, scalar2=None, op0=ALU.min), reads=['PR'], writes=kSM)
            S.op('act', lambda e: e.activation(out=DT, in_=pc('logdt', 0, 32), func=AF.Exp), reads=['PR'], writes=kSM)
            S.op('dve', lambda e: e.tensor_tensor(out=AA, in0=LRE, in1=DT, op=ALU.mult), reads=kSM, writes=kSM)
            S.op('dve', lambda e: e.tensor_tensor(out=TH, in0=pc('lamim', 0, 32), in1=DT, op=ALU.mult), reads=kSM + ['PR'], writes=kSM)
            S.op('act', lambda e: e.activation(out=R, in_=AA, func=AF.Exp), reads=kSM, writes=kSM)
            for st in range(32):
                S.op('dve', lambda e, st=st: e.tensor_scalar(out=RT[:, st * SB:(st + 1) * SB], in0=ONESF[:, 0:SB], scalar1=R[:, st:st + 1],
                                                             scalar2=None, op0=ALU.mult), reads=['ONESF'] + kSM, writes=['RT'])
            S.op('dve', lambda e: e.memset(RT3[:, :, 0], 0.0), writes=['RT'])
            MAGIC = 12582912.0
            C1 = 6.28125
            C2 = 2.0 * math.pi - 6.28125
            PI_LO = 3.1415925
            NQ = 8
            HALF = NQ * SB
            ANG = ACC[:, 0:HALF]
            KF = ACC[:, HALF:2 * HALF]
            YY = ACC[:, 2 * HALF:3 * HALF]
            kA, kK, kY = ['ACCsang'], ['ACCskf'], ['ACCsyy']
            for half in range(32 // NQ):
                for st in range(NQ):
                    S.op('dve', lambda e, st=st, half=half: e.tensor_scalar(out=ANG[:, st * SB:(st + 1) * SB], in0=TAU, scalar1=TH[:, half * NQ + st:half * NQ + st + 1],
                                                                          scalar2=None, op0=ALU.mult), reads=['ACCstau'] + kSM, writes=kA)
                for which, dst in (('sin', ST), ('cos', CT)):
                    if which == 'cos':
                        S.op('dve', lambda e: e.tensor_scalar(out=ANG, in0=ANG, scalar1=math.pi / 2, scalar2=None, op0=ALU.add), reads=kA, writes=kA)
                    S.op('dve', lambda e: e.tensor_scalar(out=KF, in0=ANG, scalar1=1.0 / (2 * math.pi), scalar2=MAGIC, op0=ALU.mult, op1=ALU.add), reads=kA, writes=kK)
                    S.op('dve', lambda e: e.tensor_scalar(out=KF, in0=KF, scalar1=-MAGIC, scalar2=None, op0=ALU.add), reads=kK, writes=kK)
                    S.op('dve', lambda e: e.scalar_tensor_tensor(out=YY, in0=KF, scalar=-C1, in1=ANG, op0=ALU.mult, op1=ALU.add), reads=kK + kA, writes=kY)
                    S.op('dve', lambda e: e.scalar_tensor_tensor(out=YY, in0=KF, scalar=-C2, in1=YY, op0=ALU.mult, op1=ALU.add), reads=kK + kY, writes=kY)
                    S.op('dve', lambda e: e.tensor_scalar(out=KF, in0=YY, scalar1=math.pi, scalar2=-2 * math.pi, op0=ALU.is_gt, op1=ALU.mult), reads=kY, writes=kK)
                    S.op('dve', lambda e: e.tensor_tensor(out=YY, in0=YY, in1=KF, op=ALU.add), reads=kY + kK, writes=kY)
                    S.op('dve', lambda e: e.tensor_scalar(out=KF, in0=YY, scalar1=-math.pi, scalar2=2 * math.pi, op0=ALU.is_lt, op1=ALU.mult), reads=kY, writes=kK)
                    S.op('dve', lambda e: e.tensor_tensor(out=YY, in0=YY, in1=KF, op=ALU.add), reads=kY + kK, writes=kY)
                    S.op('dve', lambda e: e.tensor_scalar(out=YY, in0=YY, scalar1=PI_LO, scalar2=-PI_LO, op0=ALU.min, op1=ALU.max), reads=kY, writes=kY)
                    S.op('act', lambda e, dst=dst, half=half: e.activation(out=dst[:, half * HALF:(half + 1) * HALF], in_=YY, func=AF.Sin),
                         reads=kY, writes=['CT' if which == 'cos' else 'ST'])
            C0 = CT3[:, :, 0]
            S0 = ST3[:, :, 0]
            CL = CT3[:, :, SB - 1]
            SL = ST3[:, :, SB - 1]
            tt = lambda o, a, b, op, rd=(), wr=kSM: S.op('dve', lambda e: e.tensor_tensor(out=o, in0=a, in1=b, op=op), reads=list(rd) + kSM, writes=wr)
            tt(RC128, R, CL, ALU.mult, rd=['CT'])
            tt(RS128, R, SL, ALU.mult, rd=['ST'])
            tt(T0, R, C0, ALU.mult, rd=['CT'])
            S.op('dve', lambda e: e.tensor_scalar(out=T0, in0=T0, scalar1=-1.0, scalar2=None, op0=ALU.add), reads=kSM, writes=kSM)
            tt(T1, R, S0, ALU.mult, rd=['ST'])
            LIM = pc('lamim', 0, 32)
            tt(T2, LRE, LRE, ALU.mult)
            tt(DEN, LIM, LIM, ALU.mult, rd=['PR'])
            tt(DEN, DEN, T2, ALU.add)
            S.op('dve', lambda e: e.reciprocal(out=DEN, in_=DEN), reads=kSM, writes=kSM)
            tt(QRE, T0, LRE, ALU.mult)
            tt(T2, T1, LIM, ALU.mult, rd=['PR'])
            tt(QRE, QRE, T2, ALU.add)
            tt(QRE, QRE, DEN, ALU.mult)
            tt(QIM, T1, LRE, ALU.mult)
            tt(T2, T0, LIM, ALU.mult, rd=['PR'])
            tt(QIM, QIM, T2, ALU.subtract)
            tt(QIM, QIM, DEN, ALU.mult)
            S.op('dve', lambda e: e.tensor_scalar(out=NQIM, in0=QIM, scalar1=-1.0, scalar2=None, op0=ALU.mult), reads=kSM, writes=kSM)
            S.dma('sp', lambda e: e.dma_start(out=ACC[:, 0:2048], in_=cre_l), writes=kA + kK + kY)
            S.dma('sp', lambda e: e.dma_start(out=ACC[:, 2048:4096], in_=cim_l), writes=kA + kK + kY + ['ACCstau'])
            CREL = ACC[:, 0:2048].rearrange("p (s h) -> p s h", h=64)
            CIML = ACC[:, 2048:4096].rearrange("p (s h) -> p s h", h=64)
            kT = kA + kK + kY
            for st in range(32):
                ta = ACC[:, 4224 + (st % 2) * 128:4224 + (st % 2) * 128 + 64]
                tb = ACC[:, 4224 + (st % 2) * 128 + 64:4224 + (st % 2) * 128 + 128]
                ka = ['ACCsta%d' % (st % 2)]
                kb = ['ACCstb%d' % (st % 2)]
                S.op('dve', lambda e, st=st, ta=ta: e.tensor_scalar(out=ta, in0=CIML[:, st, :], scalar1=QIM[:, st:st + 1], scalar2=None, op0=ALU.mult),
                     reads=kT + kSM, writes=ka)
                S.op('dve', lambda e, st=st, ta=ta: e.scalar_tensor_tensor(out=CRE3[:, st, :], in0=CREL[:, st, :], scalar=QRE[:, st:st + 1], in1=ta,
                                                                           op0=ALU.mult, op1=ALU.subtract), reads=kT + kSM + ka, writes=['CRE'])
                S.op('dve', lambda e, st=st, tb=tb: e.tensor_scalar(out=tb, in0=CIML[:, st, :], scalar1=QRE[:, st:st + 1], scalar2=-1.0, op0=ALU.mult, op1=ALU.mult),
                     reads=kT + kSM, writes=kb)
                S.op('dve', lambda e, st=st, tb=tb: e.scalar_tensor_tensor(out=NCIM3[:, st, :], in0=CREL[:, st, :], scalar=NQIM[:, st:st + 1], in1=tb,
                                                                           op0=ALU.mult, op1=ALU.add), reads=kT + kSM + kb, writes=['NCIM'])

        def rmsnorm(q, gname, gidx, dest, rstd_sbuf=False):
            p = q.p
            for c in range(8):
                S.op('act', lambda e, c=c: e.activation(out=q.BF3[:, c, :], in_=q.X3[:, c, :], func=AF.Square),
                     reads=[p + 'X%d' % c], writes=[p + 'BF%d' % c])
            for c in range(8):
                S.op('pe', lambda e, c=c: e.matmul(q.PO[2], ONESB[:], q.BF3[:, c, :], start=(c == 0), stop=(c == 7)),
                     reads=[p + 'BF%d' % c, 'ONESB'], writes=[q.kPO[2]])
            S.op('act', lambda e: e.activation(out=q.TMP3[:, 0, :], in_=q.PO[2], func=AF.Sqrt, bias=CONST[:, 0:1], scale=1.0),
                 reads=[q.kPO[2], 'CONST'], writes=[p + 'TMP0'])
            if rstd_sbuf:
                rs_ap, rs_k = q.RSTD[:], p + 'RSTD'
            else:
                rs_ap, rs_k = q.PO[2], q.kPO[2]
            S.op('dve', lambda e: e.reciprocal(out=rs_ap, in_=q.TMP3[:, 0, :]), reads=[p + 'TMP0'], writes=[rs_k])
            for c in range(8):
                if dest == 'bf':
                    o, wk = q.HNB3[:, c, :], p + 'HNB%d' % c
                else:
                    o, wk = HNF3[:, c, HP:HP + NT], 'HNF%d' % c
                S.op('dve', lambda e, c=c, o=o: e.scalar_tensor_tensor(out=o, in0=q.X3[:, c, :], scalar=pc(gname, gidx * 8 + c), in1=rs_ap,
                                                                      op0=ALU.mult, op1=ALU.mult),
                     reads=[p + 'X%d' % c, rs_k, 'PR'], writes=[wk])
            yield 4.0

        def mm_group(pst, kps, wv, kw, h, rhs3, rkeys, nk):
            for kc in range(nk):
                S.op('pe', lambda e, kc=kc: e.matmul(pst, wv[:, kc, h * 128:(h + 1) * 128], rhs3[:, kc, :], start=(kc == 0), stop=(kc == nk - 1)),
                     reads=[kw, rkeys[kc]], writes=[kps])

        def ffn(q, L):
            p = q.p
            yield from rmsnorm(q, 'ffn_norm', L, 'bf')
            kh = [p + 'HNB%d' % c for c in range(8)]
            for g in range(11):
                wg, kwg = W.next(('wg', L, g))
                wu, kwu = W.next(('wu', L, g))
                rp = (g % 2) * 2
                kgp = [q.kRING[rp][0], q.kRING[rp][1]]
                kup = [q.kRING[rp + 1][0], q.kRING[rp + 1][1]]
                for h in range(2):
                    mm_group(q.RING[rp][h], kgp[h], wg, kwg, h, q.HNB3, kh, 8)
                    mm_group(q.RING[rp + 1][h], kup[h], wu, kwu, h, q.HNB3, kh, 8)
                GB = PSB[4 * q.s + rp]
                UBK = PSB[4 * q.s + rp + 1]
                S.op('act', lambda e, GB=GB: e.activation(out=q.TMP[:, 0:2 * NT], in_=GB[:], func=AF.Silu),
                     reads=kgp, writes=[p + 'TMP0', p + 'TMP1'])
                UF = q.BF[:].bitcast(F32)[:, 0:2 * NT]
                kuf = [p + 'BF%d' % i for i in range(4)]
                S.op('act', lambda e, UBK=UBK, UF=UF: e.activation(out=UF, in_=UBK[:], func=AF.Copy), reads=kup, writes=kuf)
                S.op('pool', lambda e, g=g, UF=UF: e.tensor_tensor(out=q.ACTB[:, 2 * g * NT:(2 * g + 2) * NT], in0=q.TMP[:, 0:2 * NT], in1=UF, op=ALU.mult),
                     reads=[p + 'TMP0', p + 'TMP1'] + kuf, writes=[p + 'ACTB%d' % (2 * g), p + 'ACTB%d' % (2 * g + 1)])
                yield 3.8
            for mp in range(4):
                wds = [W.next(('wd', L, mp, 0)), W.next(('wd', L, mp, 1))]
                pbase = (mp % 2) * 2
                for khalf, (wd, kwd) in enumerate(wds):
                    for h in range(2):
                        for kc in range(11):
                            S.op('pe', lambda e, wd=wd, h=h, kc=kc, khalf=khalf, pbase=pbase: e.matmul(
                                q.PO[pbase + h], wd[:, kc, h * 128:(h + 1) * 128], q.ACTB3[:, khalf * 11 + kc, :],
                                start=(khalf == 0 and kc == 0), stop=(khalf == 1 and kc == 10)),
                                reads=[kwd, p + 'ACTB%d' % (khalf * 11 + kc)], writes=[q.kPO[pbase + h]])
                for h in range(2):
                    c = 2 * mp + h
                    S.op('act', lambda e, h=h, pbase=pbase: e.activation(out=q.TMP3[:, h, :], in_=q.PO[pbase + h], func=AF.Copy),
                         reads=[q.kPO[pbase + h]], writes=[p + 'TMP%d' % h])
                    S.op('pool', lambda e, c=c, h=h: e.tensor_tensor(out=q.X3[:, c, :], in0=q.X3[:, c, :], in1=q.TMP3[:, h, :], op=ALU.add),
                         reads=[p + 'X%d' % c, p + 'TMP%d' % h], writes=[p + 'X%d' % c])
                yield 5.0

        def pool_mixer(q, L, first):
            p = q.p
            j = L // 3
            ACC3 = ACCC[:, 0:8 * AW].rearrange("p (c n) -> p c n", n=AW)
            kAC = keys('ACCcp', 8)
            S.rekey('ACCc', kAC)
            PH3 = v3(q.POOLH[j], 16)
            kph = p + 'POOLH%d' % j
            if first:
                S.op('dve', lambda e: e.memset(q.POOLH[j][:], 0.0), writes=[kph])
            yield from rmsnorm(q, 'mix_norm', L, 'f32')
            S.op('act', lambda e: e.activation(out=ACC3[:, :, 0:16], in_=PH3, func=AF.Copy), reads=[kph], writes=kAC)
            for c in range(8):
                S.op('dve', lambda e, c=c: e.tensor_tensor_scan(out=ACC3[:, c, 16:16 + NT], data0=ONESF[:, 0:NT], data1=HNF3[:, c, HP:HP + NT],
                                                                initial=ACC3[:, c, 15:16], op0=ALU.mult, op1=ALU.add),
                     reads=[kAC[c], 'ONESF', 'HNF%d' % c], writes=[kAC[c]])
            S.op('act', lambda e: e.activation(out=PH3, in_=ACC3[:, :, NT:NT + 16], func=AF.Copy), reads=kAC, writes=[kph])
            yield 5.0
            for c in range(8):
                wi = c // 2
                Wn = POOL_WINDOWS[wi]
                tb = c % 2
                S.op('dve', lambda e, c=c, Wn=Wn, tb=tb: e.tensor_tensor(out=q.TMP3[:, tb, :], in0=ACC3[:, c, 16:16 + NT], in1=ACC3[:, c, 16 - Wn:16 - Wn + NT],
                                                                         op=ALU.subtract), reads=[kAC[c]], writes=[p + 'TMP%d' % tb])
                S.op('dve', lambda e, c=c, Wn=Wn, tb=tb: e.scalar_tensor_tensor(out=q.BF3[:, c, :], in0=q.TMP3[:, tb, :], scalar=1.0 / Wn, in1=HNF3[:, c, HP:HP + NT],
                                                                                op0=ALU.mult, op1=ALU.subtract),
                     reads=[p + 'TMP%d' % tb, 'HNF%d' % c], writes=[p + 'BF%d' % c])
                if first:
                    t16 = q.SMQ[:, 32 + tb * 16:32 + tb * 16 + 16]
                    S.op('dve', lambda e, c=c, wi=wi, tb=tb, t16=t16: e.tensor_tensor(out=t16, in0=q.TMP3[:, tb, 0:16], in1=INVC[:, wi * 16:(wi + 1) * 16], op=ALU.mult),
                         reads=[p + 'TMP%d' % tb, 'INVC'], writes=[p + 'T16_%d' % tb])
                    S.op('dve', lambda e, c=c, t16=t16: e.tensor_tensor(out=q.BF3[:, c, 0:16], in0=t16, in1=HNF3[:, c, HP:HP + 16], op=ALU.subtract),
                         reads=[p + 'T16_%d' % tb, 'HNF%d' % c], writes=[p + 'BF%d' % c])
            yield 5.0
            PW, kpw = W.next(('poolw', j))
            for g in range(4):
                for mo in range(2):
                    c = 2 * g + mo
                    pb = c % 2
                    for ki in range(2):
                        S.op('pe', lambda e, g=g, mo=mo, ki=ki, pb=pb: e.matmul(q.PO[pb], PW[:, 2 * g + ki, mo * 128:(mo + 1) * 128], q.BF3[:, 2 * g + ki, :],
                                                                                start=(ki == 0), stop=(ki == 1)),
                             reads=[kpw, p + 'BF%d' % (2 * g + ki)], writes=[q.kPO[pb]])
                    S.op('act', lambda e, c=c, pb=pb: e.activation(out=q.TMP3[:, 2, :], in_=q.PO[pb], func=AF.Identity,
                                                                   bias=CONST[:, 8 + j * 8 + c:8 + j * 8 + c + 1], scale=pc('pool_scale', j * 8 + c)),
                         reads=[q.kPO[pb], 'PBS', 'PR'], writes=[p + 'TMP2'])
                    S.op('dve', lambda e, c=c: e.tensor_tensor(out=q.X3[:, c, :], in0=q.X3[:, c, :], in1=q.TMP3[:, 2, :], op=ALU.add),
                         reads=[p + 'X%d' % c, p + 'TMP2'], writes=[p + 'X%d' % c])
            yield 3.0

        def conv_mixer(q, L, first):
            p = q.p
            ACC3 = ACCC[:, 0:8 * NT].rearrange("p (c n) -> p c n", n=NT)
            kAC = keys('ACCcc', 8)
            S.rekey('ACCc', kAC)
            CH3 = v3(q.CONVH, HP)
            kch = p + 'CONVH'
            if first:
                S.op('dve', lambda e: e.memset(q.CONVH[:], 0.0), writes=[kch])
            yield from rmsnorm(q, 'mix_norm', L, 'bf')
            kh = [p + 'HNB%d' % c for c in range(8)]
            UB3 = HNF[:].bitcast(BF16)[:, 0:8 * HW].rearrange("p (c n) -> p c n", n=HW)
            kU = keys('HNF', 8)
            S.op('act', lambda e: e.activation(out=UB3[:, :, 0:HP], in_=CH3, func=AF.Copy), reads=[kch], writes=kU)
            for pp in range(4):
                wa, kwa = W.next(('cin', pp))
                wg, kwg = W.next(('cin', 4 + pp))
                for h in range(2):
                    c = 2 * pp + h
                    pb = c % 2
                    mm_group(q.PG[pb], q.kPG[pb], wa, kwa, h, q.HNB3, kh, 8)
                    mm_group(q.PU[pb], q.kPU[pb], wg, kwg, h, q.HNB3, kh, 8)
                    S.op('act', lambda e, c=c, pb=pb: e.activation(out=q.TMP3[:, 2, :], in_=q.PU[pb], func=AF.Sigmoid, bias=pc('conv_b_g', c), scale=1.0),
                         reads=[q.kPU[pb], 'PR'], writes=[p + 'TMP2'])
                    S.op('dve', lambda e, c=c, pb=pb: e.scalar_tensor_tensor(out=UB3[:, c, HP:HP + NT], in0=q.PG[pb], scalar=pc('conv_b_a', c), in1=q.TMP3[:, 2, :],
                                                                             op0=ALU.add, op1=ALU.mult),
                         reads=[q.kPG[pb], p + 'TMP2', 'PR'], writes=kU)
                yield 3.6
            S.op('act', lambda e: e.activation(out=CH3, in_=UB3[:, :, NT:NT + HP], func=AF.Copy), reads=kU, writes=[kch])
            o0 = HP - (CONV_W - 1)
            PSA = [q.PG[0], q.PU[0], q.PG[1], q.PU[1], q.PO[0], q.PO[1], q.PO[2], q.PO[3]]
            kPSA = [q.kPG[0], q.kPU[0], q.kPG[1], q.kPU[1], q.kPO[0], q.kPO[1], q.kPO[2], q.kPO[3]]
            for c in range(8):
                dga, kda = W.next(('dg', c, 0))
                dgb, kdb = W.next(('dg', c, 1))
                for k in range(CONV_W):
                    lh, kl = (dga[:, k, :], kda) if k < 16 else (dgb[:, k - 16, :], kdb)
                    S.op('pe', lambda e, c=c, k=k, lh=lh: e.matmul(PSA[c], lh, UB3[:, c, o0 + k:o0 + k + NT], start=(k == 0), stop=(k == CONV_W - 1)),
                         reads=[kl] + kU, writes=[kPSA[c]])
                S.op('act', lambda e, c=c: e.activation(out=ACC3[:, c, :], in_=PSA[c], func=AF.Identity, bias=pc('conv_dw_b', c), scale=1.0),
                     reads=[kPSA[c], 'PR'], writes=[kAC[c]])
                yield 4.4
            for c in range(8):
                S.op('act', lambda e, c=c: e.activation(out=q.BF3[:, c, :], in_=ACC3[:, c, :], func=AF.Copy), reads=[kAC[c]], writes=[p + 'BF%d' % c])
            for c in range(8):
                S.op('pe', lambda e, c=c: e.matmul(q.PO[2], ONESB[:], q.BF3[:, c, :], start=(c == 0), stop=(c == 7)), reads=[p + 'BF%d' % c, 'ONESB'], writes=[q.kPO[2]])
            for c in range(8):
                S.op('act', lambda e, c=c: e.activation(out=q.BF3[:, c, :], in_=ACC3[:, c, :], func=AF.Square), reads=[kAC[c]], writes=[p + 'BF%d' % c])
            for c in range(8):
                S.op('pe', lambda e, c=c: e.matmul(q.PO[3], ONESB[:], q.BF3[:, c, :], start=(c == 0), stop=(c == 7)), reads=[p + 'BF%d' % c, 'ONESB'], writes=[q.kPO[3]])
            S.op('act', lambda e: e.activation(out=q.TMP3[:, 0, :], in_=q.PO[2], func=AF.Square), reads=[q.kPO[2]], writes=[p + 'TMP0'])
            S.op('dve', lambda e: e.tensor_tensor(out=q.TMP3[:, 1, :], in0=q.PO[3], in1=q.TMP3[:, 0, :], op=ALU.subtract), reads=[q.kPO[3], p + 'TMP0'], writes=[p + 'TMP1'])
            S.op('act', lambda e: e.activation(out=q.TMP3[:, 0, :], in_=q.TMP3[:, 1, :], func=AF.Sqrt, bias=CONST[:, 1:2], scale=1.0), reads=[p + 'TMP1', 'CONST'], writes=[p + 'TMP0'])
            S.op('dve', lambda e: e.reciprocal(out=q.TMP3[:, 1, :], in_=q.TMP3[:, 0, :]), reads=[p + 'TMP0'], writes=[p + 'TMP1'])
            yield 4.0
            for c in range(8):
                S.op('dve', lambda e, c=c: e.tensor_tensor(out=ACC3[:, c, :], in0=ACC3[:, c, :], in1=q.PO[2], op=ALU.subtract),
                     reads=[kAC[c], q.kPO[2]], writes=[kAC[c]])
            for c in range(8):
                S.op('dve', lambda e, c=c: e.tensor_tensor(out=ACC3[:, c, :], in0=ACC3[:, c, :], in1=q.TMP3[:, 1, :], op=ALU.mult),
                     reads=[kAC[c], p + 'TMP1'], writes=[kAC[c]])
            for c in range(8):
                S.op('act', lambda e, c=c: e.activation(out=q.BF3[:, c, :], in_=ACC3[:, c, :], func=AF.Silu, bias=pc('conv_ln_b', c), scale=pc('conv_ln_g', c)),
                     reads=[kAC[c], 'PR'], writes=[p + 'BF%d' % c])
            yield 5.0
            kb = [p + 'BF%d' % c for c in range(8)]
            for pp in range(4):
                wo, kwo = W.next(('cout', pp))
                for h in range(2):
                    c = 2 * pp + h
                    pb = c % 2
                    mm_group(q.PO[pb], q.kPO[pb], wo, kwo, h, q.BF3, kb, 8)
                    S.op('dve', lambda e, c=c, pb=pb: e.tensor_tensor(out=q.X3[:, c, :], in0=q.X3[:, c, :], in1=q.PO[pb], op=ALU.add),
                         reads=[p + 'X%d' % c, q.kPO[pb]], writes=[p + 'X%d' % c])
                yield 1.8

        def ssm_mixer(q, L, first):
            p = q.p
            if not ssm_ready[0] and not S.dry:
                ssm_ready[0] = True
                setup_ssm()
            kAC = keys('ACCss', 9)
            S.rekey('ACCs', kAC)
            A = lambda i: ACC[:, i * 512:(i + 1) * 512]
            if first:
                S.op('dve', lambda e: e.memset(q.WLR[:], 0.0), writes=[p + 'WLR'])
                S.op('dve', lambda e: e.memset(q.WLI[:], 0.0), writes=[p + 'WLI'])
            yield from rmsnorm(q, 'mix_norm', L, 'bf', rstd_sbuf=True)
            tt = lambda eng, o, ko, a, ka, bb, kbb, op: S.op(eng, lambda e: e.tensor_tensor(out=o, in0=a, in1=bb, op=op), reads=list(ka) + list(kbb), writes=list(ko))
            it = 0
            pending = None

            def emit_cmm(c, b, xre, xim, kxr, kxi):
                pby = c % 2
                t0 = b * SB
                for j in range(4):
                    stt = 4 * c + j
                    S.op('pe', lambda e, j=j, stt=stt: e.matmul(q.POY[pby][64 * (j // 2):64 * (j // 2) + 64, t0:t0 + SB], CRE3[:, stt, :], xre[:, j * SB:(j + 1) * SB],
                                                                start=(j % 2 == 0), stop=False), reads=['CRE', kxr], writes=[q.kPOY[pby]])
                    S.op('pe', lambda e, j=j, stt=stt: e.matmul(q.POY[pby][64 * (j // 2):64 * (j // 2) + 64, t0:t0 + SB], NCIM3[:, stt, :], xim[:, j * SB:(j + 1) * SB],
                                                                start=False, stop=(j % 2 == 1)), reads=['NCIM', kxi], writes=[q.kPOY[pby]])

            def epilogue(c):
                pby = c % 2
                S.op('dve', lambda e: e.scalar_tensor_tensor(out=q.TMP3[:, 0, :], in0=q.X3[:, c, :], scalar=CONST[:, 24 + c:25 + c], in1=q.RSTD[:],
                                                             op0=ALU.mult, op1=ALU.mult), reads=[p + 'X%d' % c, 'DG', p + 'RSTD'], writes=[p + 'TMP0'])
                S.op('dve', lambda e: e.tensor_tensor(out=q.TMP3[:, 0, :], in0=q.TMP3[:, 0, :], in1=q.POY[pby], op=ALU.add),
                     reads=[p + 'TMP0', q.kPOY[pby]], writes=[p + 'TMP0'])
                S.op('act', lambda e: e.activation(out=q.TMP3[:, 1, :], in_=q.TMP3[:, 0, :], func=AF.Square), reads=[p + 'TMP0'], writes=[p + 'TMP1'])
                S.op('dve', lambda e: e.tensor_scalar(out=q.TMP3[:, 1, :], in0=q.TMP3[:, 1, :], scalar1=0.044715, scalar2=1.0, op0=ALU.mult, op1=ALU.add),
                     reads=[p + 'TMP1'], writes=[p + 'TMP1'])
                S.op('dve', lambda e: e.tensor_tensor(out=q.TMP3[:, 1, :], in0=q.TMP3[:, 1, :], in1=q.TMP3[:, 0, :], op=ALU.mult), reads=[p + 'TMP1', p + 'TMP0'], writes=[p + 'TMP1'])
                S.op('act', lambda e: e.activation(out=q.TMP3[:, 2, :], in_=q.TMP3[:, 1, :], func=AF.Sigmoid, scale=2.0 * math.sqrt(2.0 / math.pi)),
                     reads=[p + 'TMP1'], writes=[p + 'TMP2'])
                S.op('dve', lambda e: e.tensor_tensor(out=q.ACTB3[:, c, :], in0=q.TMP3[:, 0, :], in1=q.TMP3[:, 2, :], op=ALU.mult),
                     reads=[p + 'TMP0', p + 'TMP2'], writes=[p + 'ACTB%d' % c])

            for c in range(8):
                Ct = CT[:, c * 512:(c + 1) * 512]
                St = ST[:, c * 512:(c + 1) * 512]
                Rt = RT[:, c * 512:(c + 1) * 512]
                for b in range(NT // SB):
                    t0 = b * SB
                    par = it % 2
                    it += 1
                    for j in range(4):
                        S.op('pe', lambda e, j=j, c=c, t0=t0: e.matmul(q.PGF[:, j * SB:(j + 1) * SB], BRE3[:, c, j, :], q.HNB3[:, c, t0:t0 + SB], start=True, stop=True),
                             reads=['BRE', p + 'HNB%d' % c], writes=q.kPGF)
                    for j in range(4):
                        S.op('pe', lambda e, j=j, c=c, t0=t0: e.matmul(q.PUF[:, j * SB:(j + 1) * SB], BIM3[:, c, j, :], q.HNB3[:, c, t0:t0 + SB], start=True, stop=True),
                             reads=['BIM', p + 'HNB%d' % c], writes=q.kPUF)
                    tt('dve', A(0), [kAC[0]], Ct, ['CT'], q.PGF[:], q.kPGF, ALU.mult)
                    tt('dve', A(1), [kAC[1]], St, ['ST'], q.PUF[:], q.kPUF, ALU.mult)
                    tt('dve', A(2), [kAC[2]], Ct, ['CT'], q.PUF[:], q.kPUF, ALU.mult)
                    tt('dve', A(0), [kAC[0]], A(0), [kAC[0]], A(1), [kAC[1]], ALU.add)
                    tt('dve', A(1), [kAC[1]], St, ['ST'], q.PGF[:], q.kPGF, ALU.mult)
                    tt('dve', A(2), [kAC[2]], A(2), [kAC[2]], A(1), [kAC[1]], ALU.subtract)
                    yield 3.3
                    wlr = q.WLR[:, 4 * c:4 * c + 4]
                    wli = q.WLI[:, 4 * c:4 * c + 4]
                    rc = SM[:, 12 * 32 + 4 * c:12 * 32 + 4 * c + 4]
                    rs = SM[:, 13 * 32 + 4 * c:13 * 32 + 4 * c + 4]
                    sm = lambda i: q.SMQ[:, 4 * i:4 * i + 4]
                    ksm = lambda i: [p + 'SMi%d' % i]
                    v0r = A(0).rearrange("p (a b) -> p a b", b=SB)[:, :, 0]
                    v0i = A(2).rearrange("p (a b) -> p a b", b=SB)[:, :, 0]
                    tt('dve', sm(0), ksm(0), rc, ['SM'], wlr, [p + 'WLR'], ALU.mult)
                    tt('dve', sm(1), ksm(1), rs, ['SM'], wli, [p + 'WLI'], ALU.mult)
                    tt('dve', sm(2), ksm(2), rs, ['SM'], wlr, [p + 'WLR'], ALU.mult)
                    tt('dve', sm(3), ksm(3), rc, ['SM'], wli, [p + 'WLI'], ALU.mult)
                    tt('dve', sm(4), ksm(4), sm(0), ksm(0), sm(1), ksm(1), ALU.subtract)
                    tt('dve', sm(5), ksm(5), sm(2), ksm(2), sm(3), ksm(3), ALU.add)
                    tt('dve', v0r, [kAC[0]], v0r, [kAC[0]], sm(4), ksm(4), ALU.add)
                    tt('dve', v0i, [kAC[2]], v0i, [kAC[2]], sm(5), ksm(5), ALU.add)
                    wr_, wi_ = A(3 + par), A(5 + par)
                    kwr, kwi = kAC[3 + par], kAC[5 + par]
                    S.op('dve', lambda e, Rt=Rt, wr_=wr_: e.tensor_tensor_scan(out=wr_, data0=Rt, data1=A(0), initial=0.0, op0=ALU.mult, op1=ALU.add),
                         reads=['RT', kAC[0]], writes=[kwr])
                    S.op('dve', lambda e, Rt=Rt, wi_=wi_: e.tensor_tensor_scan(out=wi_, data0=Rt, data1=A(2), initial=0.0, op0=ALU.mult, op1=ALU.add),
                         reads=['RT', kAC[2]], writes=[kwi])
                    wlast_r = wr_.rearrange("p (a b) -> p a b", b=SB)[:, :, SB - 1]
                    wlast_i = wi_.rearrange("p (a b) -> p a b", b=SB)[:, :, SB - 1]
                    S.op('act', lambda e, wlr=wlr, wlast_r=wlast_r: e.activation(out=wlr, in_=wlast_r, func=AF.Copy), reads=[kwr], writes=[p + 'WLR'])
                    S.op('act', lambda e, wli=wli, wlast_i=wlast_i: e.activation(out=wli, in_=wlast_i, func=AF.Copy), reads=[kwi], writes=[p + 'WLI'])
                    xre = q.BF[:, par * 1024:par * 1024 + 512]
                    xim = q.BF[:, par * 1024 + 512:par * 1024 + 1024]
                    kxr = [p + 'BF%d' % (par * 4), p + 'BF%d' % (par * 4 + 1)]
                    kxi = [p + 'BF%d' % (par * 4 + 2), p + 'BF%d' % (par * 4 + 3)]
                    tt('pool', A(7), [kAC[7]], Ct, ['CT'], wr_, [kwr], ALU.mult)
                    tt('pool', A(8), [kAC[8]], St, ['ST'], wi_, [kwi], ALU.mult)
                    tt('pool', xre, kxr, A(7), [kAC[7]], A(8), [kAC[8]], ALU.subtract)
                    tt('pool', A(7), [kAC[7]], St, ['ST'], wr_, [kwr], ALU.mult)
                    tt('pool', A(8), [kAC[8]], Ct, ['CT'], wi_, [kwi], ALU.mult)
                    tt('pool', xim, kxi, A(7), [kAC[7]], A(8), [kAC[8]], ALU.add)
                    if pending is not None:
                        emit_cmm(*pending[:6])
                        if pending[6]:
                            epilogue(pending[0])
                    pending = (c, b, xre, xim, kxr[0], kxi[0], b == NT // SB - 1)
                    yield 3.2
            emit_cmm(*pending[:6])
            epilogue(pending[0])
            yield 3.0
            kg = [p + 'ACTB%d' % c for c in range(8)]
            for pp in range(4):
                wa, kwa = W.next(('swa', pp))
                wb, kwb = W.next(('swb', pp))
                for h in range(2):
                    c = 2 * pp + h
                    pb = c % 2
                    mm_group(q.PG[pb], q.kPG[pb], wa, kwa, h, q.ACTB3, kg, 8)
                    mm_group(q.PU[pb], q.kPU[pb], wb, kwb, h, q.ACTB3, kg, 8)
                    S.op('act', lambda e, pb=pb: e.activation(out=q.TMP3[:, 2, :], in_=q.PU[pb], func=AF.Sigmoid), reads=[q.kPU[pb]], writes=[p + 'TMP2'])
                    S.op('dve', lambda e, pb=pb: e.tensor_tensor(out=q.TMP3[:, 2, :], in0=q.TMP3[:, 2, :], in1=q.PG[pb], op=ALU.mult),
                         reads=[p + 'TMP2', q.kPG[pb]], writes=[p + 'TMP2'])
                    S.op('dve', lambda e, c=c: e.tensor_tensor(out=q.X3[:, c, :], in0=q.X3[:, c, :], in1=q.TMP3[:, 2, :], op=ALU.add),
                         reads=[p + 'X%d' % c, p + 'TMP2'], writes=[p + 'X%d' % c])
                yield 3.6

        xsrc = xT.rearrange("(c p) t -> p c t", p=128)
        odst = outT.rearrange("(c p) t -> p c t", p=128)

        def stream_prog(q):
            p = q.p
            for ti in range(NTILES):
                first = (ti == 0)
                t0 = q.s * SEQ + ti * NT
                if ti > 0:
                    S.dma('pool', lambda e, t0=t0: e.dma_start(out=q.X3, in_=xsrc[:, :, t0:t0 + NT]), writes=[p + 'X%d' % c for c in range(8)])
                for L in range(depth_run):
                    kind = L % 3
                    lk = 'S' if kind == 2 else 'C'
                    yield ('lock', lk)
                    if kind == 0:
                        yield from pool_mixer(q, L, first)
                    elif kind == 1:
                        yield from conv_mixer(q, L, first)
                    else:
                        yield from ssm_mixer(q, L, first)
                    yield ('unlock', lk)
                    yield from ffn(q, L)
                yield ('lock', 'C')
                if do_final:
                    yield from rmsnorm(q, 'final_norm', 0, 'f32')
                    S.dma('pool', lambda e, t0=t0: e.dma_start(out=odst[:, :, t0:t0 + NT], in_=HNF3[:, :, HP:HP + NT]), reads=keys('HNF', 8), writes=['outT'])
                else:
                    S.dma('pool', lambda e, t0=t0: e.dma_start(out=odst[:, :, t0:t0 + NT], in_=q.X3), reads=[p + 'X%d' % c for c in range(8)], writes=['outT'])
                yield ('unlock', 'C')

        def drive():
            gens = [stream_prog(q) for q in streams]
            t = [0.0, 0.5]
            done = [False, False]
            blocked = [None, None]
            locks = {'C': None, 'S': None}
            while not all(done):
                cand = [s for s in range(2) if not done[s] and blocked[s] is None]
                assert cand, "driver deadlock"
                s = min(cand, key=lambda i: t[i])
                try:
                    r = next(gens[s])
                except StopIteration:
                    done[s] = True
                    continue
                if isinstance(r, tuple) and r[0] == 'lock':
                    if locks[r[1]] is None:
                        locks[r[1]] = s
                    else:
                        blocked[s] = r[1]
                elif isinstance(r, tuple) and r[0] == 'unlock':
                    assert locks[r[1]] == s
                    locks[r[1]] = None
                    o = 1 - s
                    if blocked[o] == r[1]:
                        blocked[o] = None
                        locks[r[1]] = o
                        t[o] = max(t[o], t[s])
                else:
                    t[s] += r
                    cast_ahead(2)

        S.dry = True
        W.record = True
        drive()
        S.dry = False
        W.record = False
        for q in streams:
            t0 = q.s * SEQ
            S.dma('pool', lambda e, q=q, t0=t0: e.dma_start(out=q.X3, in_=xsrc[:, :, t0:t0 + NT]), writes=[q.p + 'X%d' % c for c in range(8)])
        setup()
        S.rekey('HNF', keys('HNF', 8))
        drive()
        assert ssm_ready[0] or depth_run < 3
        S.emit()
    return nc


def _chunked(v):
    return np.ascontiguousarray(np.asarray(v, np.float32).reshape(8, 128).T)


def _prep_shared(inp):
    P = np.zeros((128, NPAR), np.float32)

    def put(name, idx, arr):
        o = PCOL[name] + idx
        P[:, o:o + arr.shape[1]] = arr

    for L in range(DEPTH):
        put("mix_norm", L * 8, _chunked(inp["mix_norm"][L]))
        put("ffn_norm", L * 8, _chunked(inp["ffn_norm"][L]))
    put("final_norm", 0, _chunked(inp["final_norm"]))
    for j in range(2):
        put("pool_b", j * 8, _chunked(inp["pool_b"][j]))
        put("pool_scale", j * 8, _chunked(inp["pool_scale"][j]))
    put("conv_b_a", 0, _chunked(inp["conv_b_in"][0][:D_MODEL]))
    put("conv_b_g", 0, _chunked(inp["conv_b_in"][0][D_MODEL:]))
    for k in range(CONV_W):
        put("conv_dw", k * 8, _chunked(inp["conv_dw"][0][k]))
    put("conv_dw_b", 0, _chunked(inp["conv_dw_b"][0]))
    put("conv_ln_g", 0, _chunked(inp["conv_ln_g"][0]))
    put("conv_ln_b", 0, _chunked(inp["conv_ln_b"][0]))
    put("ssm_d", 0, _chunked(inp["ssm_d"][0]))
    put("lamre", 0, np.asarray(inp["ssm_lam_re"][0], np.float32).reshape(32, 128).T)
    put("lamim", 0, np.asarray(inp["ssm_lam_im"][0], np.float32).reshape(32, 128).T)
    put("logdt", 0, np.repeat(np.asarray(inp["ssm_log_dt"][0], np.float32), 64).reshape(32, 128).T)

    b_re = np.asarray(inp["ssm_b_re"][0], np.float32)
    b_im = np.asarray(inp["ssm_b_im"][0], np.float32)
    c_re = np.asarray(inp["ssm_c_re"][0], np.float32)
    c_im = np.asarray(inp["ssm_c_im"][0], np.float32)
    bre_l = np.zeros((128, 8, 4, 128), np.float32)
    bim_l = np.zeros((128, 8, 4, 128), np.float32)
    cre_l = np.zeros((128, 32, 64), np.float32)
    cim_l = np.zeros((128, 32, 64), np.float32)
    for st in range(32):
        q, j = st // 4, st % 4
        for gg in range(2):
            g = 2 * st + gg
            bre_l[32 * j + gg * 16:32 * j + gg * 16 + 16, q, j, gg * 64:(gg + 1) * 64] = b_re[g].T
            bim_l[32 * j + gg * 16:32 * j + gg * 16 + 16, q, j, gg * 64:(gg + 1) * 64] = b_im[g].T
            co = 32 * (j % 2) + gg * 16
            cre_l[gg * 64:(gg + 1) * 64, st, co:co + 16] = c_re[g].T
            cim_l[gg * 64:(gg + 1) * 64, st, co:co + 16] = c_im[g].T
    f = lambda a: np.ascontiguousarray(np.asarray(a, np.float32))
    shared = {
        "params": P,
        "tau1": np.ascontiguousarray(np.broadcast_to(np.arange(1, SB + 1, dtype=np.float32), (128, SB))),
        "ident": np.eye(128, dtype=np.float32),
        "w_gate": f(inp["w_gate"]), "w_up": f(inp["w_up"]), "w_down": f(inp["w_down"]),
        "pool_w": f(inp["pool_w"]),
        "conv_w_in": f(inp["conv_w_in"][0]), "conv_w_out": f(inp["conv_w_out"][0]),
        "ssm_wa": f(inp["ssm_w_glu_a"][0]), "ssm_wb": f(inp["ssm_w_glu_b"][0]),
        "bre_l": bre_l.reshape(128, 4096), "bim_l": bim_l.reshape(128, 4096),
        "cre_l": cre_l.reshape(128, 2048), "cim_l": cim_l.reshape(128, 2048),
    }
    return shared


_NC_CACHE = {}


def run(inputs, depth_run=DEPTH, do_final=True, cores=N_CORES, trace=False):
    key = (depth_run, do_final)
    if key not in _NC_CACHE:
        _NC_CACHE[key] = build_nc(depth_run, do_final)
    nc = _NC_CACHE[key]
    shared = _prep_shared(inputs)
    x = np.asarray(inputs["x"], np.float32)
    in_maps = []
    for i in range(cores):
        xt = np.ascontiguousarray(x[2 * i:2 * i + 2].reshape(TOK, D_MODEL).T)
        m = dict(shared)
        m["xT"] = xt
        in_maps.append(m)
    res = run_bass_kernel_spmd(nc, in_maps, core_ids=list(range(cores)), **({"trace": True} if trace else {}))
    outs = []
    for i in range(cores):
        o = np.asarray(res.results[i]["outT"], np.float32)
        outs.append(o.T.reshape(2, SEQ, D_MODEL))
    return np.concatenate(outs, axis=0), res


def kernel(**inputs):
    out, _ = run(inputs)
    return out.astype(np.float32)
```
